# Optimizing a Trainium2 kernel written in Bass

```python
import math
import jax
import jax.numpy as jnp
from jax import lax
import numpy as np

D_MODEL = 1024
BATCH = 4
SEQ = 4096
DEPTH = 1

HEAD_DIM = 64
MOBA_HEADS = 8
NSA_HEADS = 8
NSA_GROUPS = 2
NSA_HPG = NSA_HEADS // NSA_GROUPS
N_HEADS_TOTAL = MOBA_HEADS + NSA_HEADS
MOBA_WIDTH = MOBA_HEADS * HEAD_DIM
NSA_WIDTH = NSA_HEADS * HEAD_DIM
NSA_KV_WIDTH = NSA_GROUPS * HEAD_DIM
MOBA_BLOCK = 256
MOBA_TOPK = 3
CMP_LEN = 32
CMP_STRIDE = 16
CMP_HIDDEN = 256
SEL_BLOCK = 64
SEL_TOPN = 16
WINDOW = 512
NUM_BUCKETS = 32
MAX_DISTANCE = 1024
Q_CHUNK = 64
EPS = 1e-6
NEG = -1e30
BIG = 1e30
SPLIT_SIZES = (MOBA_WIDTH, MOBA_WIDTH, MOBA_WIDTH, MOBA_WIDTH, NSA_WIDTH, NSA_KV_WIDTH, NSA_KV_WIDTH, NSA_KV_WIDTH, NSA_KV_WIDTH, NSA_KV_WIDTH, NSA_KV_WIDTH, 3 * NSA_HEADS, NSA_WIDTH, 2 * D_MODEL)
IN_WIDTH = sum(SPLIT_SIZES)

kernel_name = 'hybrid_moba_nsa_gated_block'


def rms_norm(x, g):
    xf = x.astype(jnp.float32)
    y = xf * lax.rsqrt(jnp.mean(xf * xf, axis=-1, keepdims=True) + EPS)
    return (y * g.astype(jnp.float32)).astype(x.dtype)


def t5_bucket(dist):
    n = jnp.maximum(dist, 0)
    max_exact = NUM_BUCKETS // 2
    nf = jnp.maximum(n, max_exact).astype(jnp.float32)
    large = max_exact + (jnp.log(nf / max_exact) / math.log(MAX_DISTANCE / max_exact) * (NUM_BUCKETS - max_exact)).astype(jnp.int32)
    return jnp.where(n < max_exact, n, jnp.minimum(large, NUM_BUCKETS - 1))


def moba_mixer(q, k, v, bias_tab):
    B, S, H, Dh = q.shape
    scale = 1.0 / math.sqrt(Dh)
    nb = -(-S // MOBA_BLOCK)
    pad = nb * MOBA_BLOCK - S
    padw = ((0, 0), (0, pad), (0, 0), (0, 0))
    kb = jnp.pad(k, padw).reshape(B, nb, MOBA_BLOCK, H, Dh).transpose(0, 3, 1, 2, 4)
    vb = jnp.pad(v, padw).reshape(B, nb, MOBA_BLOCK, H, Dh).transpose(0, 3, 1, 2, 4)
    qh = q.transpose(0, 2, 1, 3)
    kmean = jnp.mean(kb.astype(jnp.float32), axis=3)
    score = jnp.einsum('bhsd,bhjd->bhsj', qh.astype(jnp.float32), kmean)
    qblk = jnp.arange(S)[:, None] // MOBA_BLOCK
    blk = jnp.arange(nb)[None, :]
    score = jnp.where(blk == qblk, BIG, jnp.where(blk < qblk, score, NEG))
    _, sel = lax.top_k(score, min(MOBA_TOPK + 1, nb))
    bi = jnp.arange(B)[:, None, None, None]
    hi = jnp.arange(H)[None, :, None, None]
    offs = jnp.arange(MOBA_BLOCK)

    def one_chunk(c):
        t0 = c * Q_CHUNK
        tq = t0 + jnp.arange(Q_CHUNK)
        qc = lax.dynamic_slice_in_dim(qh, t0, Q_CHUNK, axis=2)
        sc = lax.dynamic_slice_in_dim(sel, t0, Q_CHUNK, axis=2)
        kg = kb[bi, hi, sc]
        vg = vb[bi, hi, sc]
        logits = jnp.einsum('bhqd,bhqnkd->bhqnk', qc, kg).astype(jnp.float32) * scale
        dist = tq[:, None, None] - (sc[..., None] * MOBA_BLOCK + offs)
        bias = bias_tab[hi[..., None], t5_bucket(dist)]
        logits = jnp.where(dist >= 0, logits + bias, NEG)
        p = jax.nn.softmax(logits.reshape(B, H, Q_CHUNK, -1), axis=-1).astype(v.dtype)
        return jnp.einsum('bhqm,bhqmd->bhqd', p, vg.reshape(B, H, Q_CHUNK, -1, Dh))

    out = lax.map(one_chunk, jnp.arange(S // Q_CHUNK))
    return out.transpose(1, 0, 3, 2, 4).reshape(B, S, H * Dh)


def compress(kv, pos, w1, w2):
    B, S, G, Dh = kv.shape
    nc = (S - CMP_LEN) // CMP_STRIDE + 1
    idx = np.arange(nc)[:, None] * CMP_STRIDE + np.arange(CMP_LEN)[None, :]
    win = kv[:, idx] + pos[None, None, :, None, :]
    win = win.transpose(0, 1, 3, 2, 4).reshape(B, nc, G, CMP_LEN * Dh)
    return jax.nn.gelu(win @ w1) @ w2


def nsa_mixer(q, k_cmp, v_cmp, k_sel, v_sel, k_win, v_win, branch_gate, bias_tab):
    B, S, H, Dh = q.shape
    G = NSA_GROUPS
    HPG = H // G
    scale = 1.0 / math.sqrt(Dh)
    qg = q.reshape(B, S, G, HPG, Dh).transpose(0, 2, 3, 1, 4)
    nc = k_cmp.shape[1]
    tpos = jnp.arange(S)
    lc = jnp.einsum('bghsd,bcgd->bghsc', qg, k_cmp).astype(jnp.float32) * scale
    cend = jnp.arange(nc) * CMP_STRIDE + CMP_LEN - 1
    cvalid = cend[None, :] <= tpos[:, None]
    p_cmp = jax.nn.softmax(jnp.where(cvalid, lc, NEG), axis=-1) * cvalid
    o_cmp = jnp.einsum('bghsc,bcgd->bghsd', p_cmp.astype(v_cmp.dtype), v_cmp)
    nsel = S // SEL_BLOCK
    cs = np.arange(nc) * CMP_STRIDE
    bs = np.arange(nsel) * SEL_BLOCK
    ov = ((cs[None, :] < bs[:, None] + SEL_BLOCK) & (cs[None, :] + CMP_LEN > bs[:, None])).astype(np.float32)
    imp = jnp.einsum('bghsc,jc->bgsj', p_cmp, jnp.asarray(ov))
    qblk = tpos[:, None] // SEL_BLOCK
    blk = jnp.arange(nsel)[None, :]
    imp = jnp.where((blk == qblk) | (blk == 0), BIG, jnp.where(blk < qblk, imp, NEG))
    _, sel = lax.top_k(imp, min(SEL_TOPN, nsel))
    ksb = k_sel.reshape(B, nsel, SEL_BLOCK, G, Dh).transpose(0, 3, 1, 2, 4)
    vsb = v_sel.reshape(B, nsel, SEL_BLOCK, G, Dh).transpose(0, 3, 1, 2, 4)
    padw = ((0, 0), (WINDOW, 0), (0, 0), (0, 0))
    kwp = jnp.pad(k_win, padw).transpose(0, 2, 1, 3)
    vwp = jnp.pad(v_win, padw).transpose(0, 2, 1, 3)
    tab_g = bias_tab.reshape(G, HPG, NUM_BUCKETS)
    bi = jnp.arange(B)[:, None, None, None]
    gi = jnp.arange(G)[None, :, None, None]
    g6 = jnp.arange(G)[None, :, None, None, None, None]
    h6 = jnp.arange(HPG)[None, None, :, None, None, None]
    offs = jnp.arange(SEL_BLOCK)

    def one_chunk(c):
        t0 = c * Q_CHUNK
        tq = t0 + jnp.arange(Q_CHUNK)
        qc = lax.dynamic_slice_in_dim(qg, t0, Q_CHUNK, axis=3)
        sc = lax.dynamic_slice_in_dim(sel, t0, Q_CHUNK, axis=2)
        kg = ksb[bi, gi, sc]
        vg = vsb[bi, gi, sc]
        ls = jnp.einsum('bghqd,bgqnkd->bghqnk', qc, kg).astype(jnp.float32) * scale
        dist = tq[:, None, None] - (sc[..., None] * SEL_BLOCK + offs)
        bias = tab_g[g6, h6, t5_bucket(dist)[:, :, None]]
        ls = jnp.where(dist[:, :, None] >= 0, ls + bias, NEG)
        ps = jax.nn.softmax(ls.reshape(B, G, HPG, Q_CHUNK, -1), axis=-1).astype(v_sel.dtype)
        o_sel = jnp.einsum('bghqm,bgqmd->bghqd', ps, vg.reshape(B, G, Q_CHUNK, -1, Dh))
        kwc = lax.dynamic_slice_in_dim(kwp, t0, Q_CHUNK + WINDOW, axis=2)
        vwc = lax.dynamic_slice_in_dim(vwp, t0, Q_CHUNK + WINDOW, axis=2)
        kpos = t0 - WINDOW + jnp.arange(Q_CHUNK + WINDOW)
        dw = tq[:, None] - kpos[None, :]
        wvalid = (dw >= 0) & (dw < WINDOW) & (kpos[None, :] >= 0)
        lw = jnp.einsum('bghqd,bgkd->bghqk', qc, kwc).astype(jnp.float32) * scale + tab_g[:, :, t5_bucket(dw)]
        pw = jax.nn.softmax(jnp.where(wvalid, lw, NEG), axis=-1).astype(v_win.dtype)
        o_win = jnp.einsum('bghqk,bgkd->bghqd', pw, vwc)
        return o_sel, o_win

    o_sel, o_win = lax.map(one_chunk, jnp.arange(S // Q_CHUNK))
    o_sel = o_sel.transpose(1, 2, 3, 0, 4, 5).reshape(B, G, HPG, S, Dh)
    o_win = o_win.transpose(1, 2, 3, 0, 4, 5).reshape(B, G, HPG, S, Dh)
    g = jax.nn.sigmoid(branch_gate.astype(jnp.float32)).astype(q.dtype)
    g = g.reshape(B, S, G, HPG, 3).transpose(0, 2, 3, 1, 4)
    o = g[..., 0:1] * o_cmp + g[..., 1:2] * o_sel + g[..., 2:3] * o_win
    return o.transpose(0, 3, 1, 2, 4).reshape(B, S, H * Dh)


def setup_inputs(seed: int = 0) -> dict:
    key = jax.random.key(seed)
    ks = jax.random.split(key, 19)

    def normal(k, shape, s):
        return jax.random.normal(k, shape, jnp.float32) * s

    def gain(k, shape):
        return 1.0 + 0.1 * jax.random.normal(k, shape, jnp.float32)

    L = DEPTH
    flat = CMP_LEN * HEAD_DIM
    return {
        'x': normal(ks[0], (BATCH, SEQ, D_MODEL), 1.0),
        'norm_w': gain(ks[1], (L, D_MODEL)),
        'w_in': normal(ks[2], (L, D_MODEL, IN_WIDTH), D_MODEL ** -0.5),
        'q_norm_a': gain(ks[3], (L, HEAD_DIM)),
        'k_norm_a': gain(ks[4], (L, HEAD_DIM)),
        'q_norm_b': gain(ks[5], (L, HEAD_DIM)),
        'k_norm_cmp': gain(ks[6], (L, HEAD_DIM)),
        'k_norm_sel': gain(ks[7], (L, HEAD_DIM)),
        'k_norm_win': gain(ks[8], (L, HEAD_DIM)),
        'cmp_pos_k': normal(ks[9], (L, CMP_LEN, HEAD_DIM), 0.1),
        'cmp_w1_k': normal(ks[10], (L, flat, CMP_HIDDEN), flat ** -0.5),
        'cmp_w2_k': normal(ks[11], (L, CMP_HIDDEN, HEAD_DIM), CMP_HIDDEN ** -0.5),
        'cmp_pos_v': normal(ks[12], (L, CMP_LEN, HEAD_DIM), 0.1),
        'cmp_w1_v': normal(ks[13], (L, flat, CMP_HIDDEN), flat ** -0.5),
        'cmp_w2_v': normal(ks[14], (L, CMP_HIDDEN, HEAD_DIM), CMP_HIDDEN ** -0.5),
        'rel_bias': normal(ks[15], (NUM_BUCKETS, N_HEADS_TOTAL), 0.5),
        'w_branch_a': normal(ks[16], (L, MOBA_WIDTH, D_MODEL), MOBA_WIDTH ** -0.5),
        'w_branch_b': normal(ks[17], (L, NSA_WIDTH, D_MODEL), NSA_WIDTH ** -0.5),
        'w_out': normal(ks[18], (L, D_MODEL, D_MODEL), D_MODEL ** -0.5),
    }


def reference(x, norm_w, w_in, q_norm_a, k_norm_a, q_norm_b, k_norm_cmp, k_norm_sel, k_norm_win,
              cmp_pos_k, cmp_w1_k, cmp_w2_k, cmp_pos_v, cmp_w1_v, cmp_w2_v, rel_bias,
              w_branch_a, w_branch_b, w_out):
    B, S, _ = x.shape
    split_points = [int(p) for p in np.cumsum(SPLIT_SIZES)[:-1]]
    bias_tab = rel_bias.T
    bias_a = bias_tab[:MOBA_HEADS]
    bias_b = bias_tab[MOBA_HEADS:]

    def heads(t):
        return t.reshape(B, S, -1, HEAD_DIM)

    for l in range(DEPTH):
        h = rms_norm(x, norm_w[l])
        proj = h @ w_in[l]
        (q_a, k_a, v_a, z_a, q_b, kc, vc, ksl, vsl, kwn, vwn, gate_b, z_b, gate_m) = jnp.split(proj, split_points, axis=-1)
        o_a = moba_mixer(rms_norm(heads(q_a), q_norm_a[l]), rms_norm(heads(k_a), k_norm_a[l]), heads(v_a), bias_a)
        k_cmp = rms_norm(compress(heads(kc), cmp_pos_k[l], cmp_w1_k[l], cmp_w2_k[l]), k_norm_cmp[l])
        v_cmp = compress(heads(vc), cmp_pos_v[l], cmp_w1_v[l], cmp_w2_v[l])
        o_b = nsa_mixer(rms_norm(heads(q_b), q_norm_b[l]), k_cmp, v_cmp,
                        rms_norm(heads(ksl), k_norm_sel[l]), heads(vsl),
                        rms_norm(heads(kwn), k_norm_win[l]), heads(vwn),
                        gate_b.reshape(B, S, NSA_HEADS, 3), bias_b)
        y_a = o_a * jax.nn.silu(z_a)
        y_b = o_b * jax.nn.silu(z_b)
        g_m = jax.nn.sigmoid(gate_m)
        merged = g_m[..., :D_MODEL] * (y_a @ w_branch_a[l]) + g_m[..., D_MODEL:] * (y_b @ w_branch_b[l])
        x = x + merged @ w_out[l]
    return x
```

```python
import numpy as np
import concourse.bass as bass
import concourse.mybir as mybir
from concourse.bass_utils import run_bass_kernel_spmd

F32 = mybir.dt.float32
BF16 = mybir.dt.bfloat16
AF = mybir.ActivationFunctionType
ALU = mybir.AluOpType
AX = mybir.AxisListType

EPS = 1e-6
MASKV = 30000.0
C_QA, C_KA, C_VA, C_ZA, C_QB, C_KC, C_VC, C_KS, C_VS, C_KW, C_VW, C_GB, C_ZB, C_GM = (
    0, 512, 1024, 1536, 2048, 2560, 2688, 2816, 2944, 3072, 3200, 3328, 3352, 3864)


class Sched:
    def __init__(self, nc, n_dma_sems=24):
        self.nc = nc
        self.engs = {'pe': nc.tensor, 'act': nc.scalar, 'dve': nc.vector, 'pool': nc.gpsimd, 'sp': nc.sync}
        self.sem = {k: nc.alloc_semaphore(name=f"s_{k}") for k in self.engs}
        self.cnt = {k: 0 for k in self.engs}
        self.waited = {k: {} for k in self.engs}
        self.dpool = {}
        for q, n in (('sp', 14), ('pool', 10)):
            self.dpool[q] = dict(sems=[nc.alloc_semaphore(name=f"d{q}{i}") for i in range(n)], val=[0] * n, nxt=0)
        self.lastw = {}
        self.readers = {}

    def _wait(self, e, deps):
        best = {}
        for d in deps:
            if d is None:
                continue
            s, v = d
            if v > best.get(id(s), (None, 0))[1]:
                best[id(s)] = (s, v)
        for sid, (s, v) in best.items():
            if self.waited[e].get(sid, 0) >= v:
                continue
            if e == 'pe' and s is self.sem['pe']:
                continue
            self.engs[e].wait_ge(s, v)
            self.waited[e][sid] = v

    def _deps(self, reads, writes):
        deps = []
        for k in reads:
            deps.append(self.lastw.get(k))
        for k in writes:
            deps.append(self.lastw.get(k))
            deps.extend(self.readers.get(k, {}).values())
        return deps

    def _commit(self, reads, writes, tok):
        for k in reads:
            r = self.readers.setdefault(k, {})
            old = r.get(id(tok[0]))
            if old is None or old[1] < tok[1]:
                r[id(tok[0])] = tok
        for k in writes:
            self.lastw[k] = tok
            self.readers[k] = {}

    def op(self, e, fn, reads=(), writes=()):
        self._wait(e, self._deps(reads, writes))
        ins = fn()
        self.cnt[e] += 1
        ins.then_inc(self.sem[e], 1)
        tok = (self.sem[e], self.cnt[e])
        self._commit(reads, writes, tok)
        return tok

    def dma(self, e, out, in_, reads=(), writes=(), **kw):
        dp = self.dpool[e]
        i = dp['nxt']
        dp['nxt'] = (i + 1) % len(dp['sems'])
        s = dp['sems'][i]
        deps = self._deps(reads, writes)
        if dp['val'][i] > 0:
            deps.append((s, dp['val'][i]))
        self._wait(e, deps)
        self.engs[e].dma_start(out=out, in_=in_, **kw).then_inc(s, 16)
        dp['val'][i] += 16
        tok = (s, dp['val'][i])
        self._commit(reads, writes, tok)
        return tok

    def barrier(self):
        toks = [(self.sem[k], self.cnt[k]) for k in self.engs if self.cnt[k] > 0]
        for dp in self.dpool.values():
            toks += [(s, v) for s, v in zip(dp['sems'], dp['val']) if v > 0]
        for e in self.engs:
            self._wait(e, toks)

    def finish(self, e, keys):
        self._wait(e, [self.lastw.get(k) for k in keys])


class Arena:
    def __init__(self, nc, base, limit):
        self.nc = nc
        self.cur = base
        self.limit = limit
        self.n = 0
        self.offs = {}

    def alloc(self, name, shape, dt):
        esz = 2 if dt == BF16 else 4
        nbytes = esz
        for d in shape[1:]:
            nbytes *= d
        self.cur = (self.cur + 63) // 64 * 64
        off = self.cur
        self.cur += nbytes
        assert self.cur <= self.limit, f"SBUF overflow {name} {self.cur}"
        self.n += 1
        self.offs[name] = off
        return self.nc.alloc_sbuf_tensor_at(f"{name}_{self.n}", list(shape), dt, offset=off)

    def mark(self):
        return self.cur

    def reset(self, m):
        self.cur = m


def build_program(debug=False):
    nc = bass.Bass("TRN2", target_bir_lowering=False)

    def din(name, shape):
        return nc.dram_tensor(name, list(shape), F32, kind="ExternalInput")

    xa = din("xa", [4096, 1024])
    norm_w = din("norm_w", [128, 8])
    w_in = din("w_in", [1024, 5912])
    gains = din("gains", [6, 64])
    gainsT = din("gainsT", [64, 6])
    cmp_pos = din("cmp_pos", [2, 64, 32])
    cmp_w1 = din("cmp_w1", [2, 2048, 256])
    cmp_w2 = din("cmp_w2", [2, 256, 64])
    rel_bias = din("rel_bias", [32, 16])
    w_ba = din("w_ba", [512, 1024])
    w_bb = din("w_bb", [512, 1024])
    w_out = din("w_out", [1024, 1024])
    ea = din("ea", [8, 128, 1024])
    es = din("es", [8, 128, 1024])
    ewo = din("ewo", [8, 128, 640])
    ewc = din("ewc", [8, 128, 640])
    cvd = din("cv", [128, 2, 2048])
    ma_past = din("ma_past", [128, 16, 16])
    ma_add = din("ma_add", [128, 16, 16])
    ma_own = din("ma_own", [128, 16, 16])
    ms_past = din("ms_past", [128, 16, 64])
    ms_add = din("ms_add", [128, 16, 64])
    oh16 = din("oh16", [16, 4096])
    oh64 = din("oh64", [64, 4096])
    ovT = din("ovT", [128, 2, 64])
    identd = din("ident", [128, 128])
    out = nc.dram_tensor("out", [2048, 1024], F32, kind="ExternalOutput")
    dbg = {}
    if debug:
        dbg['oa'] = nc.dram_tensor("dbg_oa", [2048, 512], F32, kind="ExternalOutput")
        dbg['ob'] = nc.dram_tensor("dbg_ob", [2048, 512], F32, kind="ExternalOutput")

    S = Sched(nc)
    AR = Arena(nc, 16640, 229376 - 256)
    op, dma = S.op, S.dma
    V, A, P, T = nc.vector, nc.scalar, nc.gpsimd, nc.tensor

    PS = [nc.alloc_psum_tensor(f"ps{i}", [128, 512], F32) for i in range(7)]
    PSB = nc.alloc_psum_tensor("psb", [128, 1024], BF16)
    LGB = [0, 1, 4, 5]
    NLG = 4
    NPT = 6
    DEPTH = 4
    B_OT, B_TOK, B_PJ0, B_PJ1, B_MS = 2, 3, 4, 5, 6
    PJ3 = [4, 5, 3]

    def pk(i):
        return f"ps{i}"

    hT = AR.alloc("hT", [128, 8, 4096], BF16)
    ident = AR.alloc("ident", [128, 128], F32)
    identb = AR.alloc("identb", [128, 128], BF16)
    bdiag = AR.alloc("bdiag", [128, 128], BF16)
    normw = AR.alloc("normw", [128, 8], F32)
    gn = AR.alloc("gn", [128, 6], F32)
    gnq = AR.alloc("gnq", [128, 6], F32)
    nb31 = AR.alloc("nb31", [128, 16], F32)
    o_b = AR.alloc("o_b", [128, 16, 512], F32)
    PT = [AR.alloc(f"PT{i}", [128, 512], BF16) for i in range(6)]
    OS = [AR.alloc(f"OS{i}", [128, 512], F32) for i in range(2)]
    small = AR.alloc("small", [128, 64], F32)
    sqb = [AR.alloc(f"sqb{i}", [128, 512], BF16) for i in range(2)]
    rstd = [AR.alloc(f"rstd{i}", [128, 512], F32) for i in range(2)]
    epst = AR.alloc("epst", [128, 1], F32)
    mark_persist = AR.mark()
    op('pool', lambda: P.memset(epst[:], EPS), writes=['epst'])

    dma('sp', ident[:], identd.ap(), writes=['ident'])
    dma('pool', identb[:], identd.ap(), writes=['identb'])
    op('pool', lambda: P.memset(bdiag[:], 0.0), writes=['bdiag'])
    op('pool', lambda: P.memset(bdiag[0:64, 0:64], 1.0 / 64), writes=['bdiag'])
    op('pool', lambda: P.memset(bdiag[64:128, 64:128], 1.0 / 64), writes=['bdiag'])
    dma('sp', normw[:], norm_w.ap(), writes=['normw'])
    dma('sp', gn[0:64, :], gainsT.ap(), writes=['gn'])
    dma('sp', gn[64:128, :], gainsT.ap(), writes=['gn'])
    op('dve', lambda: V.tensor_scalar(out=gnq[:], in0=gn[:], scalar1=0.125, scalar2=None, op0=ALU.mult),
       reads=['gn'], writes=['gnq'])
    dma('sp', nb31[:], rel_bias.ap()[31:32, :].partition_broadcast(128), writes=['nb31'])
    op('dve', lambda: V.tensor_scalar(out=nb31[:], in0=nb31[:], scalar1=-1.0, scalar2=None, op0=ALU.mult),
       writes=['nb31'])

    def load_w(dst, c0, ncols, key, dcol=0):
        src = w_in.ap().rearrange("(c p) n -> p c n", p=128)[:, :, c0:c0 + ncols]
        dma('pool', dst[:, :, dcol:dcol + ncols], src, writes=[key])

    pj_rot = [0]

    def proj_fm(wt, wkey, wc0, g8, banks=None):
        banks = banks or [B_PJ0, B_PJ1]
        bk = banks[pj_rot[0] % len(banks)]
        pj_rot[0] += 1
        for c in range(8):
            op('pe', lambda: T.matmul(PS[bk][:, :], wt[:, c, wc0:wc0 + 128], hT[:, c, g8 * 512:(g8 + 1) * 512],
                                      start=(c == 0), stop=(c == 7)),
               reads=[wkey, f'hT{g8}'], writes=[pk(bk)])
        return bk

    mark_h = AR.mark()

    hn_rot = [0]

    def headnorm(bk, gcol_ap, dsts):
        r2 = hn_rot[0] % 2
        hn_rot[0] += 1
        mb = [B_MS, LGB[0]][r2]
        sq_, rs_ = sqb[r2], rstd[r2]
        op('act', lambda: A.activation(out=sq_[:], in_=PS[bk][:, :], func=AF.Square), writes=[pk(bk), f'sqb{r2}'])
        op('pe', lambda: T.matmul(PS[mb][:, :], bdiag[:], sq_[:], start=True, stop=True),
           reads=['bdiag', f'sqb{r2}'], writes=[pk(mb)])
        op('act', lambda: A.activation(out=rs_[:], in_=PS[mb][:, :], func=AF.Ln, bias=epst[:, 0:1], scale=1.0),
           reads=['epst'], writes=[pk(mb), f'rstd{r2}'])
        op('act', lambda: A.activation(out=rs_[:], in_=rs_[:], func=AF.Exp, scale=-0.5), writes=[f'rstd{r2}'])
        for hf, (dst, key) in enumerate(dsts):
            r = slice(hf * 64, hf * 64 + 64)
            op('dve', lambda: V.scalar_tensor_tensor(out=dst, in0=PS[bk][r, :], scalar=gcol_ap[r, :], in1=rs_[r, :],
                                                     op0=ALU.mult, op1=ALU.mult),
               reads=[f'rstd{r2}', 'gn', 'gnq', pk(bk)], writes=[key])

    KC = AR.alloc("KC", [64, 2, 256], BF16)
    VC = AR.alloc("VC", [128, 2, 2, 129], BF16)
    mark_wB = AR.mark()
    wB = AR.alloc("wB", [128, 8, 768], BF16)
    kcT = AR.alloc("kcT", [128, 16, 256], BF16)
    vcT = AR.alloc("vcT", [128, 16, 256], BF16)
    mA = AR.mark()
    xt = [AR.alloc(f"xt{i}", [128, 1024], F32) for i in range(3)]
    xsq = AR.alloc("xsq", [128, 1024], BF16)
    xn = [AR.alloc(f"xn{i}", [128, 1024], BF16) for i in range(2)]
    ss = AR.alloc("ss", [128, 32], F32)
    load_w(wB, C_KC, 768, 'wB')

    def stageA1(t):
        b3 = t % 3
        dma('sp', xt[b3][:], xa.ap()[t * 128:(t + 1) * 128, :], writes=[f'xt{b3}'])
        op('act', lambda: A.activation(out=xsq[:], in_=xt[b3][:], func=AF.Square, accum_out=ss[:, t:t + 1]),
           reads=[f'xt{b3}'], writes=['xsq', f'ss{t}'])

    def stageA1b(t):
        op('dve', lambda: V.tensor_scalar(out=ss[:, t:t + 1], in0=ss[:, t:t + 1], scalar1=1.0 / 1024, scalar2=EPS,
                                          op0=ALU.mult, op1=ALU.add), writes=[f'ss{t}'])
        op('act', lambda: A.activation(out=ss[:, t:t + 1], in_=ss[:, t:t + 1], func=AF.Sqrt), writes=[f'ss{t}'])
        op('dve', lambda: V.reciprocal(out=ss[:, t:t + 1], in_=ss[:, t:t + 1]), writes=[f'ss{t}'])

    def stageA2a(t):
        b = t % 2
        b3 = t % 3
        op('act', lambda: A.activation(out=xn[b][:], in_=xt[b3][:], func=AF.Copy, scale=ss[:, t:t + 1]),
           reads=[f'xt{b3}', f'ss{t}'], writes=[f'xn{b}'])
        for c in range(8):
            op('pe', lambda: T.transpose(PSB[:, c * 128:(c + 1) * 128], xn[b][:, c * 128:(c + 1) * 128], identb[:]),
               reads=[f'xn{b}', 'identb'], writes=['psb'])

    def stageA2b(t):
        op('dve', lambda: V.tensor_tensor(out=hT[:, :, t * 128:(t + 1) * 128],
                                          in0=PSB[:, :].rearrange("p (c n) -> p c n", c=8),
                                          in1=normw[:, :].unsqueeze(2).to_broadcast([128, 8, 128]), op=ALU.mult),
           reads=['normw'], writes=['psb', f'hT{t // 4}'])

    defer = []

    def stageB1(g8):
        st = {}

        def p_kc():
            st['kc'] = proj_fm(wB, 'wB', 0, g8)

        def p_vc():
            st['vc'] = proj_fm(wB, 'wB', 128, g8)

        def e_kc():
            bk = st['kc']
            op('dve', lambda: V.tensor_copy(out=kcT[:, :, g8 * 32:(g8 + 1) * 32], in_=PS[bk][:, :].rearrange("p (n r) -> p r n", r=16)),
               writes=[pk(bk), 'kcT'])

        def e_vc():
            bk = st['vc']
            op('dve', lambda: V.tensor_copy(out=vcT[:, :, g8 * 32:(g8 + 1) * 32], in_=PS[bk][:, :].rearrange("p (n r) -> p r n", r=16)),
               writes=[pk(bk), 'vcT'])
        defer.extend([p_kc, p_vc, e_kc, e_vc])

    stageA1(0)
    stageA1b(0)
    for t in range(32):
        if t + 1 < 32:
            stageA1(t + 1)
        stageA2a(t)
        if t + 1 < 32:
            stageA1b(t + 1)
        stageA2b(t)
        if defer:
            defer.pop(0)()
        if t % 4 == 3:
            stageB1(t // 4)
    while defer:
        defer.pop(0)()
    S.barrier()
    AR.reset(mA)

    w1 = AR.alloc("w1", [128, 32, 256], BF16)
    w2 = AR.alloc("w2", [128, 2, 64], BF16)
    posT = AR.alloc("posT", [64, 32], BF16)
    pb = AR.alloc("pb", [128, 2], F32)
    GH = AR.alloc("GH", [128, 2, 256], BF16)
    u_t = AR.alloc("u_t", [128, 256], F32)
    t_t = AR.alloc("t_t", [128, 256], F32)
    gbc = AR.alloc("gbc", [128, 64], F32)
    kc32 = AR.alloc("kc32", [128, 64], F32)
    kcb = AR.alloc("kcb", [128, 64], BF16)
    dma('sp', gbc[:], gains.ap()[3:4, :].partition_broadcast(128), writes=['gbc'])
    dma('pool', VC[:, :, 0, 65:129], ovT.ap(), writes=['VCo'])
    dma('pool', VC[:, :, 1, 65:129], ovT.ap(), writes=['VCo'])
    op('pool', lambda: P.memset(VC[:, :, :, 64:65], 1.0), writes=['VC1'])
    op('dve', lambda: V.memset(GH[:], 0.0), writes=['GH'])
    for kind in range(2):
        srcT = kcT if kind == 0 else vcT
        w1src = cmp_w1.ap()[kind].rearrange("(l d) m -> d l m", d=64)
        dma('pool', w1[0:64, :, :], w1src, writes=['w1'])
        dma('pool', w1[64:128, :, :], w1src, writes=['w1'])
        dma('pool', w2[:, :, :], cmp_w2.ap()[kind].rearrange("(c p) d -> p c d", p=128), writes=['w2'])
        dma('pool', posT[:, :], cmp_pos.ap()[kind], writes=['posT'])
        for mc in range(2):
            for l in range(32):
                op('pe', lambda: T.matmul(PS[B_MS][:, 0:1], w1[0:64, l, mc * 128:(mc + 1) * 128], posT[:, l:l + 1],
                                          start=(l == 0), stop=(l == 31)),
                   reads=['w1', 'posT'], writes=[pk(B_MS)])
            op('dve', lambda: V.tensor_copy(out=pb[:, mc:mc + 1], in_=PS[B_MS][:, 0:1]), writes=[pk(B_MS), 'pb'])
        for g in range(2):
            r = slice(g * 64, g * 64 + 64)
            for mc in range(2):
                bk = [B_PJ0, B_PJ1][mc]
                for l in range(32):
                    rhs = srcT[r, l % 16, (l // 16):(l // 16) + 255]
                    op('pe', lambda: T.matmul(PS[bk][:, 0:255], w1[r, l, mc * 128:(mc + 1) * 128], rhs,
                                              start=(l == 0), stop=(l == 31)),
                       reads=['w1', 'kcT', 'vcT'], writes=[pk(bk)])
                op('act', lambda: A.activation(out=u_t[:, 0:255], in_=PS[bk][:, 0:255], func=AF.Identity,
                                               bias=pb[:, mc:mc + 1], scale=1.0),
                   reads=['pb'], writes=[pk(bk), 'u_t'])
                op('dve', lambda: V.tensor_tensor(out=t_t[:, 0:255], in0=u_t[:, 0:255], in1=u_t[:, 0:255], op=ALU.mult),
                   reads=['u_t'], writes=['t_t'])
                op('dve', lambda: V.tensor_scalar(out=t_t[:, 0:255], in0=t_t[:, 0:255], scalar1=0.044715, scalar2=1.0,
                                                  op0=ALU.mult, op1=ALU.add), writes=['t_t'])
                op('dve', lambda: V.tensor_tensor(out=t_t[:, 0:255], in0=t_t[:, 0:255], in1=u_t[:, 0:255], op=ALU.mult),
                   reads=['u_t'], writes=['t_t'])
                op('act', lambda: A.activation(out=t_t[:, 0:255], in_=t_t[:, 0:255], func=AF.Sigmoid,
                                               scale=1.5957691216057308), writes=['t_t'])
                op('dve', lambda: V.tensor_tensor(out=GH[:, mc, 0:255], in0=t_t[:, 0:255], in1=u_t[:, 0:255], op=ALU.mult),
                   reads=['u_t', 't_t'], writes=['GH'])
            for ct in range(2):
                for mc in range(2):
                    op('pe', lambda: T.matmul(PS[B_MS][:, 0:64], GH[:, mc, ct * 128:(ct + 1) * 128], w2[:, mc, :],
                                              start=(mc == 0), stop=(mc == 1)),
                       reads=['GH', 'w2'], writes=[pk(B_MS)])
                if kind == 1:
                    op('act', lambda: A.copy(out=VC[:, ct, g, 0:64], in_=PS[B_MS][:, 0:64]), writes=[pk(B_MS), 'VCv'])
                else:
                    op('act', lambda: A.activation(out=kc32[:], in_=PS[B_MS][:, 0:64], func=AF.Square),
                       writes=[pk(B_MS), 'kc32'])
                    op('dve', lambda: V.reduce_sum(out=small[:, 0:1], in_=kc32[:], axis=AX.X), reads=['kc32'], writes=['small'])
                    op('dve', lambda: V.tensor_scalar(out=small[:, 0:1], in0=small[:, 0:1], scalar1=1.0 / 64, scalar2=EPS,
                                                      op0=ALU.mult, op1=ALU.add), writes=['small'])
                    op('act', lambda: A.activation(out=small[:, 0:1], in_=small[:, 0:1], func=AF.Sqrt), writes=['small'])
                    op('dve', lambda: V.reciprocal(out=small[:, 0:1], in_=small[:, 0:1]), writes=['small'])
                    op('dve', lambda: V.scalar_tensor_tensor(out=kcb[:], in0=PS[B_MS][:, 0:64], scalar=small[:, 0:1],
                                                             in1=gbc[:], op0=ALU.mult, op1=ALU.mult),
                       reads=['gbc'], writes=[pk(B_MS), 'small', 'kcb'])
                    op('pe', lambda: T.transpose(PSB[0:64, 0:128], kcb[:, :], identb[:]), reads=['kcb', 'identb'], writes=['psb'])
                    op('act', lambda: A.copy(out=KC[0:64, g, ct * 128:(ct + 1) * 128], in_=PSB[0:64, 0:128]),
                       writes=['psb', 'KC'])
    S.barrier()
    AR.reset(mark_wB)

    KS = [AR.alloc(f"KS{g}", [128, 4096], BF16) for g in range(2)]
    KW = [AR.alloc(f"KW{g}", [64, 4096], BF16) for g in range(2)]
    VS = AR.alloc("VS", [128, 32, 2, 65], BF16)
    VW = AR.alloc("VW", [128, 32, 2, 65], BF16)
    mark_nsa = AR.mark()
    wB = AR.alloc("wB2", [128, 8, 768], BF16)
    load_w(wB, C_KC, 768, 'wB')
    for g in range(2):
        dma('pool', KS[g][64:128, :], oh64.ap(), writes=[f'KS{g}m'])
    op('pool', lambda: P.memset(VS[:, :, :, 64:65], 1.0), writes=['VS1'])
    op('pool', lambda: P.memset(VW[:, :, :, 64:65], 1.0), writes=['VW1'])
    prev = None
    for g8 in range(8):
        cs = slice(g8 * 512, (g8 + 1) * 512)
        for wc0_, gc_, dst_ in ((256, gn[:, 4:5], [(KS[0][0:64, cs], 'KS0d'), (KS[1][0:64, cs], 'KS1d')]),
                                (512, gn[:, 5:6], [(KW[0][0:64, cs], 'KW0'), (KW[1][0:64, cs], 'KW1')])):
            bk = proj_fm(wB, 'wB', wc0_, g8, PJ3)
            if prev is not None:
                headnorm(*prev)
            prev = (bk, gc_, dst_)
    headnorm(*prev)
    for t in range(32):
        bk = [B_PJ0, B_PJ1][t % 2]
        for j, wc in enumerate((384, 640)):
            for c in range(8):
                op('pe', lambda: T.matmul(PS[bk][:, j * 128:(j + 1) * 128], hT[:, c, t * 128:(t + 1) * 128],
                                          wB[:, c, wc:wc + 128], start=(c == 0), stop=(c == 7)),
                   reads=['wB', f'hT{t // 4}'], writes=[pk(bk)])
        op('act', lambda: A.copy(out=VS[:, t, :, 0:64], in_=PS[bk][:, 0:128].rearrange("p (g d) -> p g d", g=2)),
           writes=[pk(bk), 'VS'])
        op('dve', lambda: V.tensor_copy(out=VW[:, t, :, 0:64], in_=PS[bk][:, 128:256].rearrange("p (g d) -> p g d", g=2)),
           writes=[pk(bk), 'VW'])
    S.barrier()
    AR.reset(mark_nsa)

    lg_rot = [0]
    pt_rot = [0]
    os_rot = [0]
    pend_epi = []
    bg = []

    def attention_group(G, qt, K, tl, epilogue, qkey='QT'):
        n = len(tl)
        pend = []
        for i in range(n + DEPTH):
            if i == min(DEPTH, n) and pend_epi:
                pend_epi.pop(0)()
            if i < n:
                kl, vl, s_lo, s_hi, Eap, e_lo, e_n, rkeys = tl[i]
                a = (s_lo - 4 * G) * 128
                b_ = (s_hi - 4 * G) * 128
                lb = LGB[lg_rot[0] % NLG]
                lg_rot[0] += 1
                pb_ = pt_rot[0] % NPT
                pt_rot[0] += 1
                op('pe', lambda: T.matmul(PS[lb][:, a:b_], kl, qt[0:K, s_lo * 128:s_hi * 128], start=True, stop=True),
                   reads=list(rkeys) + [qkey], writes=[pk(lb)])
                op('act', lambda: A.activation(out=PT[pb_][:, a:b_], in_=PS[lb][:, a:b_], func=AF.Exp),
                   writes=[pk(lb), f'PT{pb_}'])
                if Eap is not None and e_n > 0:
                    op('dve', lambda: V.tensor_tensor(out=PT[pb_][:, a:a + e_n * 128], in0=PT[pb_][:, a:a + e_n * 128],
                                                      in1=Eap[:, e_lo * 128:(e_lo + e_n) * 128], op=ALU.mult),
                       reads=['E', 'Ec', 'E0', 'E1'], writes=[f'PT{pb_}'])
                pend.append((vl, a, b_, pb_, rkeys, i))
                if bg:
                    bg.pop(0)()
            if i >= DEPTH:
                vl, a, b_, pb_, rkeys, ii = pend.pop(0)
                op('pe', lambda: T.matmul(PS[B_OT][0:65, a:b_], vl, PT[pb_][:, a:b_], start=(ii == 0), stop=(ii == n - 1)),
                   reads=list(rkeys) + [f'PT{pb_}'], writes=[pk(B_OT)])
        ob = os_rot[0] % 2
        os_rot[0] += 1
        op('dve', lambda: V.tensor_copy(out=OS[ob][0:65, :], in_=PS[B_OT][0:65, :]), writes=[pk(B_OT), f'OS{ob}'])

        def fin():
            for j in range(4):
                op('pe', lambda: T.transpose(PS[B_TOK][:, j * 65:(j + 1) * 65], OS[ob][0:65, j * 128:(j + 1) * 128], ident[0:65, 0:65]),
                   reads=[f'OS{ob}', 'ident'], writes=[pk(B_TOK)])
            epilogue(G)
        pend_epi.append(fin)

    def flush_epi():
        while pend_epi:
            pend_epi.pop(0)()

    def attention(qt, K, tiles, epilogue, qkey='QT'):
        for G in range(4):
            attention_group(G, qt, K, tiles[G], epilogue, qkey)

    def flush_bg():
        while bg:
            bg.pop(0)()

    def tok_view():
        return PS[B_TOK][:, 0:260].rearrange("p (j d) -> p j d", j=4)

    GB = AR.alloc("GB", [128, 16, 24], F32)
    wg = AR.alloc("wg", [128, 8, 24], BF16)
    imp = AR.alloc("imp", [128, 16, 2, 64], F32)
    SM = [AR.alloc(f"SM{g}", [128, 2048], BF16) for g in range(2)]
    QB = [AR.alloc(f"QB{i}", [128, 2048], BF16) for i in range(2)]
    wq = [AR.alloc(f"wq{i}", [128, 8, 128], BF16) for i in range(2)]
    rs = AR.alloc("rs", [128, 4], F32)
    fct = AR.alloc("fct", [128, 4], F32)
    mark_att = AR.mark()
    cvt = AR.alloc("cvt", [128, 2, 2048], BF16)
    itmp = AR.alloc("itmp", [128, 2, 2, 64], F32)

    load_w(wg, C_GB, 24, 'wg')
    dma('pool', cvt[:], cvd.ap(), writes=['cvt'])
    for s in range(16):
        tcol = 2048 + s * 128
        for c in range(8):
            op('pe', lambda: T.matmul(PS[B_MS][:, s * 24:(s + 1) * 24], hT[:, c, tcol:tcol + 128], wg[:, c, :],
                                      start=(c == 0), stop=(c == 7)),
               reads=['wg', f'hT{4 + s // 4}'], writes=[pk(B_MS)])
    op('act', lambda: A.activation(out=GB[:, :, :], in_=PS[B_MS][:, 0:384].rearrange("p (s k) -> p s k", s=16), func=AF.Sigmoid),
       writes=[pk(B_MS), 'GB'])

    wq_n = [0]

    def prefetch_wq(i):
        w_ = wq_n[0] % 2
        load_w(wq[w_], C_QB + (i % 4) * 128, 128, f'wq{w_}')

    def project_qb(i):
        w_ = wq_n[0] % 2
        wq_n[0] += 1
        prefetch_wq(i + 1)
        prev = None
        for G in range(4):
            bk = proj_fm(wq[w_], f'wq{w_}', 0, 4 + G, PJ3)
            cs = slice(G * 512, (G + 1) * 512)
            if prev is not None:
                headnorm(*prev)
            prev = (bk, gnq[:, 2:3], [(QB[0][0:64, cs], 'QT'), (QB[1][0:64, cs], 'QT')])
        headnorm(*prev)

    prefetch_wq(0)

    def cmpA(i, hd, G):
        g = i // 2
        h = 2 * i + hd
        cs = slice(G * 512, (G + 1) * 512)
        pts = []
        for ct in range(2):
            lb = [0, 1][ct]
            pb_ = pt_rot[0] % NPT
            pt_rot[0] += 1
            op('pe', lambda: T.matmul(PS[lb][:, :], KC[0:64, g, ct * 128:(ct + 1) * 128], QB[hd][0:64, cs],
                                      start=True, stop=True), reads=['KC', 'QT'], writes=[pk(lb)])
            op('act', lambda: A.activation(out=PT[pb_][:, :], in_=PS[lb][:, :], func=AF.Exp),
               writes=[pk(lb), f'PT{pb_}'])
            op('dve', lambda: V.tensor_tensor(out=PT[pb_][:, :], in0=PT[pb_][:, :], in1=cvt[:, ct, cs], op=ALU.mult),
               reads=['cvt'], writes=[f'PT{pb_}'])
            pts.append(pb_)
        return pts

    def cmpB(i, hd, G, pts, tokb):
        g = i // 2
        h = 2 * i + hd
        for j2 in range(2):
            bk = tokb[j2]
            for jj in range(2):
                j = j2 * 2 + jj
                for ct in range(2):
                    op('pe', lambda: T.matmul(PS[bk][:, jj * 129:(jj + 1) * 129], PT[pts[ct]][:, j * 128:(j + 1) * 128],
                                              VC[:, ct, g, :], start=(ct == 0), stop=(ct == 1)),
                       reads=[f'PT{pts[ct]}', 'VCo', 'VC1', 'VCv'], writes=[pk(bk)])
        bks = tokb
        s0s = [4 * G, 4 * G + 2]
        pvs = [PS[b_][:, 0:258].rearrange("p (j c) -> p j c", j=2) for b_ in bks]
        for j2 in range(2):
            op('dve', lambda: V.tensor_scalar(out=rs[:, 2 * j2:2 * j2 + 2], in0=PS[bks[j2]][:, 64:258:129], scalar1=1e-30,
                                              scalar2=None, op0=ALU.add), writes=[pk(bks[j2]), f'rs{j2}'])
        for j2 in range(2):
            op('dve', lambda: V.reciprocal(out=rs[:, 2 * j2:2 * j2 + 2], in_=rs[:, 2 * j2:2 * j2 + 2]), writes=[f'rs{j2}'])
        for j2 in range(2):
            s0 = s0s[j2]
            op('dve', lambda: V.tensor_tensor(out=fct[:, 2 * j2:2 * j2 + 2], in0=rs[:, 2 * j2:2 * j2 + 2],
                                              in1=GB[:, s0:s0 + 2, 3 * h], op=ALU.mult),
               reads=['GB', f'rs{j2}'], writes=[f'fct{j2}'])
        for j2 in range(2):
            s0 = s0s[j2]
            if h % 4 == 0:
                op('dve', lambda: V.tensor_tensor(out=imp[:, s0:s0 + 2, g, :], in0=pvs[j2][:, :, 65:129],
                                                  in1=rs[:, 2 * j2:2 * j2 + 2].unsqueeze(2).to_broadcast([128, 2, 64]), op=ALU.mult),
                   reads=[f'rs{j2}', pk(bks[j2])], writes=[f'imp{s0}', f'imp{s0 + 1}'])
            else:
                op('dve', lambda: V.tensor_tensor(out=itmp[:, j2, :, :], in0=pvs[j2][:, :, 65:129],
                                                  in1=rs[:, 2 * j2:2 * j2 + 2].unsqueeze(2).to_broadcast([128, 2, 64]), op=ALU.mult),
                   reads=[f'rs{j2}', pk(bks[j2])], writes=[f'itmp{j2}'])
        for j2 in range(2):
            s0 = s0s[j2]
            op('dve', lambda: V.tensor_tensor(out=o_b[:, s0:s0 + 2, h * 64:(h + 1) * 64], in0=pvs[j2][:, :, 0:64],
                                              in1=fct[:, 2 * j2:2 * j2 + 2].unsqueeze(2).to_broadcast([128, 2, 64]), op=ALU.mult),
               reads=[f'fct{j2}', pk(bks[j2])], writes=[f'ob{s0}', f'ob{s0 + 1}'])
        if h % 4 != 0:
            for j2 in range(2):
                s0 = s0s[j2]
                op('pool', lambda: P.tensor_tensor(out=imp[:, s0:s0 + 2, g, :], in0=imp[:, s0:s0 + 2, g, :], in1=itmp[:, j2, :, :],
                                                   op=ALU.add), reads=[f'itmp{j2}'], writes=[f'imp{s0}', f'imp{s0 + 1}'])


    units = [(i, hd, G) for i in range(4) for hd in range(2) for G in range(4)]
    project_qb(0)
    stA = cmpA(*units[0])
    for k, u in enumerate(units):
        nxt = units[k + 1] if k + 1 < len(units) else None
        if nxt is not None:
            if nxt[0] != u[0]:
                project_qb(nxt[0])
            stN = cmpA(*nxt)
        cmpB(*u, stA, [[B_TOK, B_OT], [B_PJ0, B_PJ1]][k % 2])
        if nxt is not None:
            stA = stN

    S.barrier()
    AR.reset(mark_att)
    msp = AR.alloc("msp", [128, 8, 64], F32)
    msa = AR.alloc("msa", [128, 8, 64], F32)
    stg = AR.alloc("stg", [128, 8, 128], BF16)
    wrk = AR.alloc("wrk", [128, 8, 64], F32)
    wrk2 = AR.alloc("wrk2", [128, 8, 64], F32)
    mx8 = AR.alloc("mx8", [128, 8, 16], F32)
    op('dve', lambda: V.memset(stg[:], 0.0), writes=['stg'])
    for g in range(2):
        for half in range(2):
            sl = slice(half * 8, half * 8 + 8)
            dma('sp', msp[:], ms_past.ap()[:, sl, :], writes=['msp'])
            dma('sp', msa[:], ms_add.ap()[:, sl, :], writes=['msa'])
            op('dve', lambda: V.tensor_tensor(out=wrk[:], in0=imp[:, sl, g, :], in1=msp[:, :, :], op=ALU.mult),
               reads=[f'imp{s_}' for s_ in range(16)] + ['msp'], writes=['wrk'])
            op('dve', lambda: V.tensor_tensor(out=wrk[:], in0=wrk[:], in1=msa[:, :, :], op=ALU.add), reads=['msa'], writes=['wrk'])
            mxa = [f'mxa{s8}' for s8 in range(8)]
            mxb = [f'mxb{s8}' for s8 in range(8)]
            w2k = [f'w2_{s8}' for s8 in range(8)]
            for s8 in range(8):
                op('dve', lambda: V.max(out=mx8[:, s8, 0:8], in_=wrk[:, s8, :]), reads=['wrk'], writes=[mxa[s8]])
            for s8 in range(8):
                op('dve', lambda: V.match_replace(out=wrk2[:, s8, :], in_to_replace=mx8[:, s8, 0:8], in_values=wrk[:, s8, :],
                                                  imm_value=-3e30), reads=['wrk', mxa[s8]], writes=[w2k[s8]])
            for s8 in range(8):
                op('dve', lambda: V.max(out=mx8[:, s8, 8:16], in_=wrk2[:, s8, :]), reads=[w2k[s8]], writes=[mxb[s8]])
            op('dve', lambda: V.tensor_scalar(out=mx8[:, :, 15:16], in0=mx8[:, :, 15:16], scalar1=-1e29, scalar2=None, op0=ALU.max),
               writes=mxb)
            op('dve', lambda: V.tensor_tensor(out=wrk2[:], in0=wrk[:], in1=mx8[:, :, 15:16].to_broadcast([128, 8, 64]), op=ALU.is_ge),
               reads=['wrk'] + mxb, writes=w2k)
            op('dve', lambda: V.tensor_scalar(out=stg[:, :, 64:128], in0=wrk2[:], scalar1=-1.0, scalar2=MASKV,
                                              op0=ALU.add, op1=ALU.mult), reads=w2k, writes=['stg'])
            for s8 in range(8):
                op('pe', lambda: T.transpose(PSB[:, s8 * 128:(s8 + 1) * 128], stg[:, s8, :], identb[:]),
                   reads=['stg', 'identb'], writes=['psb'])
            op('act', lambda: A.copy(out=SM[g][64:128, half * 1024:(half + 1) * 1024], in_=PSB[64:128, :]),
               writes=['psb', f'SM{g}'])

    S.barrier()
    AR.reset(mark_att)
    Est = AR.alloc("Est", [128, 1024], F32)
    Et = AR.alloc("Et", [128, 1024], BF16)
    Etc = AR.alloc("Etc", [128, 640], BF16)
    def load_E(src_ap, ncols, dst, hcol, key='E'):
        dma('sp', Est[:, 0:ncols], src_ap, writes=['Est'])
        op('act', lambda: A.activation(out=dst[:, 0:ncols], in_=Est[:, 0:ncols], func=AF.Exp, bias=nb31[:, hcol:hcol + 1], scale=1.0),
           reads=['Est', 'nb31'], writes=[key])

    def nsa_epilogue(h, br, first=False):
        def ep(G):
            tv = tok_view()
            op('dve', lambda: V.tensor_scalar(out=rs[:, 0:4], in0=PS[B_TOK][:, 64:260:65], scalar1=1e-30, scalar2=None, op0=ALU.add),
               writes=[pk(B_TOK), 'rs'])
            op('dve', lambda: V.reciprocal(out=rs[:, 0:4], in_=rs[:, 0:4]), writes=['rs'])
            op('dve', lambda: V.tensor_tensor(out=fct[:, 0:4], in0=rs[:, 0:4], in1=GB[:, 4 * G:4 * G + 4, 3 * h + br], op=ALU.mult),
               reads=['GB', 'rs'], writes=['fct'])
            for j in range(4):
                s = 4 * G + j
                op('dve', lambda: V.scalar_tensor_tensor(out=o_b[:, s, h * 64:(h + 1) * 64], in0=tv[:, j, 0:64],
                                                         scalar=fct[:, j:j + 1], in1=o_b[:, s, h * 64:(h + 1) * 64],
                                                         op0=ALU.mult, op1=ALU.add),
                   reads=['fct', pk(B_TOK)], writes=[f'ob{s}'])
        return ep

    def dense_tiles(Kt, Vt, vsel, kkeys, Eap):
        tiles = []
        for G in range(4):
            tl = []
            for u in range(16):
                s_lo, s_hi = 4 * G, 4 * G + 4
                d_lo = 16 + s_lo - u
                e_n = max(0, min(s_hi, u + 8 - 16) - s_lo) if d_lo <= 7 else 0
                tl.append((Kt[:, u * 128:(u + 1) * 128], vsel(u), s_lo, s_hi, Eap if e_n > 0 else None, d_lo, e_n, kkeys))
            for u in range(4 * G + 4):
                s_lo, s_hi = max(u, 4 * G), 4 * G + 4
                d_lo = s_lo - u
                e_n = max(0, min(s_hi, u + 8) - s_lo) if d_lo <= 7 else 0
                ua = 16 + u
                tl.append((Kt[:, ua * 128:(ua + 1) * 128], vsel(ua), s_lo, s_hi, Eap if e_n > 0 else None, d_lo, e_n, kkeys))
            tiles.append(tl)
        return tiles

    for i in range(4):
        g = i // 2
        project_qb(i)
        for hd in range(2):
            h = 2 * i + hd
            op('dve', lambda: V.tensor_copy(out=QB[hd][64:128, :], in_=SM[g][64:128, :]), reads=[f'SM{g}'], writes=['QT'])
            load_E(es.ap()[h], 1024, Et, 8 + h)
            tiles = dense_tiles(KS[g], VS, lambda ua: VS[:, ua, g, :], ['KS0d', 'KS1d', 'KS0m', 'KS1m', 'VS', 'VS1'], Et)
            attention(QB[hd], 128, tiles, nsa_epilogue(h, 1))
            load_E(ewo.ap()[h], 640, Et, 8 + h)
            load_E(ewc.ap()[h], 640, Etc, 8 + h, key='Ec')
            for G in range(4):
                tl = []
                order = []
                if G == 0:
                    order = [('c', 15), ('c', 14), ('c', 13), ('c', 12)] + [('o', u) for u in range(4)]
                else:
                    order = [('o', 4 * G - 1)] + [('o', u) for u in range(4 * G - 4, 4 * G - 1)] + [('o', u) for u in range(4 * G, 4 * G + 4)]
                wt = []
                wkeys = ['KW0', 'KW1', 'VW', 'VW1']
                for kind, u in order:
                    if kind == 'c':
                        s_lo, s_hi = max(0, u - 15), min(4, u - 11)
                        d_lo = 16 + s_lo - u
                        wt.append((KW[g][0:64, u * 128:(u + 1) * 128], VW[:, u, g, :], s_lo, s_hi, Etc, d_lo, s_hi - s_lo, wkeys))
                    else:
                        s_lo, s_hi = max(u, 4 * G), min(4 * G + 4, u + 5)
                        d_lo = s_lo - u
                        ua = 16 + u
                        wt.append((KW[g][0:64, ua * 128:(ua + 1) * 128], VW[:, ua, g, :], s_lo, s_hi, Et, d_lo, s_hi - s_lo, wkeys))
                attention_group(G, QB[hd], 64, wt, nsa_epilogue(h, 2))
    flush_epi()
    S.barrier()
    AR.reset(mark_h)

    o_a = AR.alloc("o_a", [128, 16, 512], F32)
    mark_g = AR.mark()
    KA = [AR.alloc(f"KA{i}", [128, 4096], BF16) for i in range(2)]
    QA = [AR.alloc(f"QA{i}", [128, 2048], BF16) for i in range(2)]
    VA = AR.alloc("VA", [128, 32, 2, 65], BF16)
    wa2 = [AR.alloc(f"wa{i}", [128, 8, 384], BF16) for i in range(2)]
    Est = AR.alloc("Est2", [128, 1024], F32)
    Et = AR.alloc("Et2", [128, 1024], BF16)
    rs = AR.alloc("rs2", [128, 4], F32)
    stg = AR.alloc("stg2", [128, 16, 128], BF16)
    map_ = AR.alloc("map", [128, 16, 16], F32)
    maa = AR.alloc("maa", [128, 16, 16], F32)
    mao = AR.alloc("mao", [128, 16, 16], F32)
    km32 = AR.alloc("km32", [64, 16], F32)
    kmb = AR.alloc("kmb", [64, 16], BF16)
    wk16 = AR.alloc("wk16", [128, 256], F32)
    wk16b = AR.alloc("wk16b", [128, 256], F32)
    mx8 = AR.alloc("mx8b", [128, 128], F32)
    for j_, c0_ in enumerate((C_QA, C_KA, C_VA)):
        load_w(wa2[0], c0_, 128, 'wa0', j_ * 128)
    dma('sp', map_[:], ma_past.ap(), writes=['map'])
    dma('sp', maa[:], ma_add.ap(), writes=['maa'])
    dma('sp', mao[:], ma_own.ap(), writes=['mao'])
    op('dve', lambda: V.memset(stg[:], 0.0), writes=['stg'])
    op('pool', lambda: P.memset(VA[:, :, :, 64:65], 1.0), writes=['VA1'])
    for hd in range(2):
        op('pool', lambda: P.memset(KA[hd][64:128, :], 0.0), writes=[f'KA{hd}m'])
        dma('pool', KA[hd][64:80, :], oh16.ap(), writes=[f'KA{hd}m'])

    def moba_epilogue(h):
        def ep(G):
            tv = tok_view()
            op('dve', lambda: V.tensor_scalar(out=rs[:, 0:4], in0=PS[B_TOK][:, 64:260:65], scalar1=1e-30, scalar2=None, op0=ALU.add),
               writes=[pk(B_TOK), 'rs'])
            op('dve', lambda: V.reciprocal(out=rs[:, 0:4], in_=rs[:, 0:4]), writes=['rs'])
            for j in range(4):
                s = 4 * G + j
                op('dve', lambda: V.tensor_scalar(out=o_a[:, s, h * 64:(h + 1) * 64], in0=tv[:, j, 0:64],
                                                  scalar1=rs[:, j:j + 1], scalar2=None, op0=ALU.mult),
                   reads=['rs', pk(B_TOK)], writes=[f'oa{s}'])
        return ep

    def load_wa(i):
        w_ = wa2[i % 2]
        load_w(w_, C_QA + i * 128, 128, f'wa{i % 2}', 0)
        load_w(w_, C_KA + i * 128, 128, f'wa{i % 2}', 128)
        load_w(w_, C_VA + i * 128, 128, f'wa{i % 2}', 256)
    Etg = [Et, AR.alloc("Et3", [128, 1024], BF16)]

    def v_tile(wa, wak, t):
        bk = [1, B_OT][t % 2]
        for c in range(8):
            op('pe', lambda: T.matmul(PS[bk][:, 0:128], hT[:, c, t * 128:(t + 1) * 128], wa[:, c, 256:384],
                                      start=(c == 0), stop=(c == 7)),
               reads=[wak, f'hT{t // 4}'], writes=[pk(bk)])
        op('act', lambda: A.copy(out=VA[:, t, :, 0:64], in_=PS[bk][:, 0:128].rearrange("p (g d) -> p g d", g=2)),
           writes=[pk(bk), 'VA'])

    def setup_ops(hd, h):
        ops = []
        qk = f'QT{hd}'
        w3 = lambda t_: t_[:, :].rearrange("p (s j) -> p s j", s=16)
        ops.append(lambda: op('dve', lambda: V.reduce_sum(out=km32[:, :], in_=KA[hd][0:64, :].rearrange("p (j n) -> p j n", j=16), axis=AX.X),
                              reads=[f'KA{hd}d'], writes=['km32']))
        ops.append(lambda: op('dve', lambda: V.tensor_scalar(out=kmb[:, :], in0=km32[:, :], scalar1=1.0 / 256, scalar2=None, op0=ALU.mult),
                              reads=['km32'], writes=['kmb']))
        for s in range(16):
            ops.append(lambda s=s: op('pe', lambda: T.matmul(PS[B_MS][:, s * 16:(s + 1) * 16], QA[hd][0:64, s * 128:(s + 1) * 128], kmb[:, :],
                                                             start=True, stop=True), reads=[qk, 'kmb'], writes=[pk(B_MS)]))
        ops.append(lambda: op('dve', lambda: V.tensor_tensor(out=w3(wk16), in0=w3(PS[B_MS][:, 0:256]), in1=map_[:, :, :], op=ALU.mult),
                              reads=['map'], writes=[pk(B_MS), 'wk16']))
        ops.append(lambda: op('dve', lambda: V.tensor_tensor(out=w3(wk16), in0=w3(wk16), in1=maa[:, :, :], op=ALU.add), reads=['maa'], writes=['wk16']))
        mxk = [f'mx8_{s_}' for s_ in range(16)]
        for s in range(16):
            ops.append(lambda s=s: op('dve', lambda: V.max(out=mx8[:, s * 8:(s + 1) * 8], in_=wk16[:, s * 16:(s + 1) * 16]),
                                      reads=['wk16'], writes=[mxk[s]]))
        thr = mx8[:, :].rearrange("p (s k) -> p s k", s=16)[:, :, 2:3].to_broadcast([128, 16, 16])
        ops.append(lambda: op('dve', lambda: V.tensor_tensor(out=w3(wk16b), in0=w3(wk16), in1=thr, op=ALU.is_ge),
                              reads=['wk16'] + mxk, writes=['wk16b']))
        ops.append(lambda: op('dve', lambda: V.tensor_tensor(out=w3(wk16b), in0=w3(wk16b), in1=map_[:, :, :], op=ALU.mult), reads=['map'], writes=['wk16b']))
        ops.append(lambda: op('dve', lambda: V.tensor_tensor(out=w3(wk16b), in0=w3(wk16b), in1=mao[:, :, :], op=ALU.add), reads=['mao'], writes=['wk16b']))
        ops.append(lambda: op('dve', lambda: V.tensor_scalar(out=stg[:, :, 64:80], in0=w3(wk16b), scalar1=-1.0, scalar2=MASKV,
                                                             op0=ALU.add, op1=ALU.mult), reads=['wk16b'], writes=['stg']))
        for half in range(2):
            for s8 in range(8):
                s = half * 8 + s8
                ops.append(lambda s=s, s8=s8: op('pe', lambda: T.transpose(PSB[:, s8 * 128:(s8 + 1) * 128], stg[:, s, :], identb[:]),
                                                 reads=['stg', 'identb'], writes=['psb']))
            ops.append(lambda half=half: op('act', lambda: A.copy(out=QA[hd][64:128, half * 1024:(half + 1) * 1024], in_=PSB[64:128, :]),
                                            writes=['psb', qk]))
        if hd == 1:
            ops.append(lambda: load_E(ea.ap()[h], 1024, Etg[hd], h, key=f'E{hd}'))
        return ops

    for i in range(4):
        wa = wa2[i % 2]
        wak = f'wa{i % 2}'
        if i + 1 < 4:
            load_wa(i + 1)
        load_E(ea.ap()[2 * i], 1024, Etg[0], 2 * i, key='E0')
        vt = 0
        items = []
        for g8 in range(8):
            cs = slice(g8 * 512, (g8 + 1) * 512)
            items.append((128, g8, gn[:, 1:2], [(KA[0][0:64, cs], 'KA0d'), (KA[1][0:64, cs], 'KA1d')]))
        for G in range(4):
            cs = slice(G * 512, (G + 1) * 512)
            items.append((0, 4 + G, gnq[:, 0:1], [(QA[0][0:64, cs], 'QT0'), (QA[1][0:64, cs], 'QT1')]))
        prev = None
        for wc0, g8, gcol, dsts in items:
            bk = proj_fm(wa, wak, wc0, g8, PJ3)
            if prev is not None:
                headnorm(*prev)
            prev = (bk, gcol, dsts)
            for _ in range(3):
                if vt < 32:
                    v_tile(wa, wak, vt)
                    vt += 1
        headnorm(*prev)
        while vt < 32:
            v_tile(wa, wak, vt)
            vt += 1
        for f_ in setup_ops(0, 2 * i):
            f_()
        bg.extend(setup_ops(1, 2 * i + 1))
        for hd in range(2):
            h = 2 * i + hd
            if hd == 1:
                flush_bg()
            tiles = dense_tiles(KA[hd], VA, lambda ua: VA[:, ua, hd, :], [f'KA{hd}d', f'KA{hd}m', 'VA', 'VA1'], Etg[hd])
            attention(QA[hd], 128, tiles, moba_epilogue(h), qkey=f'QT{hd}')
    flush_epi()
    if debug:
        for s in range(16):
            dma('sp', dbg['oa'].ap()[s * 128:(s + 1) * 128, :], o_a[:, s, :], reads=[f'oa{s}'], writes=['dbgoa'])
            dma('sp', dbg['ob'].ap()[s * 128:(s + 1) * 128, :], o_b[:, s, :], reads=[f'ob{s}'], writes=['dbgob'])
    S.barrier()
    AR.reset(mark_g)

    yT = [AR.alloc(f"yT{b}", [128, 4, 2048], BF16) for b in range(2)]
    wz = [AR.alloc(f"wz{i}", [128, 8, 128], BF16) for i in range(2)]
    sz = [AR.alloc(f"sz{i}", [128, 512], F32) for i in range(2)]
    tA = [AR.alloc(f"tA{i}", [128, 512], F32) for i in range(2)]
    m_h1 = AR.mark()
    AR.reset(AR.offs['PT0'])
    xo = [AR.alloc(f"xo{i}", [128, 512], F32) for i in range(3)]
    res = [AR.alloc(f"res{i}", [128, 512], F32) for i in range(2)]
    assert AR.cur <= AR.offs['small']
    AR.reset(m_h1)
    wo = AR.alloc("wo", [128, 8, 1024], BF16)
    dma('pool', wo[:], w_out.ap().rearrange("(c p) n -> p c n", p=128), writes=['wo'])
    it = 0
    wi = 0
    zcols = [C_ZA + ci * 128 for ci in range(4)] + [C_ZB + ci * 128 for ci in range(4)]
    load_w(wz[0], zcols[0], 128, 'wz0')
    for br in range(2):
        osrc = o_a if br == 0 else o_b
        okey = 'oa' if br == 0 else 'ob'
        for ci in range(4):
            wb = wi % 2
            wi += 1
            if wi < 8:
                load_w(wz[wi % 2], zcols[wi], 128, f'wz{wi % 2}')
            for G in range(4):
                cs = slice(G * 512, (G + 1) * 512)
                b2 = it % 2
                it += 1
                tb = [B_TOK, B_OT][b2]
                bk = proj_fm(wz[wb], f'wz{wb}', 0, 4 + G)
                op('act', lambda: A.activation(out=sz[b2][:], in_=PS[bk][:, :], func=AF.Silu), writes=[pk(bk), f'sz{b2}'])
                for j in range(4):
                    s = 4 * G + j
                    op('pe', lambda: T.transpose(PS[tb][:, j * 128:(j + 1) * 128], osrc[:, s, ci * 128:(ci + 1) * 128], ident[:]),
                       reads=[f'{okey}{s}', 'ident'], writes=[pk(tb)])
                op('dve', lambda: V.tensor_tensor(out=yT[br][:, ci, cs], in0=PS[tb][:, :], in1=sz[b2][:], op=ALU.mult),
                   reads=[f'sz{b2}'], writes=[pk(tb), f'yT{br}'])
    S.barrier()
    m_end = AR.mark()
    AR.reset(AR.offs['o_b'])
    mg = AR.alloc("mg", [128, 8, 2048], BF16)
    AR.reset(AR.offs['o_a'])
    wgm = [AR.alloc(f"wgm{i}", [128, 8, 256], BF16) for i in range(2)]
    wbr = [AR.alloc(f"wbr{i}", [128, 2, 4, 128], BF16) for i in range(2)]
    assert AR.cur <= AR.offs['o_a'] + 32768
    AR.reset(m_end)
    it = 0

    def load_merge_w(m):
        wb = m % 2
        load_w(wgm[wb], C_GM + m * 128, 128, f'wgm{wb}', 0)
        load_w(wgm[wb], C_GM + 1024 + m * 128, 128, f'wgm{wb}', 128)
        dma('pool', wbr[wb][:, 0, :, :], w_ba.ap().rearrange("(c p) n -> p c n", p=128)[:, :, m * 128:(m + 1) * 128], writes=[f'wbr{wb}'])
        dma('pool', wbr[wb][:, 1, :, :], w_bb.ap().rearrange("(c p) n -> p c n", p=128)[:, :, m * 128:(m + 1) * 128], writes=[f'wbr{wb}'])
    load_merge_w(0)
    for m in range(8):
        wb = m % 2
        if m + 1 < 8:
            load_merge_w(m + 1)
        for G in range(4):
            cs = slice(G * 512, (G + 1) * 512)
            for br in range(2):
                b2 = it % 2
                it += 1
                ob = [B_TOK, B_OT][b2]
                bk = proj_fm(wgm[wb], f'wgm{wb}', br * 128, 4 + G)
                op('act', lambda: A.activation(out=sz[b2][:], in_=PS[bk][:, :], func=AF.Sigmoid), writes=[pk(bk), f'sz{b2}'])
                for ci in range(4):
                    op('pe', lambda: T.matmul(PS[ob][:, :], wbr[wb][:, br, ci, :], yT[br][:, ci, cs], start=(ci == 0), stop=(ci == 3)),
                       reads=[f'wbr{wb}', f'yT{br}'], writes=[pk(ob)])
                if br == 0:
                    ta = tA[G % 2]
                    op('dve', lambda: V.tensor_tensor(out=ta[:], in0=PS[ob][:, :], in1=sz[b2][:], op=ALU.mult),
                       reads=[f'sz{b2}'], writes=[pk(ob), f'tA{G % 2}'])
                else:
                    op('dve', lambda: V.tensor_tensor(out=sz[b2][:], in0=PS[ob][:, :], in1=sz[b2][:], op=ALU.mult),
                       writes=[pk(ob), f'sz{b2}'])
                    op('pool', lambda: P.tensor_tensor(out=mg[:, m, cs], in0=tA[G % 2][:], in1=sz[b2][:], op=ALU.add),
                       reads=[f'tA{G % 2}', f'sz{b2}'], writes=['mg'])
    def load_x(k_):
        s_, hf_ = k_ // 2, k_ % 2
        dma('pool', xo[k_ % 3][:], xa.ap()[2048 + s_ * 128:2048 + (s_ + 1) * 128, hf_ * 512:(hf_ + 1) * 512], writes=[f'xo{k_ % 3}'])
    load_x(0)
    load_x(1)
    for k in range(32):
        s, hf = k // 2, k % 2
        if k + 2 < 32:
            load_x(k + 2)
        bk = [B_PJ0, B_PJ1][k % 2]
        xb, rb = k % 3, k % 2
        for m in range(8):
            op('pe', lambda: T.matmul(PS[bk][:, :], mg[:, m, s * 128:(s + 1) * 128], wo[:, m, hf * 512:(hf + 1) * 512],
                                      start=(m == 0), stop=(m == 7)), reads=['mg', 'wo'], writes=[pk(bk)])
        op('dve', lambda: V.tensor_tensor(out=res[rb][:], in0=PS[bk][:, :], in1=xo[xb][:], op=ALU.add),
           reads=[f'xo{xb}'], writes=[pk(bk), f'res{rb}'])
        dma('sp', out.ap()[s * 128:(s + 1) * 128, hf * 512:(hf + 1) * 512], res[rb][:], reads=[f'res{rb}'], writes=[f'outd{k}'])
    S.barrier()
    return nc


def _t5_bucket(dist):
    n = np.maximum(dist, 0)
    nf = np.maximum(n, 16).astype(np.float32)
    large = 16 + (np.log(nf / np.float32(16)) / np.float32(np.log(64.0)) * np.float32(16)).astype(np.int32)
    return np.where(n < 16, n, np.minimum(large, 31))


def _tables(h, rel_bias):
    kk = np.arange(128)[:, None]
    tb = {}

    def strip(nd, heads, lo_valid, hi_valid):
        cols = np.arange(nd * 128)[None, :]
        dist = cols - kk
        bkt = _t5_bucket(dist)
        outp = np.empty((len(heads), 128, nd * 128), np.float32)
        ok = (dist >= lo_valid) & (dist < hi_valid)
        for i, hh in enumerate(heads):
            v = rel_bias[bkt, hh]
            outp[i] = np.where(ok, v, np.float32(-MASKV))
        return outp

    tb['ea'] = strip(8, list(range(8)), 0, 1 << 30)
    tb['es'] = strip(8, list(range(8, 16)), 0, 1 << 30)
    tb['ewo'] = strip(5, list(range(8, 16)), 0, 512)
    tb['ewc'] = tb['ewo'].copy() if h == 1 else np.full_like(tb['ewo'], -MASKV)
    c = (np.arange(2)[None, :, None] * 128 + np.arange(128)[:, None, None])
    t_all = 2048 + np.arange(2048)[None, None, :]
    cv = (16 * c + 31 <= t_all) & (c <= 254)
    if h == 0:
        cv &= (c >= 128)
    tb['cv'] = cv.astype(np.float32)
    q = np.arange(2048)
    qb = (2048 + q) // 256
    j = np.arange(16)[None, :]
    past = (j < qb[:, None]) & ((h == 1) | (j >= 8))
    own = (j == qb[:, None])
    pl = lambda a_: np.ascontiguousarray(a_.reshape(16, 128, a_.shape[1]).transpose(1, 0, 2))
    tb['ma_past'] = pl(past.astype(np.float32))
    tb['ma_add'] = pl(np.where(past, 0.0, -1e30).astype(np.float32))
    tb['ma_own'] = pl(own.astype(np.float32))
    qs = (2048 + q) // 64
    j = np.arange(64)[None, :]
    first = 0 if h == 1 else 32
    forced_first = (j == first)
    forced_own = (j == qs[:, None])
    pasts = (j < qs[:, None]) & (j > first) & ~forced_own
    add = np.full((2048, 64), -1e30, np.float32)
    add[pasts] = 0.0
    add[np.broadcast_to(forced_own, add.shape)] = 1e30
    add[np.broadcast_to(forced_first, add.shape)] = 2e30
    tb['ms_past'] = pl(pasts.astype(np.float32))
    tb['ms_add'] = pl(add)
    col = np.arange(4096)
    tb['oh16'] = (col[None, :] // 256 == np.arange(16)[:, None]).astype(np.float32)
    tb['oh64'] = (col[None, :] // 64 == np.arange(64)[:, None]).astype(np.float32)
    cc = c[:, :, 0][:, :, None]
    jj = np.arange(64)[None, None, :]
    ov = (16 * cc < 64 * jj + 64) & (16 * cc + 32 > 64 * jj) & (cc <= 254)
    tb['ovT'] = ov.astype(np.float32)
    tb['ident'] = np.eye(128, dtype=np.float32)
    return tb


_PROG = {}


def kernel(x, norm_w, w_in, q_norm_a, k_norm_a, q_norm_b, k_norm_cmp, k_norm_sel, k_norm_win,
           cmp_pos_k, cmp_w1_k, cmp_w2_k, cmp_pos_v, cmp_w1_v, cmp_w2_v, rel_bias,
           w_branch_a, w_branch_b, w_out, _debug=False):
    f = lambda a: np.ascontiguousarray(np.asarray(a, dtype=np.float32))
    x = f(x)
    rel_bias = f(rel_bias)
    common = dict(
        norm_w=np.ascontiguousarray(f(norm_w)[0].reshape(8, 128).T), w_in=f(w_in)[0],
        gains=np.stack([f(q_norm_a)[0], f(k_norm_a)[0], f(q_norm_b)[0], f(k_norm_cmp)[0], f(k_norm_sel)[0], f(k_norm_win)[0]]),
        cmp_pos=np.ascontiguousarray(np.stack([f(cmp_pos_k)[0].T, f(cmp_pos_v)[0].T])),
        cmp_w1=np.stack([f(cmp_w1_k)[0], f(cmp_w1_v)[0]]),
        cmp_w2=np.stack([f(cmp_w2_k)[0], f(cmp_w2_v)[0]]),
        rel_bias=rel_bias, w_ba=f(w_branch_a)[0], w_bb=f(w_branch_b)[0], w_out=f(w_out)[0],
    )
    common['gainsT'] = np.ascontiguousarray(common['gains'].T)
    tabs = [_tables(h, rel_bias) for h in range(2)]
    in_maps = []
    for c in range(8):
        b, h = c // 2, c % 2
        xa = np.zeros((4096, 1024), np.float32)
        if h == 1:
            xa[:] = x[b]
        else:
            xa[2048:] = x[b, :2048]
        m = dict(common)
        m.update(tabs[h])
        m['xa'] = xa
        in_maps.append(m)
    key = bool(_debug)
    if key not in _PROG:
        _PROG[key] = build_program(debug=key)
    nc = _PROG[key]
    r = run_bass_kernel_spmd(nc, in_maps, core_ids=list(range(8)))
    outp = np.empty((4, 4096, 1024), np.float32)
    for c in range(8):
        b, h = c // 2, c % 2
        outp[b, h * 2048:(h + 1) * 2048] = r.results[c]["out"]
    if _debug:
        return outp, r.results
    return outp
```

```python
import numpy as np
import concourse.bass as bass
import concourse.mybir as mybir
from concourse.bass_utils import run_bass_kernel_spmd

F32 = mybir.dt.float32
BF16 = mybir.dt.bfloat16
AF = mybir.ActivationFunctionType
ALU = mybir.AluOpType
AX = mybir.AxisListType

EPS = 1e-6
MASKV = 30000.0
C_QA, C_KA, C_VA, C_ZA, C_QB, C_KC, C_VC, C_KS, C_VS, C_KW, C_VW, C_GB, C_ZB, C_GM = (
    0, 512, 1024, 1536, 2048, 2560, 2688, 2816, 2944, 3072, 3200, 3328, 3352, 3864)


class Sched:
    def __init__(self, nc, n_dma_sems=24):
        self.nc = nc
        self.engs = {'pe': nc.tensor, 'act': nc.scalar, 'dve': nc.vector, 'pool': nc.gpsimd, 'sp': nc.sync}
        self.sem = {k: nc.alloc_semaphore(name=f"s_{k}") for k in self.engs}
        self.cnt = {k: 0 for k in self.engs}
        self.waited = {k: {} for k in self.engs}
        self.dpool = {}
        for q, n in (('sp', 14), ('pool', 10)):
            self.dpool[q] = dict(sems=[nc.alloc_semaphore(name=f"d{q}{i}") for i in range(n)], val=[0] * n, nxt=0)
        self.lastw = {}
        self.readers = {}

    def _wait(self, e, deps):
        best = {}
        for d in deps:
            if d is None:
                continue
            s, v = d
            if v > best.get(id(s), (None, 0))[1]:
                best[id(s)] = (s, v)
        for sid, (s, v) in best.items():
            if self.waited[e].get(sid, 0) >= v:
                continue
            if e == 'pe' and s is self.sem['pe']:
                continue
            self.engs[e].wait_ge(s, v)
            self.waited[e][sid] = v

    def _deps(self, reads, writes):
        deps = []
        for k in reads:
            deps.append(self.lastw.get(k))
        for k in writes:
            deps.append(self.lastw.get(k))
            deps.extend(self.readers.get(k, {}).values())
        return deps

    def _commit(self, reads, writes, tok):
        for k in reads:
            r = self.readers.setdefault(k, {})
            old = r.get(id(tok[0]))
            if old is None or old[1] < tok[1]:
                r[id(tok[0])] = tok
        for k in writes:
            self.lastw[k] = tok
            self.readers[k] = {}

    def op(self, e, fn, reads=(), writes=()):
        self._wait(e, self._deps(reads, writes))
        ins = fn()
        self.cnt[e] += 1
        ins.then_inc(self.sem[e], 1)
        tok = (self.sem[e], self.cnt[e])
        self._commit(reads, writes, tok)
        return tok

    def dma(self, e, out, in_, reads=(), writes=(), **kw):
        dp = self.dpool[e]
        i = dp['nxt']
        dp['nxt'] = (i + 1) % len(dp['sems'])
        s = dp['sems'][i]
        deps = self._deps(reads, writes)
        if dp['val'][i] > 0:
            deps.append((s, dp['val'][i]))
        self._wait(e, deps)
        self.engs[e].dma_start(out=out, in_=in_, **kw).then_inc(s, 16)
        dp['val'][i] += 16
        tok = (s, dp['val'][i])
        self._commit(reads, writes, tok)
        return tok

    def barrier(self):
        toks = [(self.sem[k], self.cnt[k]) for k in self.engs if self.cnt[k] > 0]
        for dp in self.dpool.values():
            toks += [(s, v) for s, v in zip(dp['sems'], dp['val']) if v > 0]
        for e in self.engs:
            self._wait(e, toks)

    def finish(self, e, keys):
        self._wait(e, [self.lastw.get(k) for k in keys])


class Arena:
    def __init__(self, nc, base, limit):
        self.nc = nc
        self.cur = base
        self.limit = limit
        self.n = 0
        self.offs = {}

    def alloc(self, name, shape, dt):
        esz = 2 if dt == BF16 else 4
        nbytes = esz
        for d in shape[1:]:
            nbytes *= d
        self.cur = (self.cur + 63) // 64 * 64
        off = self.cur
        self.cur += nbytes
        assert self.cur <= self.limit, f"SBUF overflow {name} {self.cur}"
        self.n += 1
        self.offs[name] = off
        return self.nc.alloc_sbuf_tensor_at(f"{name}_{self.n}", list(shape), dt, offset=off)

    def mark(self):
        return self.cur

    def reset(self, m):
        self.cur = m


def build_program(debug=False):
    nc = bass.Bass("TRN2", target_bir_lowering=False)

    def din(name, shape):
        return nc.dram_tensor(name, list(shape), F32, kind="ExternalInput")

    xa = din("xa", [4096, 1024])
    norm_w = din("norm_w", [128, 8])
    w_in = din("w_in", [1024, 5912])
    gains = din("gains", [6, 64])
    gainsT = din("gainsT", [64, 6])
    cmp_pos = din("cmp_pos", [2, 64, 32])
    cmp_w1 = din("cmp_w1", [2, 2048, 256])
    cmp_w2 = din("cmp_w2", [2, 256, 64])
    rel_bias = din("rel_bias", [32, 16])
    w_ba = din("w_ba", [512, 1024])
    w_bb = din("w_bb", [512, 1024])
    w_out = din("w_out", [1024, 1024])
    ea = din("ea", [8, 128, 1024])
    es = din("es", [8, 128, 1024])
    ewo = din("ewo", [8, 128, 640])
    ewc = din("ewc", [8, 128, 640])
    cvd = din("cv", [128, 2, 2048])
    ma_past = din("ma_past", [128, 16, 16])
    ma_add = din("ma_add", [128, 16, 16])
    ma_own = din("ma_own", [128, 16, 16])
    ms_past = din("ms_past", [128, 16, 64])
    ms_add = din("ms_add", [128, 16, 64])
    oh16 = din("oh16", [16, 4096])
    oh64 = din("oh64", [64, 4096])
    ovT = din("ovT", [128, 2, 64])
    identd = din("ident", [128, 128])
    out = nc.dram_tensor("out", [2048, 1024], F32, kind="ExternalOutput")
    dbg = {}
    if debug:
        dbg['oa'] = nc.dram_tensor("dbg_oa", [2048, 512], F32, kind="ExternalOutput")
        dbg['ob'] = nc.dram_tensor("dbg_ob", [2048, 512], F32, kind="ExternalOutput")

    S = Sched(nc)
    AR = Arena(nc, 16640, 229376 - 256)
    op, dma = S.op, S.dma
    V, A, P, T = nc.vector, nc.scalar, nc.gpsimd, nc.tensor

    PS = [nc.alloc_psum_tensor(f"ps{i}", [128, 512], F32) for i in range(7)]
    PSB = nc.alloc_psum_tensor("psb", [128, 1024], BF16)
    LGB = [0, 1, 4, 5]
    NLG = 4
    NPT = 6
    DEPTH = 4
    B_OT, B_TOK, B_PJ0, B_PJ1, B_MS = 2, 3, 4, 5, 6
    PJ3 = [4, 5, 3]

    def pk(i):
        return f"ps{i}"

    hT = AR.alloc("hT", [128, 8, 4096], BF16)
    ident = AR.alloc("ident", [128, 128], F32)
    identb = AR.alloc("identb", [128, 128], BF16)
    bdiag = AR.alloc("bdiag", [128, 128], BF16)
    normw = AR.alloc("normw", [128, 8], F32)
    gn = AR.alloc("gn", [128, 6], F32)
    gnq = AR.alloc("gnq", [128, 6], F32)
    nb31 = AR.alloc("nb31", [128, 16], F32)
    o_b = AR.alloc("o_b", [128, 16, 512], F32)
    PT = [AR.alloc(f"PT{i}", [128, 512], BF16) for i in range(6)]
    OS = [AR.alloc(f"OS{i}", [128, 512], F32) for i in range(2)]
    small = AR.alloc("small", [128, 64], F32)
    sqb = [AR.alloc(f"sqb{i}", [128, 512], BF16) for i in range(2)]
    rstd = [AR.alloc(f"rstd{i}", [128, 512], F32) for i in range(2)]
    epst = AR.alloc("epst", [128, 1], F32)
    mark_persist = AR.mark()
    op('pool', lambda: P.memset(epst[:], EPS), writes=['epst'])

    dma('sp', ident[:], identd.ap(), writes=['ident'])
    dma('pool', identb[:], identd.ap(), writes=['identb'])
    op('pool', lambda: P.memset(bdiag[:], 0.0), writes=['bdiag'])
    op('pool', lambda: P.memset(bdiag[0:64, 0:64], 1.0 / 64), writes=['bdiag'])
    op('pool', lambda: P.memset(bdiag[64:128, 64:128], 1.0 / 64), writes=['bdiag'])
    dma('sp', normw[:], norm_w.ap(), writes=['normw'])
    dma('sp', gn[0:64, :], gainsT.ap(), writes=['gn'])
    dma('sp', gn[64:128, :], gainsT.ap(), writes=['gn'])
    op('dve', lambda: V.tensor_scalar(out=gnq[:], in0=gn[:], scalar1=0.125, scalar2=None, op0=ALU.mult),
       reads=['gn'], writes=['gnq'])
    dma('sp', nb31[:], rel_bias.ap()[31:32, :].partition_broadcast(128), writes=['nb31'])
    op('dve', lambda: V.tensor_scalar(out=nb31[:], in0=nb31[:], scalar1=-1.0, scalar2=None, op0=ALU.mult),
       writes=['nb31'])

    def load_w(dst, c0, ncols, key, dcol=0):
        src = w_in.ap().rearrange("(c p) n -> p c n", p=128)[:, :, c0:c0 + ncols]
        dma('pool', dst[:, :, dcol:dcol + ncols], src, writes=[key])

    pj_rot = [0]

    def proj_fm(wt, wkey, wc0, g8, banks=None):
        banks = banks or [B_PJ0, B_PJ1]
        bk = banks[pj_rot[0] % len(banks)]
        pj_rot[0] += 1
        for c in range(8):
            op('pe', lambda: T.matmul(PS[bk][:, :], wt[:, c, wc0:wc0 + 128], hT[:, c, g8 * 512:(g8 + 1) * 512],
                                      start=(c == 0), stop=(c == 7)),
               reads=[wkey, f'hT{g8}'], writes=[pk(bk)])
        return bk

    mark_h = AR.mark()

    hn_rot = [0]

    def headnorm(bk, gcol_ap, dsts):
        r2 = hn_rot[0] % 2
        hn_rot[0] += 1
        mb = [B_MS, LGB[0]][r2]
        sq_, rs_ = sqb[r2], rstd[r2]
        op('act', lambda: A.activation(out=sq_[:], in_=PS[bk][:, :], func=AF.Square), writes=[pk(bk), f'sqb{r2}'])
        op('pe', lambda: T.matmul(PS[mb][:, :], bdiag[:], sq_[:], start=True, stop=True),
           reads=['bdiag', f'sqb{r2}'], writes=[pk(mb)])
        op('act', lambda: A.activation(out=rs_[:], in_=PS[mb][:, :], func=AF.Ln, bias=epst[:, 0:1], scale=1.0),
           reads=['epst'], writes=[pk(mb), f'rstd{r2}'])
        op('act', lambda: A.activation(out=rs_[:], in_=rs_[:], func=AF.Exp, scale=-0.5), writes=[f'rstd{r2}'])
        for hf, (dst, key) in enumerate(dsts):
            r = slice(hf * 64, hf * 64 + 64)
            op('dve', lambda: V.scalar_tensor_tensor(out=dst, in0=PS[bk][r, :], scalar=gcol_ap[r, :], in1=rs_[r, :],
                                                     op0=ALU.mult, op1=ALU.mult),
               reads=[f'rstd{r2}', 'gn', 'gnq', pk(bk)], writes=[key])

    KC = AR.alloc("KC", [64, 2, 256], BF16)
    VC = AR.alloc("VC", [128, 2, 2, 129], BF16)
    mark_wB = AR.mark()
    wB = AR.alloc("wB", [128, 8, 768], BF16)
    kcT = AR.alloc("kcT", [128, 16, 256], BF16)
    vcT = AR.alloc("vcT", [128, 16, 256], BF16)
    mA = AR.mark()
    xt = [AR.alloc(f"xt{i}", [128, 1024], F32) for i in range(3)]
    xsq = AR.alloc("xsq", [128, 1024], BF16)
    xn = [AR.alloc(f"xn{i}", [128, 1024], BF16) for i in range(2)]
    ss = AR.alloc("ss", [128, 32], F32)
    load_w(wB, C_KC, 768, 'wB')

    def stageA1(t):
        b3 = t % 3
        dma('sp', xt[b3][:], xa.ap()[t * 128:(t + 1) * 128, :], writes=[f'xt{b3}'])
        op('act', lambda: A.activation(out=xsq[:], in_=xt[b3][:], func=AF.Square, accum_out=ss[:, t:t + 1]),
           reads=[f'xt{b3}'], writes=['xsq', f'ss{t}'])

    def stageA1b(t):
        op('dve', lambda: V.tensor_scalar(out=ss[:, t:t + 1], in0=ss[:, t:t + 1], scalar1=1.0 / 1024, scalar2=EPS,
                                          op0=ALU.mult, op1=ALU.add), writes=[f'ss{t}'])
        op('act', lambda: A.activation(out=ss[:, t:t + 1], in_=ss[:, t:t + 1], func=AF.Sqrt), writes=[f'ss{t}'])
        op('dve', lambda: V.reciprocal(out=ss[:, t:t + 1], in_=ss[:, t:t + 1]), writes=[f'ss{t}'])

    def stageA2a(t):
        b = t % 2
        b3 = t % 3
        op('act', lambda: A.activation(out=xn[b][:], in_=xt[b3][:], func=AF.Copy, scale=ss[:, t:t + 1]),
           reads=[f'xt{b3}', f'ss{t}'], writes=[f'xn{b}'])
        for c in range(8):
            op('pe', lambda: T.transpose(PSB[:, c * 128:(c + 1) * 128], xn[b][:, c * 128:(c + 1) * 128], identb[:]),
               reads=[f'xn{b}', 'identb'], writes=['psb'])

    def stageA2b(t):
        op('dve', lambda: V.tensor_tensor(out=hT[:, :, t * 128:(t + 1) * 128],
                                          in0=PSB[:, :].rearrange("p (c n) -> p c n", c=8),
                                          in1=normw[:, :].unsqueeze(2).to_broadcast([128, 8, 128]), op=ALU.mult),
           reads=['normw'], writes=['psb', f'hT{t // 4}'])

    defer = []

    def stageB1(g8):
        st = {}

        def p_kc():
            st['kc'] = proj_fm(wB, 'wB', 0, g8)

        def p_vc():
            st['vc'] = proj_fm(wB, 'wB', 128, g8)

        def e_kc():
            bk = st['kc']
            op('dve', lambda: V.tensor_copy(out=kcT[:, :, g8 * 32:(g8 + 1) * 32], in_=PS[bk][:, :].rearrange("p (n r) -> p r n", r=16)),
               writes=[pk(bk), 'kcT'])

        def e_vc():
            bk = st['vc']
            op('dve', lambda: V.tensor_copy(out=vcT[:, :, g8 * 32:(g8 + 1) * 32], in_=PS[bk][:, :].rearrange("p (n r) -> p r n", r=16)),
               writes=[pk(bk), 'vcT'])
        defer.extend([p_kc, p_vc, e_kc, e_vc])

    stageA1(0)
    stageA1b(0)
    for t in range(32):
        if t + 1 < 32:
            stageA1(t + 1)
        stageA2a(t)
        if t + 1 < 32:
            stageA1b(t + 1)
        stageA2b(t)
        if defer:
            defer.pop(0)()
        if t % 4 == 3:
            stageB1(t // 4)
    while defer:
        defer.pop(0)()
    S.barrier()
    AR.reset(mA)

    w1 = AR.alloc("w1", [128, 32, 256], BF16)
    w2 = AR.alloc("w2", [128, 2, 64], BF16)
    posT = AR.alloc("posT", [64, 32], BF16)
    pb = AR.alloc("pb", [128, 2], F32)
    GH = AR.alloc("GH", [128, 2, 256], BF16)
    u_t = AR.alloc("u_t", [128, 256], F32)
    t_t = AR.alloc("t_t", [128, 256], F32)
    gbc = AR.alloc("gbc", [128, 64], F32)
    kc32 = AR.alloc("kc32", [128, 64], F32)
    kcb = AR.alloc("kcb", [128, 64], BF16)
    dma('sp', gbc[:], gains.ap()[3:4, :].partition_broadcast(128), writes=['gbc'])
    dma('pool', VC[:, :, 0, 65:129], ovT.ap(), writes=['VCo'])
    dma('pool', VC[:, :, 1, 65:129], ovT.ap(), writes=['VCo'])
    op('pool', lambda: P.memset(VC[:, :, :, 64:65], 1.0), writes=['VC1'])
    op('dve', lambda: V.memset(GH[:], 0.0), writes=['GH'])
    for kind in range(2):
        srcT = kcT if kind == 0 else vcT
        w1src = cmp_w1.ap()[kind].rearrange("(l d) m -> d l m", d=64)
        dma('pool', w1[0:64, :, :], w1src, writes=['w1'])
        dma('pool', w1[64:128, :, :], w1src, writes=['w1'])
        dma('pool', w2[:, :, :], cmp_w2.ap()[kind].rearrange("(c p) d -> p c d", p=128), writes=['w2'])
        dma('pool', posT[:, :], cmp_pos.ap()[kind], writes=['posT'])
        for mc in range(2):
            for l in range(32):
                op('pe', lambda: T.matmul(PS[B_MS][:, 0:1], w1[0:64, l, mc * 128:(mc + 1) * 128], posT[:, l:l + 1],
                                          start=(l == 0), stop=(l == 31)),
                   reads=['w1', 'posT'], writes=[pk(B_MS)])
            op('dve', lambda: V.tensor_copy(out=pb[:, mc:mc + 1], in_=PS[B_MS][:, 0:1]), writes=[pk(B_MS), 'pb'])
        for g in range(2):
            r = slice(g * 64, g * 64 + 64)
            for mc in range(2):
                bk = [B_PJ0, B_PJ1][mc]
                for l in range(32):
                    rhs = srcT[r, l % 16, (l // 16):(l // 16) + 255]
                    op('pe', lambda: T.matmul(PS[bk][:, 0:255], w1[r, l, mc * 128:(mc + 1) * 128], rhs,
                                              start=(l == 0), stop=(l == 31)),
                       reads=['w1', 'kcT', 'vcT'], writes=[pk(bk)])
                op('act', lambda: A.activation(out=u_t[:, 0:255], in_=PS[bk][:, 0:255], func=AF.Identity,
                                               bias=pb[:, mc:mc + 1], scale=1.0),
                   reads=['pb'], writes=[pk(bk), 'u_t'])
                op('dve', lambda: V.tensor_tensor(out=t_t[:, 0:255], in0=u_t[:, 0:255], in1=u_t[:, 0:255], op=ALU.mult),
                   reads=['u_t'], writes=['t_t'])
                op('dve', lambda: V.tensor_scalar(out=t_t[:, 0:255], in0=t_t[:, 0:255], scalar1=0.044715, scalar2=1.0,
                                                  op0=ALU.mult, op1=ALU.add), writes=['t_t'])
                op('dve', lambda: V.tensor_tensor(out=t_t[:, 0:255], in0=t_t[:, 0:255], in1=u_t[:, 0:255], op=ALU.mult),
                   reads=['u_t'], writes=['t_t'])
                op('act', lambda: A.activation(out=t_t[:, 0:255], in_=t_t[:, 0:255], func=AF.Sigmoid,
                                               scale=1.5957691216057308), writes=['t_t'])
                op('dve', lambda: V.tensor_tensor(out=GH[:, mc, 0:255], in0=t_t[:, 0:255], in1=u_t[:, 0:255], op=ALU.mult),
                   reads=['u_t', 't_t'], writes=['GH'])
            for ct in range(2):
                for mc in range(2):
                    op('pe', lambda: T.matmul(PS[B_MS][:, 0:64], GH[:, mc, ct * 128:(ct + 1) * 128], w2[:, mc, :],
                                              start=(mc == 0), stop=(mc == 1)),
                       reads=['GH', 'w2'], writes=[pk(B_MS)])
                if kind == 1:
                    op('act', lambda: A.copy(out=VC[:, ct, g, 0:64], in_=PS[B_MS][:, 0:64]), writes=[pk(B_MS), 'VCv'])
                else:
                    op('act', lambda: A.activation(out=kc32[:], in_=PS[B_MS][:, 0:64], func=AF.Square),
                       writes=[pk(B_MS), 'kc32'])
                    op('dve', lambda: V.reduce_sum(out=small[:, 0:1], in_=kc32[:], axis=AX.X), reads=['kc32'], writes=['small'])
                    op('dve', lambda: V.tensor_scalar(out=small[:, 0:1], in0=small[:, 0:1], scalar1=1.0 / 64, scalar2=EPS,
                                                      op0=ALU.mult, op1=ALU.add), writes=['small'])
                    op('act', lambda: A.activation(out=small[:, 0:1], in_=small[:, 0:1], func=AF.Sqrt), writes=['small'])
                    op('dve', lambda: V.reciprocal(out=small[:, 0:1], in_=small[:, 0:1]), writes=['small'])
                    op('dve', lambda: V.scalar_tensor_tensor(out=kcb[:], in0=PS[B_MS][:, 0:64], scalar=small[:, 0:1],
                                                             in1=gbc[:], op0=ALU.mult, op1=ALU.mult),
                       reads=['gbc'], writes=[pk(B_MS), 'small', 'kcb'])
                    op('pe', lambda: T.transpose(PSB[0:64, 0:128], kcb[:, :], identb[:]), reads=['kcb', 'identb'], writes=['psb'])
                    op('act', lambda: A.copy(out=KC[0:64, g, ct * 128:(ct + 1) * 128], in_=PSB[0:64, 0:128]),
                       writes=['psb', 'KC'])
    S.barrier()
    AR.reset(mark_wB)

    KS = [AR.alloc(f"KS{g}", [128, 4096], BF16) for g in range(2)]
    KW = [AR.alloc(f"KW{g}", [64, 4096], BF16) for g in range(2)]
    VS = AR.alloc("VS", [128, 32, 2, 65], BF16)
    VW = AR.alloc("VW", [128, 32, 2, 65], BF16)
    mark_nsa = AR.mark()
    wB = AR.alloc("wB2", [128, 8, 768], BF16)
    load_w(wB, C_KC, 768, 'wB')
    for g in range(2):
        dma('pool', KS[g][64:128, :], oh64.ap(), writes=[f'KS{g}m'])
    op('pool', lambda: P.memset(VS[:, :, :, 64:65], 1.0), writes=['VS1'])
    op('pool', lambda: P.memset(VW[:, :, :, 64:65], 1.0), writes=['VW1'])
    prev = None
    for g8 in range(8):
        cs = slice(g8 * 512, (g8 + 1) * 512)
        for wc0_, gc_, dst_ in ((256, gn[:, 4:5], [(KS[0][0:64, cs], 'KS0d'), (KS[1][0:64, cs], 'KS1d')]),
                                (512, gn[:, 5:6], [(KW[0][0:64, cs], 'KW0'), (KW[1][0:64, cs], 'KW1')])):
            bk = proj_fm(wB, 'wB', wc0_, g8, PJ3)
            if prev is not None:
                headnorm(*prev)
            prev = (bk, gc_, dst_)
    headnorm(*prev)
    for t in range(32):
        bk = [B_PJ0, B_PJ1][t % 2]
        for j, wc in enumerate((384, 640)):
            for c in range(8):
                op('pe', lambda: T.matmul(PS[bk][:, j * 128:(j + 1) * 128], hT[:, c, t * 128:(t + 1) * 128],
                                          wB[:, c, wc:wc + 128], start=(c == 0), stop=(c == 7)),
                   reads=['wB', f'hT{t // 4}'], writes=[pk(bk)])
        op('act', lambda: A.copy(out=VS[:, t, :, 0:64], in_=PS[bk][:, 0:128].rearrange("p (g d) -> p g d", g=2)),
           writes=[pk(bk), 'VS'])
        op('dve', lambda: V.tensor_copy(out=VW[:, t, :, 0:64], in_=PS[bk][:, 128:256].rearrange("p (g d) -> p g d", g=2)),
           writes=[pk(bk), 'VW'])
    S.barrier()
    AR.reset(mark_nsa)

    lg_rot = [0]
    pt_rot = [0]
    os_rot = [0]
    pend_epi = []
    bg = []

    def attention_group(G, qt, K, tl, epilogue, qkey='QT'):
        n = len(tl)
        pend = []
        for i in range(n + DEPTH):
            if i == min(DEPTH, n) and pend_epi:
                pend_epi.pop(0)()
            if i < n:
                kl, vl, s_lo, s_hi, Eap, e_lo, e_n, rkeys = tl[i]
                a = (s_lo - 4 * G) * 128
                b_ = (s_hi - 4 * G) * 128
                lb = LGB[lg_rot[0] % NLG]
                lg_rot[0] += 1
                pb_ = pt_rot[0] % NPT
                pt_rot[0] += 1
                op('pe', lambda: T.matmul(PS[lb][:, a:b_], kl, qt[0:K, s_lo * 128:s_hi * 128], start=True, stop=True),
                   reads=list(rkeys) + [qkey], writes=[pk(lb)])
                op('act', lambda: A.activation(out=PT[pb_][:, a:b_], in_=PS[lb][:, a:b_], func=AF.Exp),
                   writes=[pk(lb), f'PT{pb_}'])
                if Eap is not None and e_n > 0:
                    op('dve', lambda: V.tensor_tensor(out=PT[pb_][:, a:a + e_n * 128], in0=PT[pb_][:, a:a + e_n * 128],
                                                      in1=Eap[:, e_lo * 128:(e_lo + e_n) * 128], op=ALU.mult),
                       reads=['E', 'Ec', 'E0', 'E1'], writes=[f'PT{pb_}'])
                pend.append((vl, a, b_, pb_, rkeys, i))
                if bg:
                    bg.pop(0)()
            if i >= DEPTH:
                vl, a, b_, pb_, rkeys, ii = pend.pop(0)
                op('pe', lambda: T.matmul(PS[B_OT][0:65, a:b_], vl, PT[pb_][:, a:b_], start=(ii == 0), stop=(ii == n - 1)),
                   reads=list(rkeys) + [f'PT{pb_}'], writes=[pk(B_OT)])
        ob = os_rot[0] % 2
        os_rot[0] += 1
        op('dve', lambda: V.tensor_copy(out=OS[ob][0:65, :], in_=PS[B_OT][0:65, :]), writes=[pk(B_OT), f'OS{ob}'])

        def fin():
            for j in range(4):
                op('pe', lambda: T.transpose(PS[B_TOK][:, j * 65:(j + 1) * 65], OS[ob][0:65, j * 128:(j + 1) * 128], ident[0:65, 0:65]),
                   reads=[f'OS{ob}', 'ident'], writes=[pk(B_TOK)])
            epilogue(G)
        pend_epi.append(fin)

    def flush_epi():
        while pend_epi:
            pend_epi.pop(0)()

    def attention(qt, K, tiles, epilogue, qkey='QT'):
        for G in range(4):
            attention_group(G, qt, K, tiles[G], epilogue, qkey)

    def flush_bg():
        while bg:
            bg.pop(0)()

    def tok_view():
        return PS[B_TOK][:, 0:260].rearrange("p (j d) -> p j d", j=4)

    GB = AR.alloc("GB", [128, 16, 24], F32)
    wg = AR.alloc("wg", [128, 8, 24], BF16)
    imp = AR.alloc("imp", [128, 16, 2, 64], F32)
    SM = [AR.alloc(f"SM{g}", [128, 2048], BF16) for g in range(2)]
    QB = [AR.alloc(f"QB{i}", [128, 2048], BF16) for i in range(2)]
    wq = [AR.alloc(f"wq{i}", [128, 8, 128], BF16) for i in range(2)]
    rs = AR.alloc("rs", [128, 4], F32)
    fct = AR.alloc("fct", [128, 4], F32)
    mark_att = AR.mark()
    cvt = AR.alloc("cvt", [128, 2, 2048], BF16)
    itmp = AR.alloc("itmp", [128, 2, 2, 64], F32)

    load_w(wg, C_GB, 24, 'wg')
    dma('pool', cvt[:], cvd.ap(), writes=['cvt'])
    for s in range(16):
        tcol = 2048 + s * 128
        for c in range(8):
            op('pe', lambda: T.matmul(PS[B_MS][:, s * 24:(s + 1) * 24], hT[:, c, tcol:tcol + 128], wg[:, c, :],
                                      start=(c == 0), stop=(c == 7)),
               reads=['wg', f'hT{4 + s // 4}'], writes=[pk(B_MS)])
    op('act', lambda: A.activation(out=GB[:, :, :], in_=PS[B_MS][:, 0:384].rearrange("p (s k) -> p s k", s=16), func=AF.Sigmoid),
       writes=[pk(B_MS), 'GB'])

    wq_n = [0]

    def prefetch_wq(i):
        w_ = wq_n[0] % 2
        load_w(wq[w_], C_QB + (i % 4) * 128, 128, f'wq{w_}')

    def project_qb(i):
        w_ = wq_n[0] % 2
        wq_n[0] += 1
        prefetch_wq(i + 1)
        prev = None
        for G in range(4):
            bk = proj_fm(wq[w_], f'wq{w_}', 0, 4 + G, PJ3)
            cs = slice(G * 512, (G + 1) * 512)
            if prev is not None:
                headnorm(*prev)
            prev = (bk, gnq[:, 2:3], [(QB[0][0:64, cs], 'QT0'), (QB[1][0:64, cs], 'QT1')])
        headnorm(*prev)

    prefetch_wq(0)

    def cmpA(i, hd, G):
        g = i // 2
        h = 2 * i + hd
        cs = slice(G * 512, (G + 1) * 512)
        pts = []
        for ct in range(2):
            lb = [0, 1][ct]
            pb_ = pt_rot[0] % NPT
            pt_rot[0] += 1
            op('pe', lambda: T.matmul(PS[lb][:, :], KC[0:64, g, ct * 128:(ct + 1) * 128], QB[hd][0:64, cs],
                                      start=True, stop=True), reads=['KC', f'QT{hd}'], writes=[pk(lb)])
            op('act', lambda: A.activation(out=PT[pb_][:, :], in_=PS[lb][:, :], func=AF.Exp),
               writes=[pk(lb), f'PT{pb_}'])
            op('dve', lambda: V.tensor_tensor(out=PT[pb_][:, :], in0=PT[pb_][:, :], in1=cvt[:, ct, cs], op=ALU.mult),
               reads=['cvt'], writes=[f'PT{pb_}'])
            pts.append(pb_)
        return pts

    def cmpB(i, hd, G, pts, tokb):
        g = i // 2
        h = 2 * i + hd
        for j2 in range(2):
            bk = tokb[j2]
            for jj in range(2):
                j = j2 * 2 + jj
                for ct in range(2):
                    op('pe', lambda: T.matmul(PS[bk][:, jj * 129:(jj + 1) * 129], PT[pts[ct]][:, j * 128:(j + 1) * 128],
                                              VC[:, ct, g, :], start=(ct == 0), stop=(ct == 1)),
                       reads=[f'PT{pts[ct]}', 'VCo', 'VC1', 'VCv'], writes=[pk(bk)])
        bks = tokb
        s0s = [4 * G, 4 * G + 2]
        pvs = [PS[b_][:, 0:258].rearrange("p (j c) -> p j c", j=2) for b_ in bks]
        for j2 in range(2):
            op('dve', lambda: V.tensor_scalar(out=rs[:, 2 * j2:2 * j2 + 2], in0=PS[bks[j2]][:, 64:258:129], scalar1=1e-30,
                                              scalar2=None, op0=ALU.add), writes=[pk(bks[j2]), f'rs{j2}'])
        for j2 in range(2):
            op('dve', lambda: V.reciprocal(out=rs[:, 2 * j2:2 * j2 + 2], in_=rs[:, 2 * j2:2 * j2 + 2]), writes=[f'rs{j2}'])
        for j2 in range(2):
            s0 = s0s[j2]
            op('dve', lambda: V.tensor_tensor(out=fct[:, 2 * j2:2 * j2 + 2], in0=rs[:, 2 * j2:2 * j2 + 2],
                                              in1=GB[:, s0:s0 + 2, 3 * h], op=ALU.mult),
               reads=['GB', f'rs{j2}'], writes=[f'fct{j2}'])
        for j2 in range(2):
            s0 = s0s[j2]
            if h % 4 == 0:
                op('dve', lambda: V.tensor_tensor(out=imp[:, s0:s0 + 2, g, :], in0=pvs[j2][:, :, 65:129],
                                                  in1=rs[:, 2 * j2:2 * j2 + 2].unsqueeze(2).to_broadcast([128, 2, 64]), op=ALU.mult),
                   reads=[f'rs{j2}'], writes=[pk(bks[j2]), f'imp{s0}', f'imp{s0 + 1}'])
            else:
                op('dve', lambda: V.tensor_tensor(out=itmp[:, j2, :, :], in0=pvs[j2][:, :, 65:129],
                                                  in1=rs[:, 2 * j2:2 * j2 + 2].unsqueeze(2).to_broadcast([128, 2, 64]), op=ALU.mult),
                   reads=[f'rs{j2}'], writes=[pk(bks[j2]), f'itmp{j2}'])
        for j2 in range(2):
            s0 = s0s[j2]
            op('dve', lambda: V.tensor_tensor(out=o_b[:, s0:s0 + 2, h * 64:(h + 1) * 64], in0=pvs[j2][:, :, 0:64],
                                              in1=fct[:, 2 * j2:2 * j2 + 2].unsqueeze(2).to_broadcast([128, 2, 64]), op=ALU.mult),
               reads=[f'fct{j2}'], writes=[pk(bks[j2]), f'ob{s0}', f'ob{s0 + 1}'])
        if h % 4 != 0:
            for j2 in range(2):
                s0 = s0s[j2]
                op('pool', lambda: P.tensor_tensor(out=imp[:, s0:s0 + 2, g, :], in0=imp[:, s0:s0 + 2, g, :], in1=itmp[:, j2, :, :],
                                                   op=ALU.add), reads=[f'itmp{j2}'], writes=[f'imp{s0}', f'imp{s0 + 1}'])


    units = [(i, hd, G) for i in range(4) for hd in range(2) for G in range(4)]
    project_qb(0)
    stA = cmpA(*units[0])
    for k, u in enumerate(units):
        nxt = units[k + 1] if k + 1 < len(units) else None
        if nxt is not None:
            if nxt[0] != u[0]:
                project_qb(nxt[0])
            stN = cmpA(*nxt)
        cmpB(*u, stA, [[B_TOK, B_OT], [B_PJ0, B_PJ1]][k % 2])
        if nxt is not None:
            stA = stN

    S.barrier()
    AR.reset(mark_att)
    msp = AR.alloc("msp", [128, 8, 64], F32)
    msa = AR.alloc("msa", [128, 8, 64], F32)
    stg = AR.alloc("stg", [128, 8, 128], BF16)
    wrk = AR.alloc("wrk", [128, 8, 64], F32)
    wrk2 = AR.alloc("wrk2", [128, 8, 64], F32)
    mx8 = AR.alloc("mx8", [128, 8, 16], F32)
    op('dve', lambda: V.memset(stg[:], 0.0), writes=['stg'])
    for g in range(2):
        for half in range(2):
            sl = slice(half * 8, half * 8 + 8)
            dma('sp', msp[:], ms_past.ap()[:, sl, :], writes=['msp'])
            dma('sp', msa[:], ms_add.ap()[:, sl, :], writes=['msa'])
            op('dve', lambda: V.tensor_tensor(out=wrk[:], in0=imp[:, sl, g, :], in1=msp[:, :, :], op=ALU.mult),
               reads=[f'imp{s_}' for s_ in range(16)] + ['msp'], writes=['wrk'])
            op('dve', lambda: V.tensor_tensor(out=wrk[:], in0=wrk[:], in1=msa[:, :, :], op=ALU.add), reads=['msa'], writes=['wrk'])
            mxa = [f'mxa{s8}' for s8 in range(8)]
            mxb = [f'mxb{s8}' for s8 in range(8)]
            w2k = [f'w2_{s8}' for s8 in range(8)]
            for s8 in range(8):
                op('dve', lambda: V.max(out=mx8[:, s8, 0:8], in_=wrk[:, s8, :]), reads=['wrk'], writes=[mxa[s8]])
            for s8 in range(8):
                op('dve', lambda: V.match_replace(out=wrk2[:, s8, :], in_to_replace=mx8[:, s8, 0:8], in_values=wrk[:, s8, :],
                                                  imm_value=-3e30), reads=['wrk', mxa[s8]], writes=[w2k[s8]])
            for s8 in range(8):
                op('dve', lambda: V.max(out=mx8[:, s8, 8:16], in_=wrk2[:, s8, :]), reads=[w2k[s8]], writes=[mxb[s8]])
            op('dve', lambda: V.tensor_scalar(out=mx8[:, :, 15:16], in0=mx8[:, :, 15:16], scalar1=-1e29, scalar2=None, op0=ALU.max),
               writes=mxb)
            op('dve', lambda: V.tensor_tensor(out=wrk2[:], in0=wrk[:], in1=mx8[:, :, 15:16].to_broadcast([128, 8, 64]), op=ALU.is_ge),
               reads=['wrk'] + mxb, writes=w2k)
            op('dve', lambda: V.tensor_scalar(out=stg[:, :, 64:128], in0=wrk2[:], scalar1=-1.0, scalar2=MASKV,
                                              op0=ALU.add, op1=ALU.mult), reads=w2k, writes=['stg'])
            for s8 in range(8):
                op('pe', lambda: T.transpose(PSB[:, s8 * 128:(s8 + 1) * 128], stg[:, s8, :], identb[:]),
                   reads=['stg', 'identb'], writes=['psb'])
            op('act', lambda: A.copy(out=SM[g][64:128, half * 1024:(half + 1) * 1024], in_=PSB[64:128, :]),
               writes=['psb', f'SM{g}'])

    S.barrier()
    AR.reset(mark_att)
    Est = AR.alloc("Est", [128, 1024], F32)
    Et = AR.alloc("Et", [128, 1024], BF16)
    Etc = AR.alloc("Etc", [128, 640], BF16)
    def load_E(src_ap, ncols, dst, hcol, key='E'):
        dma('sp', Est[:, 0:ncols], src_ap, writes=['Est'])
        op('act', lambda: A.activation(out=dst[:, 0:ncols], in_=Est[:, 0:ncols], func=AF.Exp, bias=nb31[:, hcol:hcol + 1], scale=1.0),
           reads=['Est', 'nb31'], writes=[key])

    def nsa_epilogue(h, br, first=False):
        def ep(G):
            tv = tok_view()
            op('dve', lambda: V.tensor_scalar(out=rs[:, 0:4], in0=PS[B_TOK][:, 64:260:65], scalar1=1e-30, scalar2=None, op0=ALU.add),
               writes=[pk(B_TOK), 'rs'])
            op('dve', lambda: V.reciprocal(out=rs[:, 0:4], in_=rs[:, 0:4]), writes=['rs'])
            op('dve', lambda: V.tensor_tensor(out=fct[:, 0:4], in0=rs[:, 0:4], in1=GB[:, 4 * G:4 * G + 4, 3 * h + br], op=ALU.mult),
               reads=['GB', 'rs'], writes=['fct'])
            for j in range(4):
                s = 4 * G + j
                op('dve', lambda: V.scalar_tensor_tensor(out=o_b[:, s, h * 64:(h + 1) * 64], in0=tv[:, j, 0:64],
                                                         scalar=fct[:, j:j + 1], in1=o_b[:, s, h * 64:(h + 1) * 64],
                                                         op0=ALU.mult, op1=ALU.add),
                   reads=['fct', pk(B_TOK)], writes=[f'ob{s}'])
        return ep

    def dense_tiles(Kt, Vt, vsel, kkeys, Eap):
        tiles = []
        for G in range(4):
            tl = []
            for u in range(16):
                s_lo, s_hi = 4 * G, 4 * G + 4
                d_lo = 16 + s_lo - u
                e_n = max(0, min(s_hi, u + 8 - 16) - s_lo) if d_lo <= 7 else 0
                tl.append((Kt[:, u * 128:(u + 1) * 128], vsel(u), s_lo, s_hi, Eap if e_n > 0 else None, d_lo, e_n, kkeys))
            for u in range(4 * G + 4):
                s_lo, s_hi = max(u, 4 * G), 4 * G + 4
                d_lo = s_lo - u
                e_n = max(0, min(s_hi, u + 8) - s_lo) if d_lo <= 7 else 0
                ua = 16 + u
                tl.append((Kt[:, ua * 128:(ua + 1) * 128], vsel(ua), s_lo, s_hi, Eap if e_n > 0 else None, d_lo, e_n, kkeys))
            tiles.append(tl)
        return tiles

    for i in range(4):
        g = i // 2
        project_qb(i)
        for hd in range(2):
            op('dve', lambda: V.tensor_copy(out=QB[hd][64:128, :], in_=SM[g][64:128, :]), reads=[f'SM{g}'], writes=[f'QT{hd}'])
        for hd in range(2):
            h = 2 * i + hd
            load_E(es.ap()[h], 1024, Et, 8 + h)
            tiles = dense_tiles(KS[g], VS, lambda ua: VS[:, ua, g, :], ['KS0d', 'KS1d', 'KS0m', 'KS1m', 'VS', 'VS1'], Et)
            attention(QB[hd], 128, tiles, nsa_epilogue(h, 1), qkey=f'QT{hd}')
            load_E(ewo.ap()[h], 640, Et, 8 + h)
            load_E(ewc.ap()[h], 640, Etc, 8 + h, key='Ec')
            for G in range(4):
                tl = []
                order = []
                if G == 0:
                    order = [('c', 15), ('c', 14), ('c', 13), ('c', 12)] + [('o', u) for u in range(4)]
                else:
                    order = [('o', 4 * G - 1)] + [('o', u) for u in range(4 * G - 4, 4 * G - 1)] + [('o', u) for u in range(4 * G, 4 * G + 4)]
                wt = []
                wkeys = ['KW0', 'KW1', 'VW', 'VW1']
                for kind, u in order:
                    if kind == 'c':
                        s_lo, s_hi = max(0, u - 15), min(4, u - 11)
                        d_lo = 16 + s_lo - u
                        wt.append((KW[g][0:64, u * 128:(u + 1) * 128], VW[:, u, g, :], s_lo, s_hi, Etc, d_lo, s_hi - s_lo, wkeys))
                    else:
                        s_lo, s_hi = max(u, 4 * G), min(4 * G + 4, u + 5)
                        d_lo = s_lo - u
                        ua = 16 + u
                        wt.append((KW[g][0:64, ua * 128:(ua + 1) * 128], VW[:, ua, g, :], s_lo, s_hi, Et, d_lo, s_hi - s_lo, wkeys))
                attention_group(G, QB[hd], 64, wt, nsa_epilogue(h, 2), qkey=f'QT{hd}')
    flush_epi()
    S.barrier()
    AR.reset(mark_h)

    o_a = AR.alloc("o_a", [128, 16, 512], F32)
    mark_g = AR.mark()
    KA = [AR.alloc(f"KA{i}", [128, 4096], BF16) for i in range(2)]
    QA = [AR.alloc(f"QA{i}", [128, 2048], BF16) for i in range(2)]
    VA = AR.alloc("VA", [128, 32, 2, 65], BF16)
    wa2 = [AR.alloc(f"wa{i}", [128, 8, 384], BF16) for i in range(2)]
    Est = AR.alloc("Est2", [128, 1024], F32)
    Et = AR.alloc("Et2", [128, 1024], BF16)
    rs = AR.alloc("rs2", [128, 4], F32)
    stg = AR.alloc("stg2", [128, 16, 128], BF16)
    map_ = AR.alloc("map", [128, 16, 16], F32)
    maa = AR.alloc("maa", [128, 16, 16], F32)
    mao = AR.alloc("mao", [128, 16, 16], F32)
    km32 = AR.alloc("km32", [64, 16], F32)
    kmb = AR.alloc("kmb", [64, 16], BF16)
    wk16 = AR.alloc("wk16", [128, 256], F32)
    wk16b = AR.alloc("wk16b", [128, 256], F32)
    mx8 = AR.alloc("mx8b", [128, 128], F32)
    for j_, c0_ in enumerate((C_QA, C_KA, C_VA)):
        load_w(wa2[0], c0_, 128, 'wa0', j_ * 128)
    dma('sp', map_[:], ma_past.ap(), writes=['map'])
    dma('sp', maa[:], ma_add.ap(), writes=['maa'])
    dma('sp', mao[:], ma_own.ap(), writes=['mao'])
    op('dve', lambda: V.memset(stg[:], 0.0), writes=['stg'])
    op('pool', lambda: P.memset(VA[:, :, :, 64:65], 1.0), writes=['VA1'])
    for hd in range(2):
        op('pool', lambda: P.memset(KA[hd][64:128, :], 0.0), writes=[f'KA{hd}m'])
        dma('pool', KA[hd][64:80, :], oh16.ap(), writes=[f'KA{hd}m'])

    def moba_epilogue(h):
        def ep(G):
            tv = tok_view()
            op('dve', lambda: V.tensor_scalar(out=rs[:, 0:4], in0=PS[B_TOK][:, 64:260:65], scalar1=1e-30, scalar2=None, op0=ALU.add),
               writes=[pk(B_TOK), 'rs'])
            op('dve', lambda: V.reciprocal(out=rs[:, 0:4], in_=rs[:, 0:4]), writes=['rs'])
            for j in range(4):
                s = 4 * G + j
                op('dve', lambda: V.tensor_scalar(out=o_a[:, s, h * 64:(h + 1) * 64], in0=tv[:, j, 0:64],
                                                  scalar1=rs[:, j:j + 1], scalar2=None, op0=ALU.mult),
                   reads=['rs', pk(B_TOK)], writes=[f'oa{s}'])
        return ep

    def load_wa(i):
        w_ = wa2[i % 2]
        load_w(w_, C_QA + i * 128, 128, f'wa{i % 2}', 0)
        load_w(w_, C_KA + i * 128, 128, f'wa{i % 2}', 128)
        load_w(w_, C_VA + i * 128, 128, f'wa{i % 2}', 256)
    Etg = [Et, AR.alloc("Et3", [128, 1024], BF16)]

    def v_tile(wa, wak, t):
        bk = [1, B_OT][t % 2]
        for c in range(8):
            op('pe', lambda: T.matmul(PS[bk][:, 0:128], hT[:, c, t * 128:(t + 1) * 128], wa[:, c, 256:384],
                                      start=(c == 0), stop=(c == 7)),
               reads=[wak, f'hT{t // 4}'], writes=[pk(bk)])
        op('act', lambda: A.copy(out=VA[:, t, :, 0:64], in_=PS[bk][:, 0:128].rearrange("p (g d) -> p g d", g=2)),
           writes=[pk(bk), 'VA'])

    def setup_ops(hd, h):
        ops = []
        qk = f'QT{hd}'
        w3 = lambda t_: t_[:, :].rearrange("p (s j) -> p s j", s=16)
        ops.append(lambda: op('dve', lambda: V.reduce_sum(out=km32[:, :], in_=KA[hd][0:64, :].rearrange("p (j n) -> p j n", j=16), axis=AX.X),
                              reads=[f'KA{hd}d'], writes=['km32']))
        ops.append(lambda: op('dve', lambda: V.tensor_scalar(out=kmb[:, :], in0=km32[:, :], scalar1=1.0 / 256, scalar2=None, op0=ALU.mult),
                              reads=['km32'], writes=['kmb']))
        for s in range(16):
            ops.append(lambda s=s: op('pe', lambda: T.matmul(PS[B_MS][:, s * 16:(s + 1) * 16], QA[hd][0:64, s * 128:(s + 1) * 128], kmb[:, :],
                                                             start=True, stop=True), reads=[qk, 'kmb'], writes=[pk(B_MS)]))
        ops.append(lambda: op('dve', lambda: V.tensor_tensor(out=w3(wk16), in0=w3(PS[B_MS][:, 0:256]), in1=map_[:, :, :], op=ALU.mult),
                              reads=['map'], writes=[pk(B_MS), 'wk16']))
        ops.append(lambda: op('dve', lambda: V.tensor_tensor(out=w3(wk16), in0=w3(wk16), in1=maa[:, :, :], op=ALU.add), reads=['maa'], writes=['wk16']))
        mxk = [f'mx8_{s_}' for s_ in range(16)]
        for s in range(16):
            ops.append(lambda s=s: op('dve', lambda: V.max(out=mx8[:, s * 8:(s + 1) * 8], in_=wk16[:, s * 16:(s + 1) * 16]),
                                      reads=['wk16'], writes=[mxk[s]]))
        thr = mx8[:, :].rearrange("p (s k) -> p s k", s=16)[:, :, 2:3].to_broadcast([128, 16, 16])
        ops.append(lambda: op('dve', lambda: V.tensor_tensor(out=w3(wk16b), in0=w3(wk16), in1=thr, op=ALU.is_ge),
                              reads=['wk16'] + mxk, writes=['wk16b']))
        ops.append(lambda: op('dve', lambda: V.tensor_tensor(out=w3(wk16b), in0=w3(wk16b), in1=map_[:, :, :], op=ALU.mult), reads=['map'], writes=['wk16b']))
        ops.append(lambda: op('dve', lambda: V.tensor_tensor(out=w3(wk16b), in0=w3(wk16b), in1=mao[:, :, :], op=ALU.add), reads=['mao'], writes=['wk16b']))
        ops.append(lambda: op('dve', lambda: V.tensor_scalar(out=stg[:, :, 64:80], in0=w3(wk16b), scalar1=-1.0, scalar2=MASKV,
                                                             op0=ALU.add, op1=ALU.mult), reads=['wk16b'], writes=['stg']))
        for half in range(2):
            for s8 in range(8):
                s = half * 8 + s8
                ops.append(lambda s=s, s8=s8: op('pe', lambda: T.transpose(PSB[:, s8 * 128:(s8 + 1) * 128], stg[:, s, :], identb[:]),
                                                 reads=['stg', 'identb'], writes=['psb']))
            ops.append(lambda half=half: op('act', lambda: A.copy(out=QA[hd][64:128, half * 1024:(half + 1) * 1024], in_=PSB[64:128, :]),
                                            writes=['psb', qk]))
        if hd == 1:
            ops.append(lambda: load_E(ea.ap()[h], 1024, Etg[hd], h, key=f'E{hd}'))
        return ops

    for i in range(4):
        wa = wa2[i % 2]
        wak = f'wa{i % 2}'
        if i + 1 < 4:
            load_wa(i + 1)
        load_E(ea.ap()[2 * i], 1024, Etg[0], 2 * i, key='E0')
        vt = 0
        items = []
        for g8 in range(8):
            cs = slice(g8 * 512, (g8 + 1) * 512)
            items.append((128, g8, gn[:, 1:2], [(KA[0][0:64, cs], 'KA0d'), (KA[1][0:64, cs], 'KA1d')]))
        for G in range(4):
            cs = slice(G * 512, (G + 1) * 512)
            items.append((0, 4 + G, gnq[:, 0:1], [(QA[0][0:64, cs], 'QT0'), (QA[1][0:64, cs], 'QT1')]))
        prev = None
        for wc0, g8, gcol, dsts in items:
            bk = proj_fm(wa, wak, wc0, g8, PJ3)
            if prev is not None:
                headnorm(*prev)
            prev = (bk, gcol, dsts)
            for _ in range(3):
                if vt < 32:
                    v_tile(wa, wak, vt)
                    vt += 1
        headnorm(*prev)
        while vt < 32:
            v_tile(wa, wak, vt)
            vt += 1
        for f_ in setup_ops(0, 2 * i):
            f_()
        bg.extend(setup_ops(1, 2 * i + 1))
        for hd in range(2):
            h = 2 * i + hd
            if hd == 1:
                flush_bg()
            tiles = dense_tiles(KA[hd], VA, lambda ua: VA[:, ua, hd, :], [f'KA{hd}d', f'KA{hd}m', 'VA', 'VA1'], Etg[hd])
            attention(QA[hd], 128, tiles, moba_epilogue(h), qkey=f'QT{hd}')
    flush_epi()
    if debug:
        for s in range(16):
            dma('sp', dbg['oa'].ap()[s * 128:(s + 1) * 128, :], o_a[:, s, :], reads=[f'oa{s}'], writes=['dbgoa'])
            dma('sp', dbg['ob'].ap()[s * 128:(s + 1) * 128, :], o_b[:, s, :], reads=[f'ob{s}'], writes=['dbgob'])
    S.barrier()
    AR.reset(mark_g)

    yT = [AR.alloc(f"yT{b}", [128, 4, 2048], BF16) for b in range(2)]
    wz = [AR.alloc(f"wz{i}", [128, 8, 128], BF16) for i in range(2)]
    sz = [AR.alloc(f"sz{i}", [128, 512], F32) for i in range(2)]
    tA = [AR.alloc(f"tA{i}", [128, 512], F32) for i in range(2)]
    m_h1 = AR.mark()
    AR.reset(AR.offs['PT0'])
    xo = [AR.alloc(f"xo{i}", [128, 512], F32) for i in range(3)]
    res = [AR.alloc(f"res{i}", [128, 512], F32) for i in range(2)]
    assert AR.cur <= AR.offs['small']
    AR.reset(m_h1)
    wo = AR.alloc("wo", [128, 8, 1024], BF16)
    dma('pool', wo[:], w_out.ap().rearrange("(c p) n -> p c n", p=128), writes=['wo'])
    it = 0
    wi = 0
    zcols = [C_ZA + ci * 128 for ci in range(4)] + [C_ZB + ci * 128 for ci in range(4)]
    load_w(wz[0], zcols[0], 128, 'wz0')
    for br in range(2):
        osrc = o_a if br == 0 else o_b
        okey = 'oa' if br == 0 else 'ob'
        for ci in range(4):
            wb = wi % 2
            wi += 1
            if wi < 8:
                load_w(wz[wi % 2], zcols[wi], 128, f'wz{wi % 2}')
            for G in range(4):
                cs = slice(G * 512, (G + 1) * 512)
                b2 = it % 2
                it += 1
                tb = [B_TOK, B_OT][b2]
                bk = proj_fm(wz[wb], f'wz{wb}', 0, 4 + G)
                op('act', lambda: A.activation(out=sz[b2][:], in_=PS[bk][:, :], func=AF.Silu), writes=[pk(bk), f'sz{b2}'])
                for j in range(4):
                    s = 4 * G + j
                    op('pe', lambda: T.transpose(PS[tb][:, j * 128:(j + 1) * 128], osrc[:, s, ci * 128:(ci + 1) * 128], ident[:]),
                       reads=[f'{okey}{s}', 'ident'], writes=[pk(tb)])
                op('dve', lambda: V.tensor_tensor(out=yT[br][:, ci, cs], in0=PS[tb][:, :], in1=sz[b2][:], op=ALU.mult),
                   reads=[f'sz{b2}'], writes=[pk(tb), f'yT{br}'])
    S.barrier()
    m_end = AR.mark()
    AR.reset(AR.offs['o_b'])
    mg = AR.alloc("mg", [128, 8, 2048], BF16)
    AR.reset(AR.offs['o_a'])
    wgm = [AR.alloc(f"wgm{i}", [128, 8, 256], BF16) for i in range(2)]
    wbr = [AR.alloc(f"wbr{i}", [128, 2, 4, 128], BF16) for i in range(2)]
    assert AR.cur <= AR.offs['o_a'] + 32768
    AR.reset(m_end)
    it = 0

    def load_merge_w(m):
        wb = m % 2
        load_w(wgm[wb], C_GM + m * 128, 128, f'wgm{wb}', 0)
        load_w(wgm[wb], C_GM + 1024 + m * 128, 128, f'wgm{wb}', 128)
        dma('pool', wbr[wb][:, 0, :, :], w_ba.ap().rearrange("(c p) n -> p c n", p=128)[:, :, m * 128:(m + 1) * 128], writes=[f'wbr{wb}'])
        dma('pool', wbr[wb][:, 1, :, :], w_bb.ap().rearrange("(c p) n -> p c n", p=128)[:, :, m * 128:(m + 1) * 128], writes=[f'wbr{wb}'])
    load_merge_w(0)
    for m in range(8):
        wb = m % 2
        if m + 1 < 8:
            load_merge_w(m + 1)
        for G in range(4):
            cs = slice(G * 512, (G + 1) * 512)
            for br in range(2):
                b2 = it % 2
                it += 1
                ob = [B_TOK, B_OT][b2]
                bk = proj_fm(wgm[wb], f'wgm{wb}', br * 128, 4 + G)
                op('act', lambda: A.activation(out=sz[b2][:], in_=PS[bk][:, :], func=AF.Sigmoid), writes=[pk(bk), f'sz{b2}'])
                for ci in range(4):
                    op('pe', lambda: T.matmul(PS[ob][:, :], wbr[wb][:, br, ci, :], yT[br][:, ci, cs], start=(ci == 0), stop=(ci == 3)),
                       reads=[f'wbr{wb}', f'yT{br}'], writes=[pk(ob)])
                if br == 0:
                    ta = tA[G % 2]
                    op('dve', lambda: V.tensor_tensor(out=ta[:], in0=PS[ob][:, :], in1=sz[b2][:], op=ALU.mult),
                       reads=[f'sz{b2}'], writes=[pk(ob), f'tA{G % 2}'])
                else:
                    op('dve', lambda: V.tensor_tensor(out=sz[b2][:], in0=PS[ob][:, :], in1=sz[b2][:], op=ALU.mult),
                       writes=[pk(ob), f'sz{b2}'])
                    op('pool', lambda: P.tensor_tensor(out=mg[:, m, cs], in0=tA[G % 2][:], in1=sz[b2][:], op=ALU.add),
                       reads=[f'tA{G % 2}', f'sz{b2}'], writes=['mg'])
    def load_x(k_):
        s_, hf_ = k_ // 2, k_ % 2
        dma('pool', xo[k_ % 3][:], xa.ap()[2048 + s_ * 128:2048 + (s_ + 1) * 128, hf_ * 512:(hf_ + 1) * 512], writes=[f'xo{k_ % 3}'])
    load_x(0)
    load_x(1)
    for k in range(32):
        s, hf = k // 2, k % 2
        if k + 2 < 32:
            load_x(k + 2)
        bk = [B_PJ0, B_PJ1][k % 2]
        xb, rb = k % 3, k % 2
        for m in range(8):
            op('pe', lambda: T.matmul(PS[bk][:, :], mg[:, m, s * 128:(s + 1) * 128], wo[:, m, hf * 512:(hf + 1) * 512],
                                      start=(m == 0), stop=(m == 7)), reads=['mg', 'wo'], writes=[pk(bk)])
        op('dve', lambda: V.tensor_tensor(out=res[rb][:], in0=PS[bk][:, :], in1=xo[xb][:], op=ALU.add),
           reads=[f'xo{xb}'], writes=[pk(bk), f'res{rb}'])
        dma('sp', out.ap()[s * 128:(s + 1) * 128, hf * 512:(hf + 1) * 512], res[rb][:], reads=[f'res{rb}'], writes=[f'outd{k}'])
    S.barrier()
    return nc


def _t5_bucket(dist):
    n = np.maximum(dist, 0)
    nf = np.maximum(n, 16).astype(np.float32)
    large = 16 + (np.log(nf / np.float32(16)) / np.float32(np.log(64.0)) * np.float32(16)).astype(np.int32)
    return np.where(n < 16, n, np.minimum(large, 31))


def _tables(h, rel_bias):
    kk = np.arange(128)[:, None]
    tb = {}

    def strip(nd, heads, lo_valid, hi_valid):
        cols = np.arange(nd * 128)[None, :]
        dist = cols - kk
        bkt = _t5_bucket(dist)
        outp = np.empty((len(heads), 128, nd * 128), np.float32)
        ok = (dist >= lo_valid) & (dist < hi_valid)
        for i, hh in enumerate(heads):
            v = rel_bias[bkt, hh]
            outp[i] = np.where(ok, v, np.float32(-MASKV))
        return outp

    tb['ea'] = strip(8, list(range(8)), 0, 1 << 30)
    tb['es'] = strip(8, list(range(8, 16)), 0, 1 << 30)
    tb['ewo'] = strip(5, list(range(8, 16)), 0, 512)
    tb['ewc'] = tb['ewo'].copy() if h == 1 else np.full_like(tb['ewo'], -MASKV)
    c = (np.arange(2)[None, :, None] * 128 + np.arange(128)[:, None, None])
    t_all = 2048 + np.arange(2048)[None, None, :]
    cv = (16 * c + 31 <= t_all) & (c <= 254)
    if h == 0:
        cv &= (c >= 128)
    tb['cv'] = cv.astype(np.float32)
    q = np.arange(2048)
    qb = (2048 + q) // 256
    j = np.arange(16)[None, :]
    past = (j < qb[:, None]) & ((h == 1) | (j >= 8))
    own = (j == qb[:, None])
    pl = lambda a_: np.ascontiguousarray(a_.reshape(16, 128, a_.shape[1]).transpose(1, 0, 2))
    tb['ma_past'] = pl(past.astype(np.float32))
    tb['ma_add'] = pl(np.where(past, 0.0, -1e30).astype(np.float32))
    tb['ma_own'] = pl(own.astype(np.float32))
    qs = (2048 + q) // 64
    j = np.arange(64)[None, :]
    first = 0 if h == 1 else 32
    forced_first = (j == first)
    forced_own = (j == qs[:, None])
    pasts = (j < qs[:, None]) & (j > first) & ~forced_own
    add = np.full((2048, 64), -1e30, np.float32)
    add[pasts] = 0.0
    add[np.broadcast_to(forced_own, add.shape)] = 1e30
    add[np.broadcast_to(forced_first, add.shape)] = 2e30
    tb['ms_past'] = pl(pasts.astype(np.float32))
    tb['ms_add'] = pl(add)
    col = np.arange(4096)
    tb['oh16'] = (col[None, :] // 256 == np.arange(16)[:, None]).astype(np.float32)
    tb['oh64'] = (col[None, :] // 64 == np.arange(64)[:, None]).astype(np.float32)
    cc = c[:, :, 0][:, :, None]
    jj = np.arange(64)[None, None, :]
    ov = (16 * cc < 64 * jj + 64) & (16 * cc + 32 > 64 * jj) & (cc <= 254)
    tb['ovT'] = ov.astype(np.float32)
    tb['ident'] = np.eye(128, dtype=np.float32)
    return tb


_PROG = {}


def kernel(x, norm_w, w_in, q_norm_a, k_norm_a, q_norm_b, k_norm_cmp, k_norm_sel, k_norm_win,
           cmp_pos_k, cmp_w1_k, cmp_w2_k, cmp_pos_v, cmp_w1_v, cmp_w2_v, rel_bias,
           w_branch_a, w_branch_b, w_out, _debug=False):
    f = lambda a: np.ascontiguousarray(np.asarray(a, dtype=np.float32))
    x = f(x)
    rel_bias = f(rel_bias)
    common = dict(
        norm_w=np.ascontiguousarray(f(norm_w)[0].reshape(8, 128).T), w_in=f(w_in)[0],
        gains=np.stack([f(q_norm_a)[0], f(k_norm_a)[0], f(q_norm_b)[0], f(k_norm_cmp)[0], f(k_norm_sel)[0], f(k_norm_win)[0]]),
        cmp_pos=np.ascontiguousarray(np.stack([f(cmp_pos_k)[0].T, f(cmp_pos_v)[0].T])),
        cmp_w1=np.stack([f(cmp_w1_k)[0], f(cmp_w1_v)[0]]),
        cmp_w2=np.stack([f(cmp_w2_k)[0], f(cmp_w2_v)[0]]),
        rel_bias=rel_bias, w_ba=f(w_branch_a)[0], w_bb=f(w_branch_b)[0], w_out=f(w_out)[0],
    )
    common['gainsT'] = np.ascontiguousarray(common['gains'].T)
    tabs = [_tables(h, rel_bias) for h in range(2)]
    in_maps = []
    for c in range(8):
        b, h = c // 2, c % 2
        xa = np.zeros((4096, 1024), np.float32)
        if h == 1:
            xa[:] = x[b]
        else:
            xa[2048:] = x[b, :2048]
        m = dict(common)
        m.update(tabs[h])
        m['xa'] = xa
        in_maps.append(m)
    key = bool(_debug)
    if key not in _PROG:
        _PROG[key] = build_program(debug=key)
    nc = _PROG[key]
    r = run_bass_kernel_spmd(nc, in_maps, core_ids=list(range(8)))
    outp = np.empty((4, 4096, 1024), np.float32)
    for c in range(8):
        b, h = c // 2, c % 2
        outp[b, h * 2048:(h + 1) * 2048] = r.results[c]["out"]
    if _debug:
        return outp, r.results
    return outp
```

```python
import numpy as np
import concourse.bass as bass
import concourse.mybir as mybir
from concourse.bass_utils import run_bass_kernel_spmd

F32 = mybir.dt.float32
BF16 = mybir.dt.bfloat16
AF = mybir.ActivationFunctionType
ALU = mybir.AluOpType
AX = mybir.AxisListType

EPS = 1e-6
MASKV = 30000.0
C_QA, C_KA, C_VA, C_ZA, C_QB, C_KC, C_VC, C_KS, C_VS, C_KW, C_VW, C_GB, C_ZB, C_GM = (
    0, 512, 1024, 1536, 2048, 2560, 2688, 2816, 2944, 3072, 3200, 3328, 3352, 3864)


class Sched:
    def __init__(self, nc, n_dma_sems=24):
        self.nc = nc
        self.engs = {'pe': nc.tensor, 'act': nc.scalar, 'dve': nc.vector, 'pool': nc.gpsimd, 'sp': nc.sync}
        self.sem = {k: nc.alloc_semaphore(name=f"s_{k}") for k in self.engs}
        self.cnt = {k: 0 for k in self.engs}
        self.waited = {k: {} for k in self.engs}
        self.dpool = {}
        for q, n in (('sp', 14), ('pool', 10)):
            self.dpool[q] = dict(sems=[nc.alloc_semaphore(name=f"d{q}{i}") for i in range(n)], val=[0] * n, nxt=0)
        self.lastw = {}
        self.readers = {}

    def _wait(self, e, deps):
        best = {}
        for d in deps:
            if d is None:
                continue
            s, v = d
            if v > best.get(id(s), (None, 0))[1]:
                best[id(s)] = (s, v)
        for sid, (s, v) in best.items():
            if self.waited[e].get(sid, 0) >= v:
                continue
            if e == 'pe' and s is self.sem['pe']:
                continue
            self.engs[e].wait_ge(s, v)
            self.waited[e][sid] = v

    def _deps(self, reads, writes):
        deps = []
        for k in reads:
            deps.append(self.lastw.get(k))
        for k in writes:
            deps.append(self.lastw.get(k))
            deps.extend(self.readers.get(k, {}).values())
        return deps

    def _commit(self, reads, writes, tok):
        for k in reads:
            r = self.readers.setdefault(k, {})
            old = r.get(id(tok[0]))
            if old is None or old[1] < tok[1]:
                r[id(tok[0])] = tok
        for k in writes:
            self.lastw[k] = tok
            self.readers[k] = {}

    def op(self, e, fn, reads=(), writes=()):
        self._wait(e, self._deps(reads, writes))
        ins = fn()
        self.cnt[e] += 1
        ins.then_inc(self.sem[e], 1)
        tok = (self.sem[e], self.cnt[e])
        self._commit(reads, writes, tok)
        return tok

    def dma(self, e, out, in_, reads=(), writes=(), **kw):
        dp = self.dpool[e]
        i = dp['nxt']
        dp['nxt'] = (i + 1) % len(dp['sems'])
        s = dp['sems'][i]
        deps = self._deps(reads, writes)
        if dp['val'][i] > 0:
            deps.append((s, dp['val'][i]))
        self._wait(e, deps)
        self.engs[e].dma_start(out=out, in_=in_, **kw).then_inc(s, 16)
        dp['val'][i] += 16
        tok = (s, dp['val'][i])
        self._commit(reads, writes, tok)
        return tok

    def barrier(self):
        toks = [(self.sem[k], self.cnt[k]) for k in self.engs if self.cnt[k] > 0]
        for dp in self.dpool.values():
            toks += [(s, v) for s, v in zip(dp['sems'], dp['val']) if v > 0]
        for e in self.engs:
            self._wait(e, toks)

    def finish(self, e, keys):
        self._wait(e, [self.lastw.get(k) for k in keys])


class Arena:
    def __init__(self, nc, base, limit):
        self.nc = nc
        self.cur = base
        self.limit = limit
        self.n = 0
        self.offs = {}

    def alloc(self, name, shape, dt):
        esz = 2 if dt == BF16 else 4
        nbytes = esz
        for d in shape[1:]:
            nbytes *= d
        self.cur = (self.cur + 63) // 64 * 64
        off = self.cur
        self.cur += nbytes
        assert self.cur <= self.limit, f"SBUF overflow {name} {self.cur}"
        self.n += 1
        self.offs[name] = off
        return self.nc.alloc_sbuf_tensor_at(f"{name}_{self.n}", list(shape), dt, offset=off)

    def mark(self):
        return self.cur

    def reset(self, m):
        self.cur = m


def build_program(debug=False):
    nc = bass.Bass("TRN2", target_bir_lowering=False)

    def din(name, shape):
        return nc.dram_tensor(name, list(shape), F32, kind="ExternalInput")

    xa = din("xa", [4096, 1024])
    norm_w = din("norm_w", [128, 8])
    w_in = din("w_in", [1024, 5912])
    gains = din("gains", [6, 64])
    gainsT = din("gainsT", [64, 6])
    cmp_pos = din("cmp_pos", [2, 64, 32])
    cmp_w1 = din("cmp_w1", [2, 2048, 256])
    cmp_w2 = din("cmp_w2", [2, 256, 64])
    rel_bias = din("rel_bias", [32, 16])
    w_ba = din("w_ba", [512, 1024])
    w_bb = din("w_bb", [512, 1024])
    w_out = din("w_out", [1024, 1024])
    ea = din("ea", [8, 128, 1024])
    es = din("es", [8, 128, 1024])
    ewo = din("ewo", [8, 128, 640])
    ewc = din("ewc", [8, 128, 640])
    cvd = din("cv", [128, 2, 2048])
    ma_past = din("ma_past", [128, 16, 16])
    ma_add = din("ma_add", [128, 16, 16])
    ma_own = din("ma_own", [128, 16, 16])
    ms_past = din("ms_past", [128, 16, 64])
    ms_add = din("ms_add", [128, 16, 64])
    oh16 = din("oh16", [16, 4096])
    oh64 = din("oh64", [64, 4096])
    ovT = din("ovT", [128, 2, 64])
    identd = din("ident", [128, 128])
    out = nc.dram_tensor("out", [2048, 1024], F32, kind="ExternalOutput")
    dbg = {}
    if debug:
        dbg['oa'] = nc.dram_tensor("dbg_oa", [2048, 512], F32, kind="ExternalOutput")
        dbg['ob'] = nc.dram_tensor("dbg_ob", [2048, 512], F32, kind="ExternalOutput")

    S = Sched(nc)
    AR = Arena(nc, 16640, 229376 - 256)
    op, dma = S.op, S.dma
    V, A, P, T = nc.vector, nc.scalar, nc.gpsimd, nc.tensor

    PS = [nc.alloc_psum_tensor(f"ps{i}", [128, 512], F32) for i in range(7)]
    PSB = nc.alloc_psum_tensor("psb", [128, 1024], BF16)
    LGB = [0, 1, 4, 5]
    NLG = 4
    NPT = 6
    DEPTH = 4
    B_OT, B_TOK, B_PJ0, B_PJ1, B_MS = 2, 3, 4, 5, 6
    PJ3 = [4, 5, 3]

    def pk(i):
        return f"ps{i}"

    hT = AR.alloc("hT", [128, 8, 4096], BF16)
    ident = AR.alloc("ident", [128, 128], F32)
    identb = AR.alloc("identb", [128, 128], BF16)
    bdiag = AR.alloc("bdiag", [128, 128], BF16)
    normw = AR.alloc("normw", [128, 8], F32)
    gn = AR.alloc("gn", [128, 6], F32)
    gnq = AR.alloc("gnq", [128, 6], F32)
    nb31 = AR.alloc("nb31", [128, 16], F32)
    o_b = AR.alloc("o_b", [128, 16, 512], F32)
    PT = [AR.alloc(f"PT{i}", [128, 512], BF16) for i in range(6)]
    OS = [AR.alloc(f"OS{i}", [128, 512], F32) for i in range(2)]
    small = AR.alloc("small", [128, 64], F32)
    sqb = [AR.alloc(f"sqb{i}", [128, 512], BF16) for i in range(2)]
    rstd = [AR.alloc(f"rstd{i}", [128, 512], F32) for i in range(2)]
    epst = AR.alloc("epst", [128, 1], F32)
    mark_persist = AR.mark()
    op('pool', lambda: P.memset(epst[:], EPS), writes=['epst'])

    dma('sp', ident[:], identd.ap(), writes=['ident'])
    dma('pool', identb[:], identd.ap(), writes=['identb'])
    op('pool', lambda: P.memset(bdiag[:], 0.0), writes=['bdiag'])
    op('pool', lambda: P.memset(bdiag[0:64, 0:64], 1.0 / 64), writes=['bdiag'])
    op('pool', lambda: P.memset(bdiag[64:128, 64:128], 1.0 / 64), writes=['bdiag'])
    dma('sp', normw[:], norm_w.ap(), writes=['normw'])
    dma('sp', gn[0:64, :], gainsT.ap(), writes=['gn'])
    dma('sp', gn[64:128, :], gainsT.ap(), writes=['gn'])
    op('dve', lambda: V.tensor_scalar(out=gnq[:], in0=gn[:], scalar1=0.125, scalar2=None, op0=ALU.mult),
       reads=['gn'], writes=['gnq'])
    dma('sp', nb31[:], rel_bias.ap()[31:32, :].partition_broadcast(128), writes=['nb31'])
    op('dve', lambda: V.tensor_scalar(out=nb31[:], in0=nb31[:], scalar1=-1.0, scalar2=None, op0=ALU.mult),
       writes=['nb31'])

    def load_w(dst, c0, ncols, key, dcol=0):
        src = w_in.ap().rearrange("(c p) n -> p c n", p=128)[:, :, c0:c0 + ncols]
        dma('pool', dst[:, :, dcol:dcol + ncols], src, writes=[key])

    pj_rot = [0]

    def proj_fm(wt, wkey, wc0, g8, banks=None):
        banks = banks or [B_PJ0, B_PJ1]
        bk = banks[pj_rot[0] % len(banks)]
        pj_rot[0] += 1
        for c in range(8):
            op('pe', lambda: T.matmul(PS[bk][:, :], wt[:, c, wc0:wc0 + 128], hT[:, c, g8 * 512:(g8 + 1) * 512],
                                      start=(c == 0), stop=(c == 7)),
               reads=[wkey, f'hT{g8}'], writes=[pk(bk)])
        return bk

    mark_h = AR.mark()

    hn_rot = [0]

    def headnorm(bk, gcol_ap, dsts):
        r2 = hn_rot[0] % 2
        hn_rot[0] += 1
        mb = [B_MS, LGB[0]][r2]
        sq_, rs_ = sqb[r2], rstd[r2]
        op('act', lambda: A.activation(out=sq_[:], in_=PS[bk][:, :], func=AF.Square), writes=[pk(bk), f'sqb{r2}'])
        op('pe', lambda: T.matmul(PS[mb][:, :], bdiag[:], sq_[:], start=True, stop=True),
           reads=['bdiag', f'sqb{r2}'], writes=[pk(mb)])
        op('act', lambda: A.activation(out=rs_[:], in_=PS[mb][:, :], func=AF.Ln, bias=epst[:, 0:1], scale=1.0),
           reads=['epst'], writes=[pk(mb), f'rstd{r2}'])
        op('act', lambda: A.activation(out=rs_[:], in_=rs_[:], func=AF.Exp, scale=-0.5), writes=[f'rstd{r2}'])
        for hf, (dst, key) in enumerate(dsts):
            r = slice(hf * 64, hf * 64 + 64)
            op('dve', lambda: V.scalar_tensor_tensor(out=dst, in0=PS[bk][r, :], scalar=gcol_ap[r, :], in1=rs_[r, :],
                                                     op0=ALU.mult, op1=ALU.mult),
               reads=[f'rstd{r2}', 'gn', 'gnq', pk(bk)], writes=[key])

    KC = AR.alloc("KC", [64, 2, 256], BF16)
    VC = AR.alloc("VC", [128, 2, 2, 129], BF16)
    mark_wB = AR.mark()
    wB = AR.alloc("wB", [128, 8, 768], BF16)
    kcT = AR.alloc("kcT", [128, 16, 256], BF16)
    vcT = AR.alloc("vcT", [128, 16, 256], BF16)
    mA = AR.mark()
    xt = [AR.alloc(f"xt{i}", [128, 1024], F32) for i in range(3)]
    xsq = AR.alloc("xsq", [128, 1024], BF16)
    xn = [AR.alloc(f"xn{i}", [128, 1024], BF16) for i in range(2)]
    ss = AR.alloc("ss", [128, 32], F32)
    load_w(wB, C_KC, 768, 'wB')

    def stageA1(t):
        b3 = t % 3
        dma('sp', xt[b3][:], xa.ap()[t * 128:(t + 1) * 128, :], writes=[f'xt{b3}'])
        op('act', lambda: A.activation(out=xsq[:], in_=xt[b3][:], func=AF.Square, accum_out=ss[:, t:t + 1]),
           reads=[f'xt{b3}'], writes=['xsq', f'ss{t}'])

    def stageA1b(t):
        op('dve', lambda: V.tensor_scalar(out=ss[:, t:t + 1], in0=ss[:, t:t + 1], scalar1=1.0 / 1024, scalar2=EPS,
                                          op0=ALU.mult, op1=ALU.add), writes=[f'ss{t}'])
        op('act', lambda: A.activation(out=ss[:, t:t + 1], in_=ss[:, t:t + 1], func=AF.Sqrt), writes=[f'ss{t}'])
        op('dve', lambda: V.reciprocal(out=ss[:, t:t + 1], in_=ss[:, t:t + 1]), writes=[f'ss{t}'])

    def stageA2a(t):
        b = t % 2
        b3 = t % 3
        op('act', lambda: A.activation(out=xn[b][:], in_=xt[b3][:], func=AF.Copy, scale=ss[:, t:t + 1]),
           reads=[f'xt{b3}', f'ss{t}'], writes=[f'xn{b}'])
        for c in range(8):
            op('pe', lambda: T.transpose(PSB[:, c * 128:(c + 1) * 128], xn[b][:, c * 128:(c + 1) * 128], identb[:]),
               reads=[f'xn{b}', 'identb'], writes=['psb'])

    def stageA2b(t):
        op('dve', lambda: V.tensor_tensor(out=hT[:, :, t * 128:(t + 1) * 128],
                                          in0=PSB[:, :].rearrange("p (c n) -> p c n", c=8),
                                          in1=normw[:, :].unsqueeze(2).to_broadcast([128, 8, 128]), op=ALU.mult),
           reads=['normw'], writes=['psb', f'hT{t // 4}'])

    defer = []

    def stageB1(g8):
        st = {}

        def p_kc():
            st['kc'] = proj_fm(wB, 'wB', 0, g8)

        def p_vc():
            st['vc'] = proj_fm(wB, 'wB', 128, g8)

        def e_kc():
            bk = st['kc']
            op('dve', lambda: V.tensor_copy(out=kcT[:, :, g8 * 32:(g8 + 1) * 32], in_=PS[bk][:, :].rearrange("p (n r) -> p r n", r=16)),
               writes=[pk(bk), 'kcT'])

        def e_vc():
            bk = st['vc']
            op('dve', lambda: V.tensor_copy(out=vcT[:, :, g8 * 32:(g8 + 1) * 32], in_=PS[bk][:, :].rearrange("p (n r) -> p r n", r=16)),
               writes=[pk(bk), 'vcT'])
        defer.extend([p_kc, p_vc, e_kc, e_vc])

    stageA1(0)
    stageA1b(0)
    for t in range(32):
        if t + 1 < 32:
            stageA1(t + 1)
        stageA2a(t)
        if t + 1 < 32:
            stageA1b(t + 1)
        stageA2b(t)
        if defer:
            defer.pop(0)()
        if t % 4 == 3:
            stageB1(t // 4)
    while defer:
        defer.pop(0)()
    S.barrier()
    AR.reset(mA)

    w1 = AR.alloc("w1", [128, 32, 256], BF16)
    w2 = AR.alloc("w2", [128, 2, 64], BF16)
    posT = AR.alloc("posT", [64, 32], BF16)
    pb = AR.alloc("pb", [128, 2], F32)
    GH = AR.alloc("GH", [128, 2, 256], BF16)
    u_t = AR.alloc("u_t", [128, 256], F32)
    t_t = AR.alloc("t_t", [128, 256], F32)
    gbc = AR.alloc("gbc", [128, 64], F32)
    kc32 = AR.alloc("kc32", [128, 64], F32)
    kcb = AR.alloc("kcb", [128, 64], BF16)
    dma('sp', gbc[:], gains.ap()[3:4, :].partition_broadcast(128), writes=['gbc'])
    dma('pool', VC[:, :, 0, 65:129], ovT.ap(), writes=['VCo'])
    dma('pool', VC[:, :, 1, 65:129], ovT.ap(), writes=['VCo'])
    op('pool', lambda: P.memset(VC[:, :, :, 64:65], 1.0), writes=['VC1'])
    op('dve', lambda: V.memset(GH[:], 0.0), writes=['GH'])
    for kind in range(2):
        srcT = kcT if kind == 0 else vcT
        w1src = cmp_w1.ap()[kind].rearrange("(l d) m -> d l m", d=64)
        dma('pool', w1[0:64, :, :], w1src, writes=['w1'])
        dma('pool', w1[64:128, :, :], w1src, writes=['w1'])
        dma('pool', w2[:, :, :], cmp_w2.ap()[kind].rearrange("(c p) d -> p c d", p=128), writes=['w2'])
        dma('pool', posT[:, :], cmp_pos.ap()[kind], writes=['posT'])
        for mc in range(2):
            for l in range(32):
                op('pe', lambda: T.matmul(PS[B_MS][:, 0:1], w1[0:64, l, mc * 128:(mc + 1) * 128], posT[:, l:l + 1],
                                          start=(l == 0), stop=(l == 31)),
                   reads=['w1', 'posT'], writes=[pk(B_MS)])
            op('dve', lambda: V.tensor_copy(out=pb[:, mc:mc + 1], in_=PS[B_MS][:, 0:1]), writes=[pk(B_MS), 'pb'])
        for g in range(2):
            r = slice(g * 64, g * 64 + 64)
            for mc in range(2):
                bk = [B_PJ0, B_PJ1][mc]
                for l in range(32):
                    rhs = srcT[r, l % 16, (l // 16):(l // 16) + 255]
                    op('pe', lambda: T.matmul(PS[bk][:, 0:255], w1[r, l, mc * 128:(mc + 1) * 128], rhs,
                                              start=(l == 0), stop=(l == 31)),
                       reads=['w1', 'kcT', 'vcT'], writes=[pk(bk)])
                op('act', lambda: A.activation(out=u_t[:, 0:255], in_=PS[bk][:, 0:255], func=AF.Identity,
                                               bias=pb[:, mc:mc + 1], scale=1.0),
                   reads=['pb'], writes=[pk(bk), 'u_t'])
                op('dve', lambda: V.tensor_tensor(out=t_t[:, 0:255], in0=u_t[:, 0:255], in1=u_t[:, 0:255], op=ALU.mult),
                   reads=['u_t'], writes=['t_t'])
                op('dve', lambda: V.tensor_scalar(out=t_t[:, 0:255], in0=t_t[:, 0:255], scalar1=0.044715, scalar2=1.0,
                                                  op0=ALU.mult, op1=ALU.add), writes=['t_t'])
                op('dve', lambda: V.tensor_tensor(out=t_t[:, 0:255], in0=t_t[:, 0:255], in1=u_t[:, 0:255], op=ALU.mult),
                   reads=['u_t'], writes=['t_t'])
                op('act', lambda: A.activation(out=t_t[:, 0:255], in_=t_t[:, 0:255], func=AF.Sigmoid,
                                               scale=1.5957691216057308), writes=['t_t'])
                op('dve', lambda: V.tensor_tensor(out=GH[:, mc, 0:255], in0=t_t[:, 0:255], in1=u_t[:, 0:255], op=ALU.mult),
                   reads=['u_t', 't_t'], writes=['GH'])
            for ct in range(2):
                for mc in range(2):
                    op('pe', lambda: T.matmul(PS[B_MS][:, 0:64], GH[:, mc, ct * 128:(ct + 1) * 128], w2[:, mc, :],
                                              start=(mc == 0), stop=(mc == 1)),
                       reads=['GH', 'w2'], writes=[pk(B_MS)])
                if kind == 1:
                    op('act', lambda: A.copy(out=VC[:, ct, g, 0:64], in_=PS[B_MS][:, 0:64]), writes=[pk(B_MS), 'VCv'])
                else:
                    op('act', lambda: A.activation(out=kc32[:], in_=PS[B_MS][:, 0:64], func=AF.Square),
                       writes=[pk(B_MS), 'kc32'])
                    op('dve', lambda: V.reduce_sum(out=small[:, 0:1], in_=kc32[:], axis=AX.X), reads=['kc32'], writes=['small'])
                    op('dve', lambda: V.tensor_scalar(out=small[:, 0:1], in0=small[:, 0:1], scalar1=1.0 / 64, scalar2=EPS,
                                                      op0=ALU.mult, op1=ALU.add), writes=['small'])
                    op('act', lambda: A.activation(out=small[:, 0:1], in_=small[:, 0:1], func=AF.Sqrt), writes=['small'])
                    op('dve', lambda: V.reciprocal(out=small[:, 0:1], in_=small[:, 0:1]), writes=['small'])
                    op('dve', lambda: V.scalar_tensor_tensor(out=kcb[:], in0=PS[B_MS][:, 0:64], scalar=small[:, 0:1],
                                                             in1=gbc[:], op0=ALU.mult, op1=ALU.mult),
                       reads=['gbc'], writes=[pk(B_MS), 'small', 'kcb'])
                    op('pe', lambda: T.transpose(PSB[0:64, 0:128], kcb[:, :], identb[:]), reads=['kcb', 'identb'], writes=['psb'])
                    op('act', lambda: A.copy(out=KC[0:64, g, ct * 128:(ct + 1) * 128], in_=PSB[0:64, 0:128]),
                       writes=['psb', 'KC'])
    S.barrier()
    AR.reset(mark_wB)

    KS = [AR.alloc(f"KS{g}", [128, 4096], BF16) for g in range(2)]
    KW = [AR.alloc(f"KW{g}", [64, 4096], BF16) for g in range(2)]
    VS = AR.alloc("VS", [128, 32, 2, 65], BF16)
    VW = AR.alloc("VW", [128, 32, 2, 65], BF16)
    mark_nsa = AR.mark()
    wB = AR.alloc("wB2", [128, 8, 768], BF16)
    load_w(wB, C_KC, 768, 'wB')
    for g in range(2):
        dma('pool', KS[g][64:128, :], oh64.ap(), writes=[f'KS{g}m'])
    op('pool', lambda: P.memset(VS[:, :, :, 64:65], 1.0), writes=['VS1'])
    op('pool', lambda: P.memset(VW[:, :, :, 64:65], 1.0), writes=['VW1'])
    prev = None
    for g8 in range(8):
        cs = slice(g8 * 512, (g8 + 1) * 512)
        for wc0_, gc_, dst_ in ((256, gn[:, 4:5], [(KS[0][0:64, cs], 'KS0d'), (KS[1][0:64, cs], 'KS1d')]),
                                (512, gn[:, 5:6], [(KW[0][0:64, cs], 'KW0'), (KW[1][0:64, cs], 'KW1')])):
            bk = proj_fm(wB, 'wB', wc0_, g8, PJ3)
            if prev is not None:
                headnorm(*prev)
            prev = (bk, gc_, dst_)
    headnorm(*prev)
    for t in range(32):
        bk = [B_PJ0, B_PJ1][t % 2]
        for j, wc in enumerate((384, 640)):
            for c in range(8):
                op('pe', lambda: T.matmul(PS[bk][:, j * 128:(j + 1) * 128], hT[:, c, t * 128:(t + 1) * 128],
                                          wB[:, c, wc:wc + 128], start=(c == 0), stop=(c == 7)),
                   reads=['wB', f'hT{t // 4}'], writes=[pk(bk)])
        op('act', lambda: A.copy(out=VS[:, t, :, 0:64], in_=PS[bk][:, 0:128].rearrange("p (g d) -> p g d", g=2)),
           writes=[pk(bk), 'VS'])
        op('dve', lambda: V.tensor_copy(out=VW[:, t, :, 0:64], in_=PS[bk][:, 128:256].rearrange("p (g d) -> p g d", g=2)),
           writes=[pk(bk), 'VW'])
    S.barrier()
    AR.reset(mark_nsa)

    lg_rot = [0]
    pt_rot = [0]
    os_rot = [0]
    pend_epi = []
    bg = []

    def attention_group(G, qt, K, tl, epilogue, qkey='QT'):
        n = len(tl)
        pend = []
        for i in range(n + DEPTH):
            if i == min(DEPTH, n) and pend_epi:
                pend_epi.pop(0)()
            if i < n:
                kl, vl, s_lo, s_hi, Eap, e_lo, e_n, rkeys = tl[i]
                a = (s_lo - 4 * G) * 128
                b_ = (s_hi - 4 * G) * 128
                lb = LGB[lg_rot[0] % NLG]
                lg_rot[0] += 1
                pb_ = pt_rot[0] % NPT
                pt_rot[0] += 1
                op('pe', lambda: T.matmul(PS[lb][:, a:b_], kl, qt[0:K, s_lo * 128:s_hi * 128], start=True, stop=True),
                   reads=list(rkeys) + [qkey], writes=[pk(lb)])
                op('act', lambda: A.activation(out=PT[pb_][:, a:b_], in_=PS[lb][:, a:b_], func=AF.Exp),
                   writes=[pk(lb), f'PT{pb_}'])
                if Eap is not None and e_n > 0:
                    op('dve', lambda: V.tensor_tensor(out=PT[pb_][:, a:a + e_n * 128], in0=PT[pb_][:, a:a + e_n * 128],
                                                      in1=Eap[:, e_lo * 128:(e_lo + e_n) * 128], op=ALU.mult),
                       reads=['E', 'Ec', 'E0', 'E1'], writes=[f'PT{pb_}'])
                pend.append((vl, a, b_, pb_, rkeys, i))
                if bg:
                    bg.pop(0)()
            if i >= DEPTH:
                vl, a, b_, pb_, rkeys, ii = pend.pop(0)
                op('pe', lambda: T.matmul(PS[B_OT][0:65, a:b_], vl, PT[pb_][:, a:b_], start=(ii == 0), stop=(ii == n - 1)),
                   reads=list(rkeys) + [f'PT{pb_}'], writes=[pk(B_OT)])
        ob = os_rot[0] % 2
        os_rot[0] += 1
        op('dve', lambda: V.tensor_copy(out=OS[ob][0:65, :], in_=PS[B_OT][0:65, :]), writes=[pk(B_OT), f'OS{ob}'])

        def fin():
            for j in range(4):
                op('pe', lambda: T.transpose(PS[B_TOK][:, j * 65:(j + 1) * 65], OS[ob][0:65, j * 128:(j + 1) * 128], ident[0:65, 0:65]),
                   reads=[f'OS{ob}', 'ident'], writes=[pk(B_TOK)])
            epilogue(G)
        pend_epi.append(fin)

    def flush_epi():
        while pend_epi:
            pend_epi.pop(0)()

    def attention(qt, K, tiles, epilogue, qkey='QT'):
        for G in range(4):
            attention_group(G, qt, K, tiles[G], epilogue, qkey)

    def flush_bg():
        while bg:
            bg.pop(0)()

    def tok_view():
        return PS[B_TOK][:, 0:260].rearrange("p (j d) -> p j d", j=4)

    GB = AR.alloc("GB", [128, 16, 24], F32)
    wg = AR.alloc("wg", [128, 8, 24], BF16)
    imp = AR.alloc("imp", [128, 16, 2, 64], F32)
    SM = [AR.alloc(f"SM{g}", [128, 2048], BF16) for g in range(2)]
    QB = [AR.alloc(f"QB{i}", [128, 2048], BF16) for i in range(2)]
    wq = [AR.alloc(f"wq{i}", [128, 8, 128], BF16) for i in range(2)]
    rs = AR.alloc("rs", [128, 4], F32)
    fct = AR.alloc("fct", [128, 4], F32)
    mark_att = AR.mark()
    cvt = AR.alloc("cvt", [128, 2, 2048], BF16)
    itmp = AR.alloc("itmp", [128, 2, 2, 64], F32)

    load_w(wg, C_GB, 24, 'wg')
    dma('pool', cvt[:], cvd.ap(), writes=['cvt'])
    for s in range(16):
        tcol = 2048 + s * 128
        for c in range(8):
            op('pe', lambda: T.matmul(PS[B_MS][:, s * 24:(s + 1) * 24], hT[:, c, tcol:tcol + 128], wg[:, c, :],
                                      start=(c == 0), stop=(c == 7)),
               reads=['wg', f'hT{4 + s // 4}'], writes=[pk(B_MS)])
    op('act', lambda: A.activation(out=GB[:, :, :], in_=PS[B_MS][:, 0:384].rearrange("p (s k) -> p s k", s=16), func=AF.Sigmoid),
       writes=[pk(B_MS), 'GB'])

    wq_n = [0]

    def prefetch_wq(i):
        w_ = wq_n[0] % 2
        load_w(wq[w_], C_QB + (i % 4) * 128, 128, f'wq{w_}')

    def project_qb(i):
        w_ = wq_n[0] % 2
        wq_n[0] += 1
        prefetch_wq(i + 1)
        prev = None
        for G in range(4):
            bk = proj_fm(wq[w_], f'wq{w_}', 0, 4 + G, PJ3)
            cs = slice(G * 512, (G + 1) * 512)
            if prev is not None:
                headnorm(*prev)
            prev = (bk, gnq[:, 2:3], [(QB[0][0:64, cs], 'QT0'), (QB[1][0:64, cs], 'QT1')])
        headnorm(*prev)

    prefetch_wq(0)

    def cmpA(i, hd, G):
        g = i // 2
        h = 2 * i + hd
        cs = slice(G * 512, (G + 1) * 512)
        pts = []
        for ct in range(2):
            lb = [0, 1][ct]
            pb_ = pt_rot[0] % NPT
            pt_rot[0] += 1
            op('pe', lambda: T.matmul(PS[lb][:, :], KC[0:64, g, ct * 128:(ct + 1) * 128], QB[hd][0:64, cs],
                                      start=True, stop=True), reads=['KC', f'QT{hd}'], writes=[pk(lb)])
            op('act', lambda: A.activation(out=PT[pb_][:, :], in_=PS[lb][:, :], func=AF.Exp),
               writes=[pk(lb), f'PT{pb_}'])
            op('dve', lambda: V.tensor_tensor(out=PT[pb_][:, :], in0=PT[pb_][:, :], in1=cvt[:, ct, cs], op=ALU.mult),
               reads=['cvt'], writes=[f'PT{pb_}'])
            pts.append(pb_)
        return pts

    def cmpB(i, hd, G, pts, tokb):
        g = i // 2
        h = 2 * i + hd
        for j2 in range(2):
            bk = tokb[j2]
            for jj in range(2):
                j = j2 * 2 + jj
                for ct in range(2):
                    op('pe', lambda: T.matmul(PS[bk][:, jj * 129:(jj + 1) * 129], PT[pts[ct]][:, j * 128:(j + 1) * 128],
                                              VC[:, ct, g, :], start=(ct == 0), stop=(ct == 1)),
                       reads=[f'PT{pts[ct]}', 'VCo', 'VC1', 'VCv'], writes=[pk(bk)])
        bks = tokb
        s0s = [4 * G, 4 * G + 2]
        pvs = [PS[b_][:, 0:258].rearrange("p (j c) -> p j c", j=2) for b_ in bks]
        for j2 in range(2):
            op('dve', lambda: V.tensor_scalar(out=rs[:, 2 * j2:2 * j2 + 2], in0=PS[bks[j2]][:, 64:258:129], scalar1=1e-30,
                                              scalar2=None, op0=ALU.add), writes=[pk(bks[j2]), f'rs{j2}'])
        for j2 in range(2):
            op('dve', lambda: V.reciprocal(out=rs[:, 2 * j2:2 * j2 + 2], in_=rs[:, 2 * j2:2 * j2 + 2]), writes=[f'rs{j2}'])
        for j2 in range(2):
            s0 = s0s[j2]
            op('dve', lambda: V.tensor_tensor(out=fct[:, 2 * j2:2 * j2 + 2], in0=rs[:, 2 * j2:2 * j2 + 2],
                                              in1=GB[:, s0:s0 + 2, 3 * h], op=ALU.mult),
               reads=['GB', f'rs{j2}'], writes=[f'fct{j2}'])
        for j2 in range(2):
            s0 = s0s[j2]
            if h % 4 == 0:
                op('dve', lambda: V.tensor_tensor(out=imp[:, s0:s0 + 2, g, :], in0=pvs[j2][:, :, 65:129],
                                                  in1=rs[:, 2 * j2:2 * j2 + 2].unsqueeze(2).to_broadcast([128, 2, 64]), op=ALU.mult),
                   reads=[f'rs{j2}'], writes=[pk(bks[j2]), f'imp{s0}', f'imp{s0 + 1}'])
            else:
                op('dve', lambda: V.tensor_tensor(out=itmp[:, j2, :, :], in0=pvs[j2][:, :, 65:129],
                                                  in1=rs[:, 2 * j2:2 * j2 + 2].unsqueeze(2).to_broadcast([128, 2, 64]), op=ALU.mult),
                   reads=[f'rs{j2}'], writes=[pk(bks[j2]), f'itmp{j2}'])
        for j2 in range(2):
            s0 = s0s[j2]
            op('dve', lambda: V.tensor_tensor(out=o_b[:, s0:s0 + 2, h * 64:(h + 1) * 64], in0=pvs[j2][:, :, 0:64],
                                              in1=fct[:, 2 * j2:2 * j2 + 2].unsqueeze(2).to_broadcast([128, 2, 64]), op=ALU.mult),
               reads=[f'fct{j2}'], writes=[pk(bks[j2]), f'ob{s0}', f'ob{s0 + 1}'])
        if h % 4 != 0:
            for j2 in range(2):
                s0 = s0s[j2]
                op('pool', lambda: P.tensor_tensor(out=imp[:, s0:s0 + 2, g, :], in0=imp[:, s0:s0 + 2, g, :], in1=itmp[:, j2, :, :],
                                                   op=ALU.add), reads=[f'itmp{j2}'], writes=[f'imp{s0}', f'imp{s0 + 1}'])


    units = [(i, hd, G) for i in range(4) for hd in range(2) for G in range(4)]
    project_qb(0)
    stA = cmpA(*units[0])
    for k, u in enumerate(units):
        nxt = units[k + 1] if k + 1 < len(units) else None
        if nxt is not None:
            if nxt[0] != u[0]:
                project_qb(nxt[0])
            stN = cmpA(*nxt)
        cmpB(*u, stA, [[B_TOK, B_OT], [B_PJ0, B_PJ1]][k % 2])
        if nxt is not None:
            stA = stN

    S.barrier()
    AR.reset(mark_att)
    msp = AR.alloc("msp", [128, 8, 64], F32)
    msa = AR.alloc("msa", [128, 8, 64], F32)
    stg = AR.alloc("stg", [128, 8, 128], BF16)
    wrk = AR.alloc("wrk", [128, 8, 64], F32)
    wrk2 = AR.alloc("wrk2", [128, 8, 64], F32)
    mx8 = AR.alloc("mx8", [128, 8, 16], F32)
    op('dve', lambda: V.memset(stg[:], 0.0), writes=['stg'])
    for g in range(2):
        for half in range(2):
            sl = slice(half * 8, half * 8 + 8)
            dma('sp', msp[:], ms_past.ap()[:, sl, :], writes=['msp'])
            dma('sp', msa[:], ms_add.ap()[:, sl, :], writes=['msa'])
            op('dve', lambda: V.tensor_tensor(out=wrk[:], in0=imp[:, sl, g, :], in1=msp[:, :, :], op=ALU.mult),
               reads=[f'imp{s_}' for s_ in range(16)] + ['msp'], writes=['wrk'])
            op('dve', lambda: V.tensor_tensor(out=wrk[:], in0=wrk[:], in1=msa[:, :, :], op=ALU.add), reads=['msa'], writes=['wrk'])
            mxa = [f'mxa{s8}' for s8 in range(8)]
            mxb = [f'mxb{s8}' for s8 in range(8)]
            w2k = [f'w2_{s8}' for s8 in range(8)]
            for s8 in range(8):
                op('dve', lambda: V.max(out=mx8[:, s8, 0:8], in_=wrk[:, s8, :]), reads=['wrk'], writes=[mxa[s8]])
            for s8 in range(8):
                op('dve', lambda: V.match_replace(out=wrk2[:, s8, :], in_to_replace=mx8[:, s8, 0:8], in_values=wrk[:, s8, :],
                                                  imm_value=-3e30), reads=['wrk', mxa[s8]], writes=[w2k[s8]])
            for s8 in range(8):
                op('dve', lambda: V.max(out=mx8[:, s8, 8:16], in_=wrk2[:, s8, :]), reads=[w2k[s8]], writes=[mxb[s8]])
            op('dve', lambda: V.tensor_scalar(out=mx8[:, :, 15:16], in0=mx8[:, :, 15:16], scalar1=-1e29, scalar2=None, op0=ALU.max),
               writes=mxb)
            op('dve', lambda: V.tensor_tensor(out=wrk2[:], in0=wrk[:], in1=mx8[:, :, 15:16].to_broadcast([128, 8, 64]), op=ALU.is_ge),
               reads=['wrk'] + mxb, writes=w2k)
            op('dve', lambda: V.tensor_scalar(out=stg[:, :, 64:128], in0=wrk2[:], scalar1=-1.0, scalar2=MASKV,
                                              op0=ALU.add, op1=ALU.mult), reads=w2k, writes=['stg'])
            for s8 in range(8):
                op('pe', lambda: T.transpose(PSB[:, s8 * 128:(s8 + 1) * 128], stg[:, s8, :], identb[:]),
                   reads=['stg', 'identb'], writes=['psb'])
            op('act', lambda: A.copy(out=SM[g][64:128, half * 1024:(half + 1) * 1024], in_=PSB[64:128, :]),
               writes=['psb', f'SM{g}'])

    S.barrier()
    AR.reset(mark_att)
    Est = AR.alloc("Est", [128, 1024], F32)
    Et = AR.alloc("Et", [128, 1024], BF16)
    Etc = AR.alloc("Etc", [128, 640], BF16)
    Esc = AR.alloc("Esc", [128, 640], F32)

    def E_exp(ncols, stg_, skey, dst, hcol, key):
        op('act', lambda: A.activation(out=dst[:, 0:ncols], in_=stg_[:, 0:ncols], func=AF.Exp, bias=nb31[:, hcol:hcol + 1], scale=1.0),
           reads=[skey, 'nb31'], writes=[key])

    def load_E(src_ap, ncols, dst, hcol, key='E'):
        dma('sp', Est[:, 0:ncols], src_ap, writes=['Est'])
        op('act', lambda: A.activation(out=dst[:, 0:ncols], in_=Est[:, 0:ncols], func=AF.Exp, bias=nb31[:, hcol:hcol + 1], scale=1.0),
           reads=['Est', 'nb31'], writes=[key])

    def nsa_epilogue(h, br, first=False):
        def ep(G):
            tv = tok_view()
            op('dve', lambda: V.tensor_scalar(out=rs[:, 0:4], in0=PS[B_TOK][:, 64:260:65], scalar1=1e-30, scalar2=None, op0=ALU.add),
               writes=[pk(B_TOK), 'rs'])
            op('dve', lambda: V.reciprocal(out=rs[:, 0:4], in_=rs[:, 0:4]), writes=['rs'])
            op('dve', lambda: V.tensor_tensor(out=fct[:, 0:4], in0=rs[:, 0:4], in1=GB[:, 4 * G:4 * G + 4, 3 * h + br], op=ALU.mult),
               reads=['GB', 'rs'], writes=['fct'])
            for j in range(4):
                s = 4 * G + j
                op('dve', lambda: V.scalar_tensor_tensor(out=o_b[:, s, h * 64:(h + 1) * 64], in0=tv[:, j, 0:64],
                                                         scalar=fct[:, j:j + 1], in1=o_b[:, s, h * 64:(h + 1) * 64],
                                                         op0=ALU.mult, op1=ALU.add),
                   reads=['fct', pk(B_TOK)], writes=[f'ob{s}'])
        return ep

    def dense_tiles(Kt, Vt, vsel, kkeys, Eap):
        tiles = []
        for G in range(4):
            tl = []
            for u in range(16):
                s_lo, s_hi = 4 * G, 4 * G + 4
                d_lo = 16 + s_lo - u
                e_n = max(0, min(s_hi, u + 8 - 16) - s_lo) if d_lo <= 7 else 0
                tl.append((Kt[:, u * 128:(u + 1) * 128], vsel(u), s_lo, s_hi, Eap if e_n > 0 else None, d_lo, e_n, kkeys))
            for u in range(4 * G + 4):
                s_lo, s_hi = max(u, 4 * G), 4 * G + 4
                d_lo = s_lo - u
                e_n = max(0, min(s_hi, u + 8) - s_lo) if d_lo <= 7 else 0
                ua = 16 + u
                tl.append((Kt[:, ua * 128:(ua + 1) * 128], vsel(ua), s_lo, s_hi, Eap if e_n > 0 else None, d_lo, e_n, kkeys))
            tiles.append(tl)
        return tiles

    for i in range(4):
        g = i // 2
        project_qb(i)
        for hd in range(2):
            op('dve', lambda: V.tensor_copy(out=QB[hd][64:128, :], in_=SM[g][64:128, :]), reads=[f'SM{g}'], writes=[f'QT{hd}'])
        for hd in range(2):
            h = 2 * i + hd
            if h == 0:
                dma('sp', Est[:, 0:1024], es.ap()[h], writes=['Est'])
            E_exp(1024, Est, 'Est', Et, 8 + h, 'E')
            dma('sp', Est[:, 0:640], ewo.ap()[h], writes=['Est'])
            dma('sp', Esc[:, 0:640], ewc.ap()[h], writes=['Esc'])
            tiles = dense_tiles(KS[g], VS, lambda ua: VS[:, ua, g, :], ['KS0d', 'KS1d', 'KS0m', 'KS1m', 'VS', 'VS1'], Et)
            attention(QB[hd], 128, tiles, nsa_epilogue(h, 1), qkey=f'QT{hd}')
            E_exp(640, Est, 'Est', Et, 8 + h, 'E')
            E_exp(640, Esc, 'Esc', Etc, 8 + h, 'Ec')
            if h + 1 < 8:
                dma('sp', Est[:, 0:1024], es.ap()[h + 1], writes=['Est'])
            for G in range(4):
                tl = []
                order = []
                if G == 0:
                    order = [('c', 15), ('c', 14), ('c', 13), ('c', 12)] + [('o', u) for u in range(4)]
                else:
                    order = [('o', 4 * G - 1)] + [('o', u) for u in range(4 * G - 4, 4 * G - 1)] + [('o', u) for u in range(4 * G, 4 * G + 4)]
                wt = []
                wkeys = ['KW0', 'KW1', 'VW', 'VW1']
                for kind, u in order:
                    if kind == 'c':
                        s_lo, s_hi = max(0, u - 15), min(4, u - 11)
                        d_lo = 16 + s_lo - u
                        wt.append((KW[g][0:64, u * 128:(u + 1) * 128], VW[:, u, g, :], s_lo, s_hi, Etc, d_lo, s_hi - s_lo, wkeys))
                    else:
                        s_lo, s_hi = max(u, 4 * G), min(4 * G + 4, u + 5)
                        d_lo = s_lo - u
                        ua = 16 + u
                        wt.append((KW[g][0:64, ua * 128:(ua + 1) * 128], VW[:, ua, g, :], s_lo, s_hi, Et, d_lo, s_hi - s_lo, wkeys))
                attention_group(G, QB[hd], 64, wt, nsa_epilogue(h, 2), qkey=f'QT{hd}')
    flush_epi()
    S.barrier()
    AR.reset(mark_h)

    o_a = AR.alloc("o_a", [128, 16, 512], F32)
    mark_g = AR.mark()
    KA = [AR.alloc(f"KA{i}", [128, 4096], BF16) for i in range(2)]
    QA = [AR.alloc(f"QA{i}", [128, 2048], BF16) for i in range(2)]
    VA = AR.alloc("VA", [128, 32, 2, 65], BF16)
    wa2 = [AR.alloc(f"wa{i}", [128, 8, 384], BF16) for i in range(2)]
    Est = AR.alloc("Est2", [128, 1024], F32)
    Et = AR.alloc("Et2", [128, 1024], BF16)
    rs = AR.alloc("rs2", [128, 4], F32)
    stg = AR.alloc("stg2", [128, 16, 128], BF16)
    map_ = AR.alloc("map", [128, 16, 16], F32)
    maa = AR.alloc("maa", [128, 16, 16], F32)
    mao = AR.alloc("mao", [128, 16, 16], F32)
    km32 = AR.alloc("km32", [64, 16], F32)
    kmb = AR.alloc("kmb", [64, 16], BF16)
    wk16 = AR.alloc("wk16", [128, 256], F32)
    wk16b = AR.alloc("wk16b", [128, 256], F32)
    mx8 = AR.alloc("mx8b", [128, 128], F32)
    for j_, c0_ in enumerate((C_QA, C_KA, C_VA)):
        load_w(wa2[0], c0_, 128, 'wa0', j_ * 128)
    dma('sp', map_[:], ma_past.ap(), writes=['map'])
    dma('sp', maa[:], ma_add.ap(), writes=['maa'])
    dma('sp', mao[:], ma_own.ap(), writes=['mao'])
    op('dve', lambda: V.memset(stg[:], 0.0), writes=['stg'])
    op('pool', lambda: P.memset(VA[:, :, :, 64:65], 1.0), writes=['VA1'])
    for hd in range(2):
        op('pool', lambda: P.memset(KA[hd][64:128, :], 0.0), writes=[f'KA{hd}m'])
        dma('pool', KA[hd][64:80, :], oh16.ap(), writes=[f'KA{hd}m'])

    def moba_epilogue(h):
        def ep(G):
            tv = tok_view()
            op('dve', lambda: V.tensor_scalar(out=rs[:, 0:4], in0=PS[B_TOK][:, 64:260:65], scalar1=1e-30, scalar2=None, op0=ALU.add),
               writes=[pk(B_TOK), 'rs'])
            op('dve', lambda: V.reciprocal(out=rs[:, 0:4], in_=rs[:, 0:4]), writes=['rs'])
            for j in range(4):
                s = 4 * G + j
                op('dve', lambda: V.tensor_scalar(out=o_a[:, s, h * 64:(h + 1) * 64], in0=tv[:, j, 0:64],
                                                  scalar1=rs[:, j:j + 1], scalar2=None, op0=ALU.mult),
                   reads=['rs', pk(B_TOK)], writes=[f'oa{s}'])
        return ep

    def load_wa(i):
        w_ = wa2[i % 2]
        load_w(w_, C_QA + i * 128, 128, f'wa{i % 2}', 0)
        load_w(w_, C_KA + i * 128, 128, f'wa{i % 2}', 128)
        load_w(w_, C_VA + i * 128, 128, f'wa{i % 2}', 256)
    Etg = [Et, AR.alloc("Et3", [128, 1024], BF16)]

    def v_tile(wa, wak, t):
        bk = [1, B_OT][t % 2]
        for c in range(8):
            op('pe', lambda: T.matmul(PS[bk][:, 0:128], hT[:, c, t * 128:(t + 1) * 128], wa[:, c, 256:384],
                                      start=(c == 0), stop=(c == 7)),
               reads=[wak, f'hT{t // 4}'], writes=[pk(bk)])
        op('act', lambda: A.copy(out=VA[:, t, :, 0:64], in_=PS[bk][:, 0:128].rearrange("p (g d) -> p g d", g=2)),
           writes=[pk(bk), 'VA'])

    def setup_ops(hd, h):
        ops = []
        qk = f'QT{hd}'
        w3 = lambda t_: t_[:, :].rearrange("p (s j) -> p s j", s=16)
        ops.append(lambda: op('dve', lambda: V.reduce_sum(out=km32[:, :], in_=KA[hd][0:64, :].rearrange("p (j n) -> p j n", j=16), axis=AX.X),
                              reads=[f'KA{hd}d'], writes=['km32']))
        ops.append(lambda: op('dve', lambda: V.tensor_scalar(out=kmb[:, :], in0=km32[:, :], scalar1=1.0 / 256, scalar2=None, op0=ALU.mult),
                              reads=['km32'], writes=['kmb']))
        for s in range(16):
            ops.append(lambda s=s: op('pe', lambda: T.matmul(PS[B_MS][:, s * 16:(s + 1) * 16], QA[hd][0:64, s * 128:(s + 1) * 128], kmb[:, :],
                                                             start=True, stop=True), reads=[qk, 'kmb'], writes=[pk(B_MS)]))
        ops.append(lambda: op('dve', lambda: V.tensor_tensor(out=w3(wk16), in0=w3(PS[B_MS][:, 0:256]), in1=map_[:, :, :], op=ALU.mult),
                              reads=['map'], writes=[pk(B_MS), 'wk16']))
        ops.append(lambda: op('dve', lambda: V.tensor_tensor(out=w3(wk16), in0=w3(wk16), in1=maa[:, :, :], op=ALU.add), reads=['maa'], writes=['wk16']))
        mxk = [f'mx8_{s_}' for s_ in range(16)]
        for s in range(16):
            ops.append(lambda s=s: op('dve', lambda: V.max(out=mx8[:, s * 8:(s + 1) * 8], in_=wk16[:, s * 16:(s + 1) * 16]),
                                      reads=['wk16'], writes=[mxk[s]]))
        thr = mx8[:, :].rearrange("p (s k) -> p s k", s=16)[:, :, 2:3].to_broadcast([128, 16, 16])
        ops.append(lambda: op('dve', lambda: V.tensor_tensor(out=w3(wk16b), in0=w3(wk16), in1=thr, op=ALU.is_ge),
                              reads=['wk16'] + mxk, writes=['wk16b']))
        ops.append(lambda: op('dve', lambda: V.tensor_tensor(out=w3(wk16b), in0=w3(wk16b), in1=map_[:, :, :], op=ALU.mult), reads=['map'], writes=['wk16b']))
        ops.append(lambda: op('dve', lambda: V.tensor_tensor(out=w3(wk16b), in0=w3(wk16b), in1=mao[:, :, :], op=ALU.add), reads=['mao'], writes=['wk16b']))
        ops.append(lambda: op('dve', lambda: V.tensor_scalar(out=stg[:, :, 64:80], in0=w3(wk16b), scalar1=-1.0, scalar2=MASKV,
                                                             op0=ALU.add, op1=ALU.mult), reads=['wk16b'], writes=['stg']))
        for half in range(2):
            for s8 in range(8):
                s = half * 8 + s8
                ops.append(lambda s=s, s8=s8: op('pe', lambda: T.transpose(PSB[:, s8 * 128:(s8 + 1) * 128], stg[:, s, :], identb[:]),
                                                 reads=['stg', 'identb'], writes=['psb']))
            ops.append(lambda half=half: op('act', lambda: A.copy(out=QA[hd][64:128, half * 1024:(half + 1) * 1024], in_=PSB[64:128, :]),
                                            writes=['psb', qk]))
        if hd == 1:
            ops.append(lambda: load_E(ea.ap()[h], 1024, Etg[hd], h, key=f'E{hd}'))
        return ops

    for i in range(4):
        wa = wa2[i % 2]
        wak = f'wa{i % 2}'
        if i + 1 < 4:
            load_wa(i + 1)
        load_E(ea.ap()[2 * i], 1024, Etg[0], 2 * i, key='E0')
        vt = 0
        items = []
        for g8 in range(8):
            cs = slice(g8 * 512, (g8 + 1) * 512)
            items.append((128, g8, gn[:, 1:2], [(KA[0][0:64, cs], 'KA0d'), (KA[1][0:64, cs], 'KA1d')]))
        for G in range(4):
            cs = slice(G * 512, (G + 1) * 512)
            items.append((0, 4 + G, gnq[:, 0:1], [(QA[0][0:64, cs], 'QT0'), (QA[1][0:64, cs], 'QT1')]))
        prev = None
        for wc0, g8, gcol, dsts in items:
            bk = proj_fm(wa, wak, wc0, g8, PJ3)
            if prev is not None:
                headnorm(*prev)
            prev = (bk, gcol, dsts)
            for _ in range(3):
                if vt < 32:
                    v_tile(wa, wak, vt)
                    vt += 1
        headnorm(*prev)
        while vt < 32:
            v_tile(wa, wak, vt)
            vt += 1
        for f_ in setup_ops(0, 2 * i):
            f_()
        bg.extend(setup_ops(1, 2 * i + 1))
        for hd in range(2):
            h = 2 * i + hd
            if hd == 1:
                flush_bg()
            tiles = dense_tiles(KA[hd], VA, lambda ua: VA[:, ua, hd, :], [f'KA{hd}d', f'KA{hd}m', 'VA', 'VA1'], Etg[hd])
            attention(QA[hd], 128, tiles, moba_epilogue(h), qkey=f'QT{hd}')
    flush_epi()
    if debug:
        for s in range(16):
            dma('sp', dbg['oa'].ap()[s * 128:(s + 1) * 128, :], o_a[:, s, :], reads=[f'oa{s}'], writes=['dbgoa'])
            dma('sp', dbg['ob'].ap()[s * 128:(s + 1) * 128, :], o_b[:, s, :], reads=[f'ob{s}'], writes=['dbgob'])
    S.barrier()
    AR.reset(mark_g)

    yT = [AR.alloc(f"yT{b}", [128, 4, 2048], BF16) for b in range(2)]
    wz = [AR.alloc(f"wz{i}", [128, 8, 128], BF16) for i in range(2)]
    sz = [AR.alloc(f"sz{i}", [128, 512], F32) for i in range(2)]
    tA = [AR.alloc(f"tA{i}", [128, 512], F32) for i in range(2)]
    m_h1 = AR.mark()
    AR.reset(AR.offs['PT0'])
    xo = [AR.alloc(f"xo{i}", [128, 512], F32) for i in range(3)]
    res = [AR.alloc(f"res{i}", [128, 512], F32) for i in range(2)]
    assert AR.cur <= AR.offs['small']
    AR.reset(m_h1)
    wo = AR.alloc("wo", [128, 8, 1024], BF16)
    dma('pool', wo[:], w_out.ap().rearrange("(c p) n -> p c n", p=128), writes=['wo'])
    it = 0
    wi = 0
    zcols = [C_ZA + ci * 128 for ci in range(4)] + [C_ZB + ci * 128 for ci in range(4)]
    load_w(wz[0], zcols[0], 128, 'wz0')
    for br in range(2):
        osrc = o_a if br == 0 else o_b
        okey = 'oa' if br == 0 else 'ob'
        for ci in range(4):
            wb = wi % 2
            wi += 1
            if wi < 8:
                load_w(wz[wi % 2], zcols[wi], 128, f'wz{wi % 2}')
            for G in range(4):
                cs = slice(G * 512, (G + 1) * 512)
                b2 = it % 2
                it += 1
                tb = [B_TOK, B_OT][b2]
                bk = proj_fm(wz[wb], f'wz{wb}', 0, 4 + G)
                op('act', lambda: A.activation(out=sz[b2][:], in_=PS[bk][:, :], func=AF.Silu), writes=[pk(bk), f'sz{b2}'])
                for j in range(4):
                    s = 4 * G + j
                    op('pe', lambda: T.transpose(PS[tb][:, j * 128:(j + 1) * 128], osrc[:, s, ci * 128:(ci + 1) * 128], ident[:]),
                       reads=[f'{okey}{s}', 'ident'], writes=[pk(tb)])
                op('dve', lambda: V.tensor_tensor(out=yT[br][:, ci, cs], in0=PS[tb][:, :], in1=sz[b2][:], op=ALU.mult),
                   reads=[f'sz{b2}'], writes=[pk(tb), f'yT{br}'])
    S.barrier()
    m_end = AR.mark()
    AR.reset(AR.offs['o_b'])
    mg = AR.alloc("mg", [128, 8, 2048], BF16)
    AR.reset(AR.offs['o_a'])
    wgm = [AR.alloc(f"wgm{i}", [128, 8, 256], BF16) for i in range(2)]
    wbr = [AR.alloc(f"wbr{i}", [128, 2, 4, 128], BF16) for i in range(2)]
    assert AR.cur <= AR.offs['o_a'] + 32768
    AR.reset(m_end)
    it = 0

    def load_merge_w(m):
        wb = m % 2
        load_w(wgm[wb], C_GM + m * 128, 128, f'wgm{wb}', 0)
        load_w(wgm[wb], C_GM + 1024 + m * 128, 128, f'wgm{wb}', 128)
        dma('pool', wbr[wb][:, 0, :, :], w_ba.ap().rearrange("(c p) n -> p c n", p=128)[:, :, m * 128:(m + 1) * 128], writes=[f'wbr{wb}'])
        dma('pool', wbr[wb][:, 1, :, :], w_bb.ap().rearrange("(c p) n -> p c n", p=128)[:, :, m * 128:(m + 1) * 128], writes=[f'wbr{wb}'])
    load_merge_w(0)
    for m in range(8):
        wb = m % 2
        if m + 1 < 8:
            load_merge_w(m + 1)
        for G in range(4):
            cs = slice(G * 512, (G + 1) * 512)
            for br in range(2):
                b2 = it % 2
                it += 1
                ob = [B_TOK, B_OT][b2]
                bk = proj_fm(wgm[wb], f'wgm{wb}', br * 128, 4 + G)
                op('act', lambda: A.activation(out=sz[b2][:], in_=PS[bk][:, :], func=AF.Sigmoid), writes=[pk(bk), f'sz{b2}'])
                for ci in range(4):
                    op('pe', lambda: T.matmul(PS[ob][:, :], wbr[wb][:, br, ci, :], yT[br][:, ci, cs], start=(ci == 0), stop=(ci == 3)),
                       reads=[f'wbr{wb}', f'yT{br}'], writes=[pk(ob)])
                if br == 0:
                    ta = tA[G % 2]
                    op('dve', lambda: V.tensor_tensor(out=ta[:], in0=PS[ob][:, :], in1=sz[b2][:], op=ALU.mult),
                       reads=[f'sz{b2}'], writes=[pk(ob), f'tA{G % 2}'])
                else:
                    op('dve', lambda: V.tensor_tensor(out=sz[b2][:], in0=PS[ob][:, :], in1=sz[b2][:], op=ALU.mult),
                       writes=[pk(ob), f'sz{b2}'])
                    op('pool', lambda: P.tensor_tensor(out=mg[:, m, cs], in0=tA[G % 2][:], in1=sz[b2][:], op=ALU.add),
                       reads=[f'tA{G % 2}', f'sz{b2}'], writes=['mg'])
    def load_x(k_):
        s_, hf_ = k_ // 2, k_ % 2
        dma('pool', xo[k_ % 3][:], xa.ap()[2048 + s_ * 128:2048 + (s_ + 1) * 128, hf_ * 512:(hf_ + 1) * 512], writes=[f'xo{k_ % 3}'])
    load_x(0)
    load_x(1)
    for k in range(32):
        s, hf = k // 2, k % 2
        if k + 2 < 32:
            load_x(k + 2)
        bk = [B_PJ0, B_PJ1][k % 2]
        xb, rb = k % 3, k % 2
        for m in range(8):
            op('pe', lambda: T.matmul(PS[bk][:, :], mg[:, m, s * 128:(s + 1) * 128], wo[:, m, hf * 512:(hf + 1) * 512],
                                      start=(m == 0), stop=(m == 7)), reads=['mg', 'wo'], writes=[pk(bk)])
        op('dve', lambda: V.tensor_tensor(out=res[rb][:], in0=PS[bk][:, :], in1=xo[xb][:], op=ALU.add),
           reads=[f'xo{xb}'], writes=[pk(bk), f'res{rb}'])
        dma('sp', out.ap()[s * 128:(s + 1) * 128, hf * 512:(hf + 1) * 512], res[rb][:], reads=[f'res{rb}'], writes=[f'outd{k}'])
    S.barrier()
    return nc


def _t5_bucket(dist):
    n = np.maximum(dist, 0)
    nf = np.maximum(n, 16).astype(np.float32)
    large = 16 + (np.log(nf / np.float32(16)) / np.float32(np.log(64.0)) * np.float32(16)).astype(np.int32)
    return np.where(n < 16, n, np.minimum(large, 31))


def _tables(h, rel_bias):
    kk = np.arange(128)[:, None]
    tb = {}

    def strip(nd, heads, lo_valid, hi_valid):
        cols = np.arange(nd * 128)[None, :]
        dist = cols - kk
        bkt = _t5_bucket(dist)
        outp = np.empty((len(heads), 128, nd * 128), np.float32)
        ok = (dist >= lo_valid) & (dist < hi_valid)
        for i, hh in enumerate(heads):
            v = rel_bias[bkt, hh]
            outp[i] = np.where(ok, v, np.float32(-MASKV))
        return outp

    tb['ea'] = strip(8, list(range(8)), 0, 1 << 30)
    tb['es'] = strip(8, list(range(8, 16)), 0, 1 << 30)
    tb['ewo'] = strip(5, list(range(8, 16)), 0, 512)
    tb['ewc'] = tb['ewo'].copy() if h == 1 else np.full_like(tb['ewo'], -MASKV)
    c = (np.arange(2)[None, :, None] * 128 + np.arange(128)[:, None, None])
    t_all = 2048 + np.arange(2048)[None, None, :]
    cv = (16 * c + 31 <= t_all) & (c <= 254)
    if h == 0:
        cv &= (c >= 128)
    tb['cv'] = cv.astype(np.float32)
    q = np.arange(2048)
    qb = (2048 + q) // 256
    j = np.arange(16)[None, :]
    past = (j < qb[:, None]) & ((h == 1) | (j >= 8))
    own = (j == qb[:, None])
    pl = lambda a_: np.ascontiguousarray(a_.reshape(16, 128, a_.shape[1]).transpose(1, 0, 2))
    tb['ma_past'] = pl(past.astype(np.float32))
    tb['ma_add'] = pl(np.where(past, 0.0, -1e30).astype(np.float32))
    tb['ma_own'] = pl(own.astype(np.float32))
    qs = (2048 + q) // 64
    j = np.arange(64)[None, :]
    first = 0 if h == 1 else 32
    forced_first = (j == first)
    forced_own = (j == qs[:, None])
    pasts = (j < qs[:, None]) & (j > first) & ~forced_own
    add = np.full((2048, 64), -1e30, np.float32)
    add[pasts] = 0.0
    add[np.broadcast_to(forced_own, add.shape)] = 1e30
    add[np.broadcast_to(forced_first, add.shape)] = 2e30
    tb['ms_past'] = pl(pasts.astype(np.float32))
    tb['ms_add'] = pl(add)
    col = np.arange(4096)
    tb['oh16'] = (col[None, :] // 256 == np.arange(16)[:, None]).astype(np.float32)
    tb['oh64'] = (col[None, :] // 64 == np.arange(64)[:, None]).astype(np.float32)
    cc = c[:, :, 0][:, :, None]
    jj = np.arange(64)[None, None, :]
    ov = (16 * cc < 64 * jj + 64) & (16 * cc + 32 > 64 * jj) & (cc <= 254)
    tb['ovT'] = ov.astype(np.float32)
    tb['ident'] = np.eye(128, dtype=np.float32)
    return tb


_PROG = {}


def kernel(x, norm_w, w_in, q_norm_a, k_norm_a, q_norm_b, k_norm_cmp, k_norm_sel, k_norm_win,
           cmp_pos_k, cmp_w1_k, cmp_w2_k, cmp_pos_v, cmp_w1_v, cmp_w2_v, rel_bias,
           w_branch_a, w_branch_b, w_out, _debug=False):
    f = lambda a: np.ascontiguousarray(np.asarray(a, dtype=np.float32))
    x = f(x)
    rel_bias = f(rel_bias)
    common = dict(
        norm_w=np.ascontiguousarray(f(norm_w)[0].reshape(8, 128).T), w_in=f(w_in)[0],
        gains=np.stack([f(q_norm_a)[0], f(k_norm_a)[0], f(q_norm_b)[0], f(k_norm_cmp)[0], f(k_norm_sel)[0], f(k_norm_win)[0]]),
        cmp_pos=np.ascontiguousarray(np.stack([f(cmp_pos_k)[0].T, f(cmp_pos_v)[0].T])),
        cmp_w1=np.stack([f(cmp_w1_k)[0], f(cmp_w1_v)[0]]),
        cmp_w2=np.stack([f(cmp_w2_k)[0], f(cmp_w2_v)[0]]),
        rel_bias=rel_bias, w_ba=f(w_branch_a)[0], w_bb=f(w_branch_b)[0], w_out=f(w_out)[0],
    )
    common['gainsT'] = np.ascontiguousarray(common['gains'].T)
    tabs = [_tables(h, rel_bias) for h in range(2)]
    in_maps = []
    for c in range(8):
        b, h = c // 2, c % 2
        xa = np.zeros((4096, 1024), np.float32)
        if h == 1:
            xa[:] = x[b]
        else:
            xa[2048:] = x[b, :2048]
        m = dict(common)
        m.update(tabs[h])
        m['xa'] = xa
        in_maps.append(m)
    key = bool(_debug)
    if key not in _PROG:
        _PROG[key] = build_program(debug=key)
    nc = _PROG[key]
    r = run_bass_kernel_spmd(nc, in_maps, core_ids=list(range(8)))
    outp = np.empty((4, 4096, 1024), np.float32)
    for c in range(8):
        b, h = c // 2, c % 2
        outp[b, h * 2048:(h + 1) * 2048] = r.results[c]["out"]
    if _debug:
        return outp, r.results
    return outp
```

```python
import numpy as np
import concourse.bass as bass
import concourse.mybir as mybir
from concourse.bass_utils import run_bass_kernel_spmd

F32 = mybir.dt.float32
BF16 = mybir.dt.bfloat16
AF = mybir.ActivationFunctionType
ALU = mybir.AluOpType
AX = mybir.AxisListType

EPS = 1e-6
MASKV = 30000.0
C_QA, C_KA, C_VA, C_ZA, C_QB, C_KC, C_VC, C_KS, C_VS, C_KW, C_VW, C_GB, C_ZB, C_GM = (
    0, 512, 1024, 1536, 2048, 2560, 2688, 2816, 2944, 3072, 3200, 3328, 3352, 3864)


class Sched:
    def __init__(self, nc, n_dma_sems=24):
        self.nc = nc
        self.engs = {'pe': nc.tensor, 'act': nc.scalar, 'dve': nc.vector, 'pool': nc.gpsimd, 'sp': nc.sync}
        self.sem = {k: nc.alloc_semaphore(name=f"s_{k}") for k in self.engs}
        self.cnt = {k: 0 for k in self.engs}
        self.waited = {k: {} for k in self.engs}
        self.dpool = {}
        for q, n in (('sp', 14), ('pool', 10)):
            self.dpool[q] = dict(sems=[nc.alloc_semaphore(name=f"d{q}{i}") for i in range(n)], val=[0] * n, nxt=0)
        self.lastw = {}
        self.readers = {}

    def _wait(self, e, deps):
        best = {}
        for d in deps:
            if d is None:
                continue
            s, v = d
            if v > best.get(id(s), (None, 0))[1]:
                best[id(s)] = (s, v)
        for sid, (s, v) in best.items():
            if self.waited[e].get(sid, 0) >= v:
                continue
            if e == 'pe' and s is self.sem['pe']:
                continue
            self.engs[e].wait_ge(s, v)
            self.waited[e][sid] = v

    def _deps(self, reads, writes):
        deps = []
        for k in reads:
            deps.append(self.lastw.get(k))
        for k in writes:
            deps.append(self.lastw.get(k))
            deps.extend(self.readers.get(k, {}).values())
        return deps

    def _commit(self, reads, writes, tok):
        for k in reads:
            r = self.readers.setdefault(k, {})
            old = r.get(id(tok[0]))
            if old is None or old[1] < tok[1]:
                r[id(tok[0])] = tok
        for k in writes:
            self.lastw[k] = tok
            self.readers[k] = {}

    def op(self, e, fn, reads=(), writes=()):
        self._wait(e, self._deps(reads, writes))
        ins = fn()
        self.cnt[e] += 1
        ins.then_inc(self.sem[e], 1)
        tok = (self.sem[e], self.cnt[e])
        self._commit(reads, writes, tok)
        return tok

    def dma(self, e, out, in_, reads=(), writes=(), **kw):
        dp = self.dpool[e]
        i = dp['nxt']
        dp['nxt'] = (i + 1) % len(dp['sems'])
        s = dp['sems'][i]
        deps = self._deps(reads, writes)
        if dp['val'][i] > 0:
            deps.append((s, dp['val'][i]))
        self._wait(e, deps)
        self.engs[e].dma_start(out=out, in_=in_, **kw).then_inc(s, 16)
        dp['val'][i] += 16
        tok = (s, dp['val'][i])
        self._commit(reads, writes, tok)
        return tok

    def barrier(self):
        toks = [(self.sem[k], self.cnt[k]) for k in self.engs if self.cnt[k] > 0]
        for dp in self.dpool.values():
            toks += [(s, v) for s, v in zip(dp['sems'], dp['val']) if v > 0]
        for e in self.engs:
            self._wait(e, toks)

    def finish(self, e, keys):
        self._wait(e, [self.lastw.get(k) for k in keys])


class Arena:
    def __init__(self, nc, base, limit):
        self.nc = nc
        self.cur = base
        self.limit = limit
        self.n = 0
        self.offs = {}

    def alloc(self, name, shape, dt):
        esz = 2 if dt == BF16 else 4
        nbytes = esz
        for d in shape[1:]:
            nbytes *= d
        self.cur = (self.cur + 63) // 64 * 64
        off = self.cur
        self.cur += nbytes
        assert self.cur <= self.limit, f"SBUF overflow {name} {self.cur}"
        self.n += 1
        self.offs[name] = off
        return self.nc.alloc_sbuf_tensor_at(f"{name}_{self.n}", list(shape), dt, offset=off)

    def mark(self):
        return self.cur

    def reset(self, m):
        self.cur = m


def build_program(debug=False):
    nc = bass.Bass("TRN2", target_bir_lowering=False)

    def din(name, shape):
        return nc.dram_tensor(name, list(shape), F32, kind="ExternalInput")

    xa = din("xa", [4096, 1024])
    norm_w = din("norm_w", [128, 8])
    w_in = din("w_in", [1024, 5912])
    gains = din("gains", [6, 64])
    gainsT = din("gainsT", [64, 6])
    cmp_pos = din("cmp_pos", [2, 64, 32])
    cmp_w1 = din("cmp_w1", [2, 2048, 256])
    cmp_w2 = din("cmp_w2", [2, 256, 64])
    rel_bias = din("rel_bias", [32, 16])
    w_ba = din("w_ba", [512, 1024])
    w_bb = din("w_bb", [512, 1024])
    w_out = din("w_out", [1024, 1024])
    ea = din("ea", [8, 128, 1024])
    es = din("es", [8, 128, 1024])
    ewo = din("ewo", [8, 128, 640])
    ewc = din("ewc", [8, 128, 640])
    cvd = din("cv", [128, 2, 2048])
    ma_past = din("ma_past", [128, 16, 16])
    ma_add = din("ma_add", [128, 16, 16])
    ma_own = din("ma_own", [128, 16, 16])
    ms_past = din("ms_past", [128, 16, 64])
    ms_add = din("ms_add", [128, 16, 64])
    oh16 = din("oh16", [16, 4096])
    oh64 = din("oh64", [64, 4096])
    ovT = din("ovT", [128, 2, 64])
    identd = din("ident", [128, 128])
    out = nc.dram_tensor("out", [2048, 1024], F32, kind="ExternalOutput")
    dbg = {}
    if debug:
        dbg['oa'] = nc.dram_tensor("dbg_oa", [2048, 512], F32, kind="ExternalOutput")
        dbg['ob'] = nc.dram_tensor("dbg_ob", [2048, 512], F32, kind="ExternalOutput")

    S = Sched(nc)
    AR = Arena(nc, 16640, 229376 - 256)
    op, dma = S.op, S.dma
    V, A, P, T = nc.vector, nc.scalar, nc.gpsimd, nc.tensor

    PS = [nc.alloc_psum_tensor(f"ps{i}", [128, 512], F32) for i in range(7)]
    PSB = nc.alloc_psum_tensor("psb", [128, 1024], BF16)
    LGB = [0, 1, 4, 5]
    NLG = 4
    NPT = 6
    DEPTH = 4
    B_OT, B_TOK, B_PJ0, B_PJ1, B_MS = 2, 3, 4, 5, 6
    PJ3 = [4, 5, 3]

    def pk(i):
        return f"ps{i}"

    hT = AR.alloc("hT", [128, 8, 4096], BF16)
    ident = AR.alloc("ident", [128, 128], F32)
    identb = AR.alloc("identb", [128, 128], BF16)
    bdiag = AR.alloc("bdiag", [128, 128], BF16)
    normw = AR.alloc("normw", [128, 8], F32)
    gn = AR.alloc("gn", [128, 6], F32)
    gnq = AR.alloc("gnq", [128, 6], F32)
    nb31 = AR.alloc("nb31", [128, 16], F32)
    o_b = AR.alloc("o_b", [128, 16, 512], F32)
    PT = [AR.alloc(f"PT{i}", [128, 512], BF16) for i in range(6)]
    OS = [AR.alloc(f"OS{i}", [128, 512], F32) for i in range(2)]
    small = AR.alloc("small", [128, 64], F32)
    sqb = [AR.alloc(f"sqb{i}", [128, 512], BF16) for i in range(2)]
    rstd = [AR.alloc(f"rstd{i}", [128, 512], F32) for i in range(2)]
    epst = AR.alloc("epst", [128, 1], F32)
    mark_persist = AR.mark()
    op('pool', lambda: P.memset(epst[:], EPS), writes=['epst'])

    dma('sp', ident[:], identd.ap(), writes=['ident'])
    dma('pool', identb[:], identd.ap(), writes=['identb'])
    op('pool', lambda: P.memset(bdiag[:], 0.0), writes=['bdiag'])
    op('pool', lambda: P.memset(bdiag[0:64, 0:64], 1.0 / 64), writes=['bdiag'])
    op('pool', lambda: P.memset(bdiag[64:128, 64:128], 1.0 / 64), writes=['bdiag'])
    dma('sp', normw[:], norm_w.ap(), writes=['normw'])
    dma('sp', gn[0:64, :], gainsT.ap(), writes=['gn'])
    dma('sp', gn[64:128, :], gainsT.ap(), writes=['gn'])
    op('dve', lambda: V.tensor_scalar(out=gnq[:], in0=gn[:], scalar1=0.125, scalar2=None, op0=ALU.mult),
       reads=['gn'], writes=['gnq'])
    dma('sp', nb31[:], rel_bias.ap()[31:32, :].partition_broadcast(128), writes=['nb31'])
    op('dve', lambda: V.tensor_scalar(out=nb31[:], in0=nb31[:], scalar1=-1.0, scalar2=None, op0=ALU.mult),
       writes=['nb31'])

    def load_w(dst, c0, ncols, key, dcol=0):
        src = w_in.ap().rearrange("(c p) n -> p c n", p=128)[:, :, c0:c0 + ncols]
        dma('pool', dst[:, :, dcol:dcol + ncols], src, writes=[key])

    pj_rot = [0]

    def proj_fm(wt, wkey, wc0, g8, banks=None):
        banks = banks or [B_PJ0, B_PJ1]
        bk = banks[pj_rot[0] % len(banks)]
        pj_rot[0] += 1
        for c in range(8):
            op('pe', lambda: T.matmul(PS[bk][:, :], wt[:, c, wc0:wc0 + 128], hT[:, c, g8 * 512:(g8 + 1) * 512],
                                      start=(c == 0), stop=(c == 7)),
               reads=[wkey, f'hT{g8}'], writes=[pk(bk)])
        return bk

    mark_h = AR.mark()

    hn_rot = [0]

    def headnorm(bk, gcol_ap, dsts):
        r2 = hn_rot[0] % 2
        hn_rot[0] += 1
        mb = [B_MS, LGB[0]][r2]
        sq_, rs_ = sqb[r2], rstd[r2]
        op('act', lambda: A.activation(out=sq_[:], in_=PS[bk][:, :], func=AF.Square), writes=[pk(bk), f'sqb{r2}'])
        op('pe', lambda: T.matmul(PS[mb][:, :], bdiag[:], sq_[:], start=True, stop=True),
           reads=['bdiag', f'sqb{r2}'], writes=[pk(mb)])
        op('act', lambda: A.activation(out=rs_[:], in_=PS[mb][:, :], func=AF.Ln, bias=epst[:, 0:1], scale=1.0),
           reads=['epst'], writes=[pk(mb), f'rstd{r2}'])
        op('act', lambda: A.activation(out=rs_[:], in_=rs_[:], func=AF.Exp, scale=-0.5), writes=[f'rstd{r2}'])
        for hf, (dst, key) in enumerate(dsts):
            r = slice(hf * 64, hf * 64 + 64)
            op('dve', lambda: V.scalar_tensor_tensor(out=dst, in0=PS[bk][r, :], scalar=gcol_ap[r, :], in1=rs_[r, :],
                                                     op0=ALU.mult, op1=ALU.mult),
               reads=[f'rstd{r2}', 'gn', 'gnq', pk(bk)], writes=[key])

    KC = AR.alloc("KC", [64, 2, 256], BF16)
    VC = AR.alloc("VC", [128, 2, 2, 129], BF16)
    mark_wB = AR.mark()
    wB = AR.alloc("wB", [128, 8, 768], BF16)
    kcT = AR.alloc("kcT", [128, 16, 256], BF16)
    vcT = AR.alloc("vcT", [128, 16, 256], BF16)
    mA = AR.mark()
    xt = [AR.alloc(f"xt{i}", [128, 1024], F32) for i in range(3)]
    xsq = AR.alloc("xsq", [128, 1024], BF16)
    xn = [AR.alloc(f"xn{i}", [128, 1024], BF16) for i in range(2)]
    ss = AR.alloc("ss", [128, 32], F32)
    load_w(wB, C_KC, 768, 'wB')

    def stageA1(t):
        b3 = t % 3
        dma('sp', xt[b3][:], xa.ap()[t * 128:(t + 1) * 128, :], writes=[f'xt{b3}'])
        op('act', lambda: A.activation(out=xsq[:], in_=xt[b3][:], func=AF.Square, accum_out=ss[:, t:t + 1]),
           reads=[f'xt{b3}'], writes=['xsq', f'ss{t}'])

    def stageA1b(t):
        op('dve', lambda: V.tensor_scalar(out=ss[:, t:t + 1], in0=ss[:, t:t + 1], scalar1=1.0 / 1024, scalar2=EPS,
                                          op0=ALU.mult, op1=ALU.add), writes=[f'ss{t}'])
        op('act', lambda: A.activation(out=ss[:, t:t + 1], in_=ss[:, t:t + 1], func=AF.Sqrt), writes=[f'ss{t}'])
        op('dve', lambda: V.reciprocal(out=ss[:, t:t + 1], in_=ss[:, t:t + 1]), writes=[f'ss{t}'])

    def stageA2a(t):
        b = t % 2
        b3 = t % 3
        op('act', lambda: A.activation(out=xn[b][:], in_=xt[b3][:], func=AF.Copy, scale=ss[:, t:t + 1]),
           reads=[f'xt{b3}', f'ss{t}'], writes=[f'xn{b}'])
        for c in range(8):
            op('pe', lambda: T.transpose(PSB[:, c * 128:(c + 1) * 128], xn[b][:, c * 128:(c + 1) * 128], identb[:]),
               reads=[f'xn{b}', 'identb'], writes=['psb'])

    def stageA2b(t):
        op('dve', lambda: V.tensor_tensor(out=hT[:, :, t * 128:(t + 1) * 128],
                                          in0=PSB[:, :].rearrange("p (c n) -> p c n", c=8),
                                          in1=normw[:, :].unsqueeze(2).to_broadcast([128, 8, 128]), op=ALU.mult),
           reads=['normw'], writes=['psb', f'hT{t // 4}'])

    defer = []

    def stageB1(g8):
        st = {}

        def p_kc():
            st['kc'] = proj_fm(wB, 'wB', 0, g8)

        def p_vc():
            st['vc'] = proj_fm(wB, 'wB', 128, g8)

        def e_kc():
            bk = st['kc']
            op('dve', lambda: V.tensor_copy(out=kcT[:, :, g8 * 32:(g8 + 1) * 32], in_=PS[bk][:, :].rearrange("p (n r) -> p r n", r=16)),
               writes=[pk(bk), 'kcT'])

        def e_vc():
            bk = st['vc']
            op('dve', lambda: V.tensor_copy(out=vcT[:, :, g8 * 32:(g8 + 1) * 32], in_=PS[bk][:, :].rearrange("p (n r) -> p r n", r=16)),
               writes=[pk(bk), 'vcT'])
        defer.extend([p_kc, p_vc, e_kc, e_vc])

    stageA1(0)
    stageA1b(0)
    for t in range(32):
        if t + 1 < 32:
            stageA1(t + 1)
        stageA2a(t)
        if t + 1 < 32:
            stageA1b(t + 1)
        stageA2b(t)
        if defer:
            defer.pop(0)()
        if t % 4 == 3:
            stageB1(t // 4)
    while defer:
        defer.pop(0)()
    S.barrier()
    AR.reset(mA)

    w1 = AR.alloc("w1", [128, 32, 256], BF16)
    w2 = AR.alloc("w2", [128, 2, 64], BF16)
    posT = AR.alloc("posT", [64, 32], BF16)
    pb = AR.alloc("pb", [128, 2], F32)
    GH = AR.alloc("GH", [128, 2, 256], BF16)
    u_t = AR.alloc("u_t", [128, 256], F32)
    t_t = AR.alloc("t_t", [128, 256], F32)
    gbc = AR.alloc("gbc", [128, 64], F32)
    kc32 = AR.alloc("kc32", [128, 64], F32)
    kcb = AR.alloc("kcb", [128, 64], BF16)
    dma('sp', gbc[:], gains.ap()[3:4, :].partition_broadcast(128), writes=['gbc'])
    dma('pool', VC[:, :, 0, 65:129], ovT.ap(), writes=['VCo'])
    dma('pool', VC[:, :, 1, 65:129], ovT.ap(), writes=['VCo'])
    op('pool', lambda: P.memset(VC[:, :, :, 64:65], 1.0), writes=['VC1'])
    op('dve', lambda: V.memset(GH[:], 0.0), writes=['GH'])
    for kind in range(2):
        srcT = kcT if kind == 0 else vcT
        w1src = cmp_w1.ap()[kind].rearrange("(l d) m -> d l m", d=64)
        dma('pool', w1[0:64, :, :], w1src, writes=['w1'])
        dma('pool', w1[64:128, :, :], w1src, writes=['w1'])
        dma('pool', w2[:, :, :], cmp_w2.ap()[kind].rearrange("(c p) d -> p c d", p=128), writes=['w2'])
        dma('pool', posT[:, :], cmp_pos.ap()[kind], writes=['posT'])
        for mc in range(2):
            for l in range(32):
                op('pe', lambda: T.matmul(PS[B_MS][:, 0:1], w1[0:64, l, mc * 128:(mc + 1) * 128], posT[:, l:l + 1],
                                          start=(l == 0), stop=(l == 31)),
                   reads=['w1', 'posT'], writes=[pk(B_MS)])
            op('dve', lambda: V.tensor_copy(out=pb[:, mc:mc + 1], in_=PS[B_MS][:, 0:1]), writes=[pk(B_MS), 'pb'])
        for g in range(2):
            r = slice(g * 64, g * 64 + 64)
            for mc in range(2):
                bk = [B_PJ0, B_PJ1][mc]
                for l in range(32):
                    rhs = srcT[r, l % 16, (l // 16):(l // 16) + 255]
                    op('pe', lambda: T.matmul(PS[bk][:, 0:255], w1[r, l, mc * 128:(mc + 1) * 128], rhs,
                                              start=(l == 0), stop=(l == 31)),
                       reads=['w1', 'kcT', 'vcT'], writes=[pk(bk)])
                op('act', lambda: A.activation(out=u_t[:, 0:255], in_=PS[bk][:, 0:255], func=AF.Identity,
                                               bias=pb[:, mc:mc + 1], scale=1.0),
                   reads=['pb'], writes=[pk(bk), 'u_t'])
                op('dve', lambda: V.tensor_tensor(out=t_t[:, 0:255], in0=u_t[:, 0:255], in1=u_t[:, 0:255], op=ALU.mult),
                   reads=['u_t'], writes=['t_t'])
                op('dve', lambda: V.tensor_scalar(out=t_t[:, 0:255], in0=t_t[:, 0:255], scalar1=0.044715, scalar2=1.0,
                                                  op0=ALU.mult, op1=ALU.add), writes=['t_t'])
                op('dve', lambda: V.tensor_tensor(out=t_t[:, 0:255], in0=t_t[:, 0:255], in1=u_t[:, 0:255], op=ALU.mult),
                   reads=['u_t'], writes=['t_t'])
                op('act', lambda: A.activation(out=t_t[:, 0:255], in_=t_t[:, 0:255], func=AF.Sigmoid,
                                               scale=1.5957691216057308), writes=['t_t'])
                op('dve', lambda: V.tensor_tensor(out=GH[:, mc, 0:255], in0=t_t[:, 0:255], in1=u_t[:, 0:255], op=ALU.mult),
                   reads=['u_t', 't_t'], writes=['GH'])
            for ct in range(2):
                for mc in range(2):
                    op('pe', lambda: T.matmul(PS[B_MS][:, 0:64], GH[:, mc, ct * 128:(ct + 1) * 128], w2[:, mc, :],
                                              start=(mc == 0), stop=(mc == 1)),
                       reads=['GH', 'w2'], writes=[pk(B_MS)])
                if kind == 1:
                    op('act', lambda: A.copy(out=VC[:, ct, g, 0:64], in_=PS[B_MS][:, 0:64]), writes=[pk(B_MS), 'VCv'])
                else:
                    op('act', lambda: A.activation(out=kc32[:], in_=PS[B_MS][:, 0:64], func=AF.Square),
                       writes=[pk(B_MS), 'kc32'])
                    op('dve', lambda: V.reduce_sum(out=small[:, 0:1], in_=kc32[:], axis=AX.X), reads=['kc32'], writes=['small'])
                    op('dve', lambda: V.tensor_scalar(out=small[:, 0:1], in0=small[:, 0:1], scalar1=1.0 / 64, scalar2=EPS,
                                                      op0=ALU.mult, op1=ALU.add), writes=['small'])
                    op('act', lambda: A.activation(out=small[:, 0:1], in_=small[:, 0:1], func=AF.Sqrt), writes=['small'])
                    op('dve', lambda: V.reciprocal(out=small[:, 0:1], in_=small[:, 0:1]), writes=['small'])
                    op('dve', lambda: V.scalar_tensor_tensor(out=kcb[:], in0=PS[B_MS][:, 0:64], scalar=small[:, 0:1],
                                                             in1=gbc[:], op0=ALU.mult, op1=ALU.mult),
                       reads=['gbc'], writes=[pk(B_MS), 'small', 'kcb'])
                    op('pe', lambda: T.transpose(PSB[0:64, 0:128], kcb[:, :], identb[:]), reads=['kcb', 'identb'], writes=['psb'])
                    op('act', lambda: A.copy(out=KC[0:64, g, ct * 128:(ct + 1) * 128], in_=PSB[0:64, 0:128]),
                       writes=['psb', 'KC'])
    S.barrier()
    AR.reset(mark_wB)

    KS = [AR.alloc(f"KS{g}", [128, 4096], BF16) for g in range(2)]
    KW = [AR.alloc(f"KW{g}", [64, 4096], BF16) for g in range(2)]
    VS = AR.alloc("VS", [128, 32, 2, 65], BF16)
    VW = AR.alloc("VW", [128, 32, 2, 65], BF16)
    mark_nsa = AR.mark()
    wB = AR.alloc("wB2", [128, 8, 768], BF16)
    load_w(wB, C_KC, 768, 'wB')
    for g in range(2):
        dma('pool', KS[g][64:128, :], oh64.ap(), writes=[f'KS{g}m'])
    op('pool', lambda: P.memset(VS[:, :, :, 64:65], 1.0), writes=['VS1'])
    op('pool', lambda: P.memset(VW[:, :, :, 64:65], 1.0), writes=['VW1'])
    prev = None
    for g8 in range(8):
        cs = slice(g8 * 512, (g8 + 1) * 512)
        for wc0_, gc_, dst_ in ((256, gn[:, 4:5], [(KS[0][0:64, cs], 'KS0d'), (KS[1][0:64, cs], 'KS1d')]),
                                (512, gn[:, 5:6], [(KW[0][0:64, cs], 'KW0'), (KW[1][0:64, cs], 'KW1')])):
            bk = proj_fm(wB, 'wB', wc0_, g8, PJ3)
            if prev is not None:
                headnorm(*prev)
            prev = (bk, gc_, dst_)
    headnorm(*prev)
    for t in range(32):
        bk = [B_PJ0, B_PJ1][t % 2]
        for j, wc in enumerate((384, 640)):
            for c in range(8):
                op('pe', lambda: T.matmul(PS[bk][:, j * 128:(j + 1) * 128], hT[:, c, t * 128:(t + 1) * 128],
                                          wB[:, c, wc:wc + 128], start=(c == 0), stop=(c == 7)),
                   reads=['wB', f'hT{t // 4}'], writes=[pk(bk)])
        op('act', lambda: A.copy(out=VS[:, t, :, 0:64], in_=PS[bk][:, 0:128].rearrange("p (g d) -> p g d", g=2)),
           writes=[pk(bk), 'VS'])
        op('dve', lambda: V.tensor_copy(out=VW[:, t, :, 0:64], in_=PS[bk][:, 128:256].rearrange("p (g d) -> p g d", g=2)),
           writes=[pk(bk), 'VW'])
    S.barrier()
    AR.reset(mark_nsa)

    lg_rot = [0]
    pt_rot = [0]
    os_rot = [0]
    pend_epi = []
    bg = []

    def attention_group(G, qt, K, tl, epilogue, qkey='QT'):
        n = len(tl)
        pend = []
        for i in range(n + DEPTH):
            if i == min(DEPTH, n) and pend_epi:
                pend_epi.pop(0)()
            if i < n:
                kl, vl, s_lo, s_hi, Eap, e_lo, e_n, rkeys = tl[i]
                a = (s_lo - 4 * G) * 128
                b_ = (s_hi - 4 * G) * 128
                lb = LGB[lg_rot[0] % NLG]
                lg_rot[0] += 1
                pb_ = pt_rot[0] % NPT
                pt_rot[0] += 1
                op('pe', lambda: T.matmul(PS[lb][:, a:b_], kl, qt[0:K, s_lo * 128:s_hi * 128], start=True, stop=True),
                   reads=list(rkeys) + [qkey], writes=[pk(lb)])
                op('act', lambda: A.activation(out=PT[pb_][:, a:b_], in_=PS[lb][:, a:b_], func=AF.Exp),
                   writes=[pk(lb), f'PT{pb_}'])
                if Eap is not None and e_n > 0:
                    op('dve', lambda: V.tensor_tensor(out=PT[pb_][:, a:a + e_n * 128], in0=PT[pb_][:, a:a + e_n * 128],
                                                      in1=Eap[:, e_lo * 128:(e_lo + e_n) * 128], op=ALU.mult),
                       reads=['E', 'Ec', 'E0', 'E1'], writes=[f'PT{pb_}'])
                pend.append((vl, a, b_, pb_, rkeys, i))
                if bg:
                    bg.pop(0)()
            if i >= DEPTH:
                vl, a, b_, pb_, rkeys, ii = pend.pop(0)
                op('pe', lambda: T.matmul(PS[B_OT][0:65, a:b_], vl, PT[pb_][:, a:b_], start=(ii == 0), stop=(ii == n - 1)),
                   reads=list(rkeys) + [f'PT{pb_}'], writes=[pk(B_OT)])
        ob = os_rot[0] % 2
        os_rot[0] += 1
        op('dve', lambda: V.tensor_copy(out=OS[ob][0:65, :], in_=PS[B_OT][0:65, :]), writes=[pk(B_OT), f'OS{ob}'])

        def fin():
            for j in range(4):
                op('pe', lambda: T.transpose(PS[B_TOK][:, j * 65:(j + 1) * 65], OS[ob][0:65, j * 128:(j + 1) * 128], ident[0:65, 0:65]),
                   reads=[f'OS{ob}', 'ident'], writes=[pk(B_TOK)])
            epilogue(G)
        pend_epi.append(fin)

    def flush_epi():
        while pend_epi:
            pend_epi.pop(0)()

    def attention(qt, K, tiles, epilogue, qkey='QT'):
        for G in range(4):
            attention_group(G, qt, K, tiles[G], epilogue, qkey)

    def flush_bg():
        while bg:
            bg.pop(0)()

    def tok_view():
        return PS[B_TOK][:, 0:260].rearrange("p (j d) -> p j d", j=4)

    GB = AR.alloc("GB", [128, 16, 24], F32)
    wg = AR.alloc("wg", [128, 8, 24], BF16)
    imp = AR.alloc("imp", [128, 16, 2, 64], F32)
    SM = [AR.alloc(f"SM{g}", [128, 2048], BF16) for g in range(2)]
    QB = [AR.alloc(f"QB{i}", [128, 2048], BF16) for i in range(2)]
    wq = [AR.alloc(f"wq{i}", [128, 8, 128], BF16) for i in range(2)]
    rs = AR.alloc("rs", [128, 4], F32)
    fct = AR.alloc("fct", [128, 4], F32)
    mark_att = AR.mark()
    cvt = AR.alloc("cvt", [128, 2, 2048], BF16)
    itmp = AR.alloc("itmp", [128, 2, 2, 64], F32)

    load_w(wg, C_GB, 24, 'wg')
    dma('pool', cvt[:], cvd.ap(), writes=['cvt'])
    for s in range(16):
        tcol = 2048 + s * 128
        for c in range(8):
            op('pe', lambda: T.matmul(PS[B_MS][:, s * 24:(s + 1) * 24], hT[:, c, tcol:tcol + 128], wg[:, c, :],
                                      start=(c == 0), stop=(c == 7)),
               reads=['wg', f'hT{4 + s // 4}'], writes=[pk(B_MS)])
    op('act', lambda: A.activation(out=GB[:, :, :], in_=PS[B_MS][:, 0:384].rearrange("p (s k) -> p s k", s=16), func=AF.Sigmoid),
       writes=[pk(B_MS), 'GB'])

    wq_n = [0]

    def prefetch_wq(i):
        w_ = wq_n[0] % 2
        load_w(wq[w_], C_QB + (i % 4) * 128, 128, f'wq{w_}')

    def project_qb(i):
        w_ = wq_n[0] % 2
        wq_n[0] += 1
        prefetch_wq(i + 1)
        prev = None
        for G in range(4):
            bk = proj_fm(wq[w_], f'wq{w_}', 0, 4 + G, PJ3)
            cs = slice(G * 512, (G + 1) * 512)
            if prev is not None:
                headnorm(*prev)
            prev = (bk, gnq[:, 2:3], [(QB[0][0:64, cs], 'QT0'), (QB[1][0:64, cs], 'QT1')])
        headnorm(*prev)

    prefetch_wq(0)

    def cmpA(i, hd, G):
        g = i // 2
        h = 2 * i + hd
        cs = slice(G * 512, (G + 1) * 512)
        pts = []
        for ct in range(2):
            lb = [0, 1][ct]
            pb_ = pt_rot[0] % NPT
            pt_rot[0] += 1
            op('pe', lambda: T.matmul(PS[lb][:, :], KC[0:64, g, ct * 128:(ct + 1) * 128], QB[hd][0:64, cs],
                                      start=True, stop=True), reads=['KC', f'QT{hd}'], writes=[pk(lb)])
            op('act', lambda: A.activation(out=PT[pb_][:, :], in_=PS[lb][:, :], func=AF.Exp),
               writes=[pk(lb), f'PT{pb_}'])
            op('dve', lambda: V.tensor_tensor(out=PT[pb_][:, :], in0=PT[pb_][:, :], in1=cvt[:, ct, cs], op=ALU.mult),
               reads=['cvt'], writes=[f'PT{pb_}'])
            pts.append(pb_)
        return pts

    def cmpB(i, hd, G, pts, tokb):
        g = i // 2
        h = 2 * i + hd
        for j2 in range(2):
            bk = tokb[j2]
            for jj in range(2):
                j = j2 * 2 + jj
                for ct in range(2):
                    op('pe', lambda: T.matmul(PS[bk][:, jj * 129:(jj + 1) * 129], PT[pts[ct]][:, j * 128:(j + 1) * 128],
                                              VC[:, ct, g, :], start=(ct == 0), stop=(ct == 1)),
                       reads=[f'PT{pts[ct]}', 'VCo', 'VC1', 'VCv'], writes=[pk(bk)])
        bks = tokb
        s0s = [4 * G, 4 * G + 2]
        pvs = [PS[b_][:, 0:258].rearrange("p (j c) -> p j c", j=2) for b_ in bks]
        for j2 in range(2):
            op('dve', lambda: V.tensor_scalar(out=rs[:, 2 * j2:2 * j2 + 2], in0=PS[bks[j2]][:, 64:258:129], scalar1=1e-30,
                                              scalar2=None, op0=ALU.add), writes=[pk(bks[j2]), f'rs{j2}'])
        for j2 in range(2):
            op('dve', lambda: V.reciprocal(out=rs[:, 2 * j2:2 * j2 + 2], in_=rs[:, 2 * j2:2 * j2 + 2]), writes=[f'rs{j2}'])
        for j2 in range(2):
            s0 = s0s[j2]
            op('dve', lambda: V.tensor_tensor(out=fct[:, 2 * j2:2 * j2 + 2], in0=rs[:, 2 * j2:2 * j2 + 2],
                                              in1=GB[:, s0:s0 + 2, 3 * h], op=ALU.mult),
               reads=['GB', f'rs{j2}'], writes=[f'fct{j2}'])
        for j2 in range(2):
            s0 = s0s[j2]
            if h % 4 == 0:
                op('dve', lambda: V.tensor_tensor(out=imp[:, s0:s0 + 2, g, :], in0=pvs[j2][:, :, 65:129],
                                                  in1=rs[:, 2 * j2:2 * j2 + 2].unsqueeze(2).to_broadcast([128, 2, 64]), op=ALU.mult),
                   reads=[f'rs{j2}'], writes=[pk(bks[j2]), f'imp{s0}', f'imp{s0 + 1}'])
            else:
                op('dve', lambda: V.tensor_tensor(out=itmp[:, j2, :, :], in0=pvs[j2][:, :, 65:129],
                                                  in1=rs[:, 2 * j2:2 * j2 + 2].unsqueeze(2).to_broadcast([128, 2, 64]), op=ALU.mult),
                   reads=[f'rs{j2}'], writes=[pk(bks[j2]), f'itmp{j2}'])
        for j2 in range(2):
            s0 = s0s[j2]
            op('dve', lambda: V.tensor_tensor(out=o_b[:, s0:s0 + 2, h * 64:(h + 1) * 64], in0=pvs[j2][:, :, 0:64],
                                              in1=fct[:, 2 * j2:2 * j2 + 2].unsqueeze(2).to_broadcast([128, 2, 64]), op=ALU.mult),
               reads=[f'fct{j2}'], writes=[pk(bks[j2]), f'ob{s0}', f'ob{s0 + 1}'])
        if h % 4 != 0:
            for j2 in range(2):
                s0 = s0s[j2]
                op('pool', lambda: P.tensor_tensor(out=imp[:, s0:s0 + 2, g, :], in0=imp[:, s0:s0 + 2, g, :], in1=itmp[:, j2, :, :],
                                                   op=ALU.add), reads=[f'itmp{j2}'], writes=[f'imp{s0}', f'imp{s0 + 1}'])


    units = [(i, hd, G) for i in range(4) for hd in range(2) for G in range(4)]
    project_qb(0)
    stA = cmpA(*units[0])
    for k, u in enumerate(units):
        nxt = units[k + 1] if k + 1 < len(units) else None
        if nxt is not None:
            if nxt[0] != u[0]:
                project_qb(nxt[0])
            stN = cmpA(*nxt)
        cmpB(*u, stA, [[B_TOK, B_OT], [B_PJ0, B_PJ1]][k % 2])
        if nxt is not None:
            stA = stN

    S.barrier()
    AR.reset(mark_att)
    msp = AR.alloc("msp", [128, 8, 64], F32)
    msa = AR.alloc("msa", [128, 8, 64], F32)
    stg = AR.alloc("stg", [128, 8, 128], BF16)
    wrk = AR.alloc("wrk", [128, 8, 64], F32)
    wrk2 = AR.alloc("wrk2", [128, 8, 64], F32)
    mx8 = AR.alloc("mx8", [128, 8, 16], F32)
    op('dve', lambda: V.memset(stg[:], 0.0), writes=['stg'])
    for g in range(2):
        for half in range(2):
            sl = slice(half * 8, half * 8 + 8)
            dma('sp', msp[:], ms_past.ap()[:, sl, :], writes=['msp'])
            dma('sp', msa[:], ms_add.ap()[:, sl, :], writes=['msa'])
            op('dve', lambda: V.tensor_tensor(out=wrk[:], in0=imp[:, sl, g, :], in1=msp[:, :, :], op=ALU.mult),
               reads=[f'imp{s_}' for s_ in range(16)] + ['msp'], writes=['wrk'])
            op('dve', lambda: V.tensor_tensor(out=wrk[:], in0=wrk[:], in1=msa[:, :, :], op=ALU.add), reads=['msa'], writes=['wrk'])
            mxa = [f'mxa{s8}' for s8 in range(8)]
            mxb = [f'mxb{s8}' for s8 in range(8)]
            w2k = [f'w2_{s8}' for s8 in range(8)]
            for s8 in range(8):
                op('dve', lambda: V.max(out=mx8[:, s8, 0:8], in_=wrk[:, s8, :]), reads=['wrk'], writes=[mxa[s8]])
            for s8 in range(8):
                op('dve', lambda: V.match_replace(out=wrk2[:, s8, :], in_to_replace=mx8[:, s8, 0:8], in_values=wrk[:, s8, :],
                                                  imm_value=-3e30), reads=['wrk', mxa[s8]], writes=[w2k[s8]])
            for s8 in range(8):
                op('dve', lambda: V.max(out=mx8[:, s8, 8:16], in_=wrk2[:, s8, :]), reads=[w2k[s8]], writes=[mxb[s8]])
            op('dve', lambda: V.tensor_scalar(out=mx8[:, :, 15:16], in0=mx8[:, :, 15:16], scalar1=-1e29, scalar2=None, op0=ALU.max),
               writes=mxb)
            op('dve', lambda: V.tensor_tensor(out=wrk2[:], in0=wrk[:], in1=mx8[:, :, 15:16].to_broadcast([128, 8, 64]), op=ALU.is_ge),
               reads=['wrk'] + mxb, writes=w2k)
            op('dve', lambda: V.tensor_scalar(out=stg[:, :, 64:128], in0=wrk2[:], scalar1=-1.0, scalar2=MASKV,
                                              op0=ALU.add, op1=ALU.mult), reads=w2k, writes=['stg'])
            for s8 in range(8):
                op('pe', lambda: T.transpose(PSB[:, s8 * 128:(s8 + 1) * 128], stg[:, s8, :], identb[:]),
                   reads=['stg', 'identb'], writes=['psb'])
            op('act', lambda: A.copy(out=SM[g][64:128, half * 1024:(half + 1) * 1024], in_=PSB[64:128, :]),
               writes=['psb', f'SM{g}'])

    S.barrier()
    AR.reset(mark_att)
    Est = AR.alloc("Est", [128, 1024], F32)
    Et = AR.alloc("Et", [128, 1024], BF16)
    Etc = AR.alloc("Etc", [128, 640], BF16)
    Esc = AR.alloc("Esc", [128, 640], F32)

    def E_exp(ncols, stg_, skey, dst, hcol, key):
        op('act', lambda: A.activation(out=dst[:, 0:ncols], in_=stg_[:, 0:ncols], func=AF.Exp, bias=nb31[:, hcol:hcol + 1], scale=1.0),
           reads=[skey, 'nb31'], writes=[key])

    def load_E(src_ap, ncols, dst, hcol, key='E'):
        dma('sp', Est[:, 0:ncols], src_ap, writes=['Est'])
        op('act', lambda: A.activation(out=dst[:, 0:ncols], in_=Est[:, 0:ncols], func=AF.Exp, bias=nb31[:, hcol:hcol + 1], scale=1.0),
           reads=['Est', 'nb31'], writes=[key])

    def nsa_epilogue(h, br, first=False):
        def ep(G):
            tv = tok_view()
            op('dve', lambda: V.tensor_scalar(out=rs[:, 0:4], in0=PS[B_TOK][:, 64:260:65], scalar1=1e-30, scalar2=None, op0=ALU.add),
               writes=[pk(B_TOK), 'rs'])
            op('dve', lambda: V.reciprocal(out=rs[:, 0:4], in_=rs[:, 0:4]), writes=['rs'])
            op('dve', lambda: V.tensor_tensor(out=fct[:, 0:4], in0=rs[:, 0:4], in1=GB[:, 4 * G:4 * G + 4, 3 * h + br], op=ALU.mult),
               reads=['GB', 'rs'], writes=['fct'])
            for j in range(4):
                s = 4 * G + j
                op('dve', lambda: V.scalar_tensor_tensor(out=o_b[:, s, h * 64:(h + 1) * 64], in0=tv[:, j, 0:64],
                                                         scalar=fct[:, j:j + 1], in1=o_b[:, s, h * 64:(h + 1) * 64],
                                                         op0=ALU.mult, op1=ALU.add),
                   reads=['fct', pk(B_TOK)], writes=[f'ob{s}'])
        return ep

    def dense_tiles(Kt, Vt, vsel, kkeys, Eap):
        tiles = []
        for G in range(4):
            tl = []
            for u in range(16):
                s_lo, s_hi = 4 * G, 4 * G + 4
                d_lo = 16 + s_lo - u
                e_n = max(0, min(s_hi, u + 8 - 16) - s_lo) if d_lo <= 7 else 0
                tl.append((Kt[:, u * 128:(u + 1) * 128], vsel(u), s_lo, s_hi, Eap if e_n > 0 else None, d_lo, e_n, kkeys))
            for u in range(4 * G + 4):
                s_lo, s_hi = max(u, 4 * G), 4 * G + 4
                d_lo = s_lo - u
                e_n = max(0, min(s_hi, u + 8) - s_lo) if d_lo <= 7 else 0
                ua = 16 + u
                tl.append((Kt[:, ua * 128:(ua + 1) * 128], vsel(ua), s_lo, s_hi, Eap if e_n > 0 else None, d_lo, e_n, kkeys))
            tiles.append(tl)
        return tiles

    for i in range(4):
        g = i // 2
        project_qb(i)
        for hd in range(2):
            op('dve', lambda: V.tensor_copy(out=QB[hd][64:128, :], in_=SM[g][64:128, :]), reads=[f'SM{g}'], writes=[f'QT{hd}'])
        for hd in range(2):
            h = 2 * i + hd
            if h == 0:
                dma('sp', Est[:, 0:1024], es.ap()[h], writes=['Est'])
            E_exp(1024, Est, 'Est', Et, 8 + h, 'E')
            dma('sp', Est[:, 0:640], ewo.ap()[h], writes=['Est'])
            dma('sp', Esc[:, 0:640], ewc.ap()[h], writes=['Esc'])
            tiles = dense_tiles(KS[g], VS, lambda ua: VS[:, ua, g, :], ['KS0d', 'KS1d', 'KS0m', 'KS1m', 'VS', 'VS1'], Et)
            attention(QB[hd], 128, tiles, nsa_epilogue(h, 1), qkey=f'QT{hd}')
            E_exp(640, Est, 'Est', Et, 8 + h, 'E')
            E_exp(640, Esc, 'Esc', Etc, 8 + h, 'Ec')
            if h + 1 < 8:
                dma('sp', Est[:, 0:1024], es.ap()[h + 1], writes=['Est'])
            for G in range(4):
                tl = []
                order = []
                if G == 0:
                    order = [('c', 15), ('c', 14), ('c', 13), ('c', 12)] + [('o', u) for u in range(4)]
                else:
                    order = [('o', 4 * G - 1)] + [('o', u) for u in range(4 * G - 4, 4 * G - 1)] + [('o', u) for u in range(4 * G, 4 * G + 4)]
                wt = []
                wkeys = ['KW0', 'KW1', 'VW', 'VW1']
                for kind, u in order:
                    if kind == 'c':
                        s_lo, s_hi = max(0, u - 15), min(4, u - 11)
                        d_lo = 16 + s_lo - u
                        wt.append((KW[g][0:64, u * 128:(u + 1) * 128], VW[:, u, g, :], s_lo, s_hi, Etc, d_lo, s_hi - s_lo, wkeys))
                    else:
                        s_lo, s_hi = max(u, 4 * G), min(4 * G + 4, u + 5)
                        d_lo = s_lo - u
                        ua = 16 + u
                        wt.append((KW[g][0:64, ua * 128:(ua + 1) * 128], VW[:, ua, g, :], s_lo, s_hi, Et, d_lo, s_hi - s_lo, wkeys))
                attention_group(G, QB[hd], 64, wt, nsa_epilogue(h, 2), qkey=f'QT{hd}')
    flush_epi()
    S.barrier()
    AR.reset(mark_h)

    o_a = AR.alloc("o_a", [128, 16, 512], F32)
    mark_g = AR.mark()
    KA = [AR.alloc(f"KA{i}", [128, 4096], BF16) for i in range(2)]
    QA = [AR.alloc(f"QA{i}", [128, 2048], BF16) for i in range(2)]
    VA = AR.alloc("VA", [128, 32, 2, 65], BF16)
    wa2 = [AR.alloc(f"wa{i}", [128, 8, 384], BF16) for i in range(2)]
    Est = AR.alloc("Est2", [128, 1024], F32)
    Et = AR.alloc("Et2", [128, 1024], BF16)
    rs = AR.alloc("rs2", [128, 4], F32)
    stg = AR.alloc("stg2", [128, 16, 128], BF16)
    map_ = AR.alloc("map", [128, 16, 16], F32)
    maa = AR.alloc("maa", [128, 16, 16], F32)
    mao = AR.alloc("mao", [128, 16, 16], F32)
    km32 = AR.alloc("km32", [64, 16], F32)
    kmb = AR.alloc("kmb", [64, 16], BF16)
    wk16 = AR.alloc("wk16", [128, 256], F32)
    wk16b = AR.alloc("wk16b", [128, 256], F32)
    mx8 = AR.alloc("mx8b", [128, 128], F32)
    for j_, c0_ in enumerate((C_QA, C_KA, C_VA)):
        load_w(wa2[0], c0_, 128, 'wa0', j_ * 128)
    dma('sp', map_[:], ma_past.ap(), writes=['map'])
    dma('sp', maa[:], ma_add.ap(), writes=['maa'])
    dma('sp', mao[:], ma_own.ap(), writes=['mao'])
    op('dve', lambda: V.memset(stg[:], 0.0), writes=['stg'])
    op('pool', lambda: P.memset(VA[:, :, :, 64:65], 1.0), writes=['VA1'])
    for hd in range(2):
        op('pool', lambda: P.memset(KA[hd][64:128, :], 0.0), writes=[f'KA{hd}m'])
        dma('pool', KA[hd][64:80, :], oh16.ap(), writes=[f'KA{hd}m'])

    def moba_epilogue(h):
        def ep(G):
            tv = tok_view()
            op('dve', lambda: V.tensor_scalar(out=rs[:, 0:4], in0=PS[B_TOK][:, 64:260:65], scalar1=1e-30, scalar2=None, op0=ALU.add),
               writes=[pk(B_TOK), 'rs'])
            op('dve', lambda: V.reciprocal(out=rs[:, 0:4], in_=rs[:, 0:4]), writes=['rs'])
            for j in range(4):
                s = 4 * G + j
                op('dve', lambda: V.tensor_scalar(out=o_a[:, s, h * 64:(h + 1) * 64], in0=tv[:, j, 0:64],
                                                  scalar1=rs[:, j:j + 1], scalar2=None, op0=ALU.mult),
                   reads=['rs', pk(B_TOK)], writes=[f'oa{s}'])
        return ep

    def load_wa(i):
        w_ = wa2[i % 2]
        load_w(w_, C_QA + i * 128, 128, f'wa{i % 2}', 0)
        load_w(w_, C_KA + i * 128, 128, f'wa{i % 2}', 128)
        load_w(w_, C_VA + i * 128, 128, f'wa{i % 2}', 256)
    Etg = [Et, AR.alloc("Et3", [128, 1024], BF16)]

    def v_tile(wa, wak, t):
        bk = [1, B_OT][t % 2]
        for c in range(8):
            op('pe', lambda: T.matmul(PS[bk][:, 0:128], hT[:, c, t * 128:(t + 1) * 128], wa[:, c, 256:384],
                                      start=(c == 0), stop=(c == 7)),
               reads=[wak, f'hT{t // 4}'], writes=[pk(bk)])
        op('act', lambda: A.copy(out=VA[:, t, :, 0:64], in_=PS[bk][:, 0:128].rearrange("p (g d) -> p g d", g=2)),
           writes=[pk(bk), 'VA'])

    def setup_ops(hd, h):
        ops = []
        qk = f'QT{hd}'
        w3 = lambda t_: t_[:, :].rearrange("p (s j) -> p s j", s=16)
        ops.append(lambda: op('dve', lambda: V.reduce_sum(out=km32[:, :], in_=KA[hd][0:64, :].rearrange("p (j n) -> p j n", j=16), axis=AX.X),
                              reads=[f'KA{hd}d'], writes=['km32']))
        ops.append(lambda: op('dve', lambda: V.tensor_scalar(out=kmb[:, :], in0=km32[:, :], scalar1=1.0 / 256, scalar2=None, op0=ALU.mult),
                              reads=['km32'], writes=['kmb']))
        for s in range(16):
            ops.append(lambda s=s: op('pe', lambda: T.matmul(PS[B_MS][:, s * 16:(s + 1) * 16], QA[hd][0:64, s * 128:(s + 1) * 128], kmb[:, :],
                                                             start=True, stop=True), reads=[qk, 'kmb'], writes=[pk(B_MS)]))
        ops.append(lambda: op('dve', lambda: V.tensor_tensor(out=w3(wk16), in0=w3(PS[B_MS][:, 0:256]), in1=map_[:, :, :], op=ALU.mult),
                              reads=['map'], writes=[pk(B_MS), 'wk16']))
        ops.append(lambda: op('dve', lambda: V.tensor_tensor(out=w3(wk16), in0=w3(wk16), in1=maa[:, :, :], op=ALU.add), reads=['maa'], writes=['wk16']))
        mxk = [f'mx8_{s_}' for s_ in range(16)]
        for s in range(16):
            ops.append(lambda s=s: op('dve', lambda: V.max(out=mx8[:, s * 8:(s + 1) * 8], in_=wk16[:, s * 16:(s + 1) * 16]),
                                      reads=['wk16'], writes=[mxk[s]]))
        thr = mx8[:, :].rearrange("p (s k) -> p s k", s=16)[:, :, 2:3].to_broadcast([128, 16, 16])
        ops.append(lambda: op('dve', lambda: V.tensor_tensor(out=w3(wk16b), in0=w3(wk16), in1=thr, op=ALU.is_ge),
                              reads=['wk16'] + mxk, writes=['wk16b']))
        ops.append(lambda: op('dve', lambda: V.tensor_tensor(out=w3(wk16b), in0=w3(wk16b), in1=map_[:, :, :], op=ALU.mult), reads=['map'], writes=['wk16b']))
        ops.append(lambda: op('dve', lambda: V.tensor_tensor(out=w3(wk16b), in0=w3(wk16b), in1=mao[:, :, :], op=ALU.add), reads=['mao'], writes=['wk16b']))
        ops.append(lambda: op('dve', lambda: V.tensor_scalar(out=stg[:, :, 64:80], in0=w3(wk16b), scalar1=-1.0, scalar2=MASKV,
                                                             op0=ALU.add, op1=ALU.mult), reads=['wk16b'], writes=['stg']))
        for half in range(2):
            for s8 in range(8):
                s = half * 8 + s8
                ops.append(lambda s=s, s8=s8: op('pe', lambda: T.transpose(PSB[:, s8 * 128:(s8 + 1) * 128], stg[:, s, :], identb[:]),
                                                 reads=['stg', 'identb'], writes=['psb']))
            ops.append(lambda half=half: op('act', lambda: A.copy(out=QA[hd][64:128, half * 1024:(half + 1) * 1024], in_=PSB[64:128, :]),
                                            writes=['psb', qk]))
        if hd == 1:
            ops.append(lambda: E_exp(1024, Est, 'Est', Etg[hd], h, f'E{hd}'))
        return ops

    for i in range(4):
        wa = wa2[i % 2]
        wak = f'wa{i % 2}'
        if i + 1 < 4:
            load_wa(i + 1)
        load_E(ea.ap()[2 * i], 1024, Etg[0], 2 * i, key='E0')
        dma('sp', Est[:, 0:1024], ea.ap()[2 * i + 1], writes=['Est'])
        vt = 0
        items = []
        for g8 in range(8):
            cs = slice(g8 * 512, (g8 + 1) * 512)
            items.append((128, g8, gn[:, 1:2], [(KA[0][0:64, cs], 'KA0d'), (KA[1][0:64, cs], 'KA1d')]))
        for G in range(4):
            cs = slice(G * 512, (G + 1) * 512)
            items.append((0, 4 + G, gnq[:, 0:1], [(QA[0][0:64, cs], 'QT0'), (QA[1][0:64, cs], 'QT1')]))
        prev = None
        for wc0, g8, gcol, dsts in items:
            bk = proj_fm(wa, wak, wc0, g8, PJ3)
            if prev is not None:
                headnorm(*prev)
            prev = (bk, gcol, dsts)
            for _ in range(3):
                if vt < 32:
                    v_tile(wa, wak, vt)
                    vt += 1
        headnorm(*prev)
        while vt < 32:
            v_tile(wa, wak, vt)
            vt += 1
        for f_ in setup_ops(0, 2 * i):
            f_()
        bg.extend(setup_ops(1, 2 * i + 1))
        for hd in range(2):
            h = 2 * i + hd
            if hd == 1:
                flush_bg()
            tiles = dense_tiles(KA[hd], VA, lambda ua: VA[:, ua, hd, :], [f'KA{hd}d', f'KA{hd}m', 'VA', 'VA1'], Etg[hd])
            attention(QA[hd], 128, tiles, moba_epilogue(h), qkey=f'QT{hd}')
    flush_epi()
    if debug:
        for s in range(16):
            dma('sp', dbg['oa'].ap()[s * 128:(s + 1) * 128, :], o_a[:, s, :], reads=[f'oa{s}'], writes=['dbgoa'])
            dma('sp', dbg['ob'].ap()[s * 128:(s + 1) * 128, :], o_b[:, s, :], reads=[f'ob{s}'], writes=['dbgob'])
    S.barrier()
    AR.reset(mark_g)

    yT = [AR.alloc(f"yT{b}", [128, 4, 2048], BF16) for b in range(2)]
    wz = [AR.alloc(f"wz{i}", [128, 8, 128], BF16) for i in range(2)]
    sz = [AR.alloc(f"sz{i}", [128, 512], F32) for i in range(2)]
    tA = [AR.alloc(f"tA{i}", [128, 512], F32) for i in range(2)]
    m_h1 = AR.mark()
    AR.reset(AR.offs['PT0'])
    xo = [AR.alloc(f"xo{i}", [128, 512], F32) for i in range(3)]
    res = [AR.alloc(f"res{i}", [128, 512], F32) for i in range(2)]
    assert AR.cur <= AR.offs['small']
    AR.reset(m_h1)
    wo = AR.alloc("wo", [128, 8, 1024], BF16)
    dma('pool', wo[:], w_out.ap().rearrange("(c p) n -> p c n", p=128), writes=['wo'])
    it = 0
    wi = 0
    zcols = [C_ZA + ci * 128 for ci in range(4)] + [C_ZB + ci * 128 for ci in range(4)]
    load_w(wz[0], zcols[0], 128, 'wz0')
    for br in range(2):
        osrc = o_a if br == 0 else o_b
        okey = 'oa' if br == 0 else 'ob'
        for ci in range(4):
            wb = wi % 2
            wi += 1
            if wi < 8:
                load_w(wz[wi % 2], zcols[wi], 128, f'wz{wi % 2}')
            for G in range(4):
                cs = slice(G * 512, (G + 1) * 512)
                b2 = it % 2
                it += 1
                tb = [B_TOK, B_OT][b2]
                bk = proj_fm(wz[wb], f'wz{wb}', 0, 4 + G)
                op('act', lambda: A.activation(out=sz[b2][:], in_=PS[bk][:, :], func=AF.Silu), writes=[pk(bk), f'sz{b2}'])
                for j in range(4):
                    s = 4 * G + j
                    op('pe', lambda: T.transpose(PS[tb][:, j * 128:(j + 1) * 128], osrc[:, s, ci * 128:(ci + 1) * 128], ident[:]),
                       reads=[f'{okey}{s}', 'ident'], writes=[pk(tb)])
                op('dve', lambda: V.tensor_tensor(out=yT[br][:, ci, cs], in0=PS[tb][:, :], in1=sz[b2][:], op=ALU.mult),
                   reads=[f'sz{b2}'], writes=[pk(tb), f'yT{br}'])
    S.barrier()
    m_end = AR.mark()
    AR.reset(AR.offs['o_b'])
    mg = AR.alloc("mg", [128, 8, 2048], BF16)
    AR.reset(AR.offs['o_a'])
    wgm = [AR.alloc(f"wgm{i}", [128, 8, 256], BF16) for i in range(2)]
    wbr = [AR.alloc(f"wbr{i}", [128, 2, 4, 128], BF16) for i in range(2)]
    assert AR.cur <= AR.offs['o_a'] + 32768
    AR.reset(m_end)
    it = 0

    def load_merge_w(m):
        wb = m % 2
        load_w(wgm[wb], C_GM + m * 128, 128, f'wgm{wb}', 0)
        load_w(wgm[wb], C_GM + 1024 + m * 128, 128, f'wgm{wb}', 128)
        dma('pool', wbr[wb][:, 0, :, :], w_ba.ap().rearrange("(c p) n -> p c n", p=128)[:, :, m * 128:(m + 1) * 128], writes=[f'wbr{wb}'])
        dma('pool', wbr[wb][:, 1, :, :], w_bb.ap().rearrange("(c p) n -> p c n", p=128)[:, :, m * 128:(m + 1) * 128], writes=[f'wbr{wb}'])
    load_merge_w(0)
    for m in range(8):
        wb = m % 2
        if m + 1 < 8:
            load_merge_w(m + 1)
        for G in range(4):
            cs = slice(G * 512, (G + 1) * 512)
            for br in range(2):
                b2 = it % 2
                it += 1
                ob = [B_TOK, B_OT][b2]
                bk = proj_fm(wgm[wb], f'wgm{wb}', br * 128, 4 + G)
                op('act', lambda: A.activation(out=sz[b2][:], in_=PS[bk][:, :], func=AF.Sigmoid), writes=[pk(bk), f'sz{b2}'])
                for ci in range(4):
                    op('pe', lambda: T.matmul(PS[ob][:, :], wbr[wb][:, br, ci, :], yT[br][:, ci, cs], start=(ci == 0), stop=(ci == 3)),
                       reads=[f'wbr{wb}', f'yT{br}'], writes=[pk(ob)])
                if br == 0:
                    ta = tA[G % 2]
                    op('dve', lambda: V.tensor_tensor(out=ta[:], in0=PS[ob][:, :], in1=sz[b2][:], op=ALU.mult),
                       reads=[f'sz{b2}'], writes=[pk(ob), f'tA{G % 2}'])
                else:
                    op('dve', lambda: V.tensor_tensor(out=sz[b2][:], in0=PS[ob][:, :], in1=sz[b2][:], op=ALU.mult),
                       writes=[pk(ob), f'sz{b2}'])
                    op('pool', lambda: P.tensor_tensor(out=mg[:, m, cs], in0=tA[G % 2][:], in1=sz[b2][:], op=ALU.add),
                       reads=[f'tA{G % 2}', f'sz{b2}'], writes=['mg'])
    def load_x(k_):
        s_, hf_ = k_ // 2, k_ % 2
        dma('pool', xo[k_ % 3][:], xa.ap()[2048 + s_ * 128:2048 + (s_ + 1) * 128, hf_ * 512:(hf_ + 1) * 512], writes=[f'xo{k_ % 3}'])
    load_x(0)
    load_x(1)
    for k in range(32):
        s, hf = k // 2, k % 2
        if k + 2 < 32:
            load_x(k + 2)
        bk = [B_PJ0, B_PJ1][k % 2]
        xb, rb = k % 3, k % 2
        for m in range(8):
            op('pe', lambda: T.matmul(PS[bk][:, :], mg[:, m, s * 128:(s + 1) * 128], wo[:, m, hf * 512:(hf + 1) * 512],
                                      start=(m == 0), stop=(m == 7)), reads=['mg', 'wo'], writes=[pk(bk)])
        op('dve', lambda: V.tensor_tensor(out=res[rb][:], in0=PS[bk][:, :], in1=xo[xb][:], op=ALU.add),
           reads=[f'xo{xb}'], writes=[pk(bk), f'res{rb}'])
        dma('sp', out.ap()[s * 128:(s + 1) * 128, hf * 512:(hf + 1) * 512], res[rb][:], reads=[f'res{rb}'], writes=[f'outd{k}'])
    S.barrier()
    return nc


def _t5_bucket(dist):
    n = np.maximum(dist, 0)
    nf = np.maximum(n, 16).astype(np.float32)
    large = 16 + (np.log(nf / np.float32(16)) / np.float32(np.log(64.0)) * np.float32(16)).astype(np.int32)
    return np.where(n < 16, n, np.minimum(large, 31))


def _tables(h, rel_bias):
    kk = np.arange(128)[:, None]
    tb = {}

    def strip(nd, heads, lo_valid, hi_valid):
        cols = np.arange(nd * 128)[None, :]
        dist = cols - kk
        bkt = _t5_bucket(dist)
        outp = np.empty((len(heads), 128, nd * 128), np.float32)
        ok = (dist >= lo_valid) & (dist < hi_valid)
        for i, hh in enumerate(heads):
            v = rel_bias[bkt, hh]
            outp[i] = np.where(ok, v, np.float32(-MASKV))
        return outp

    tb['ea'] = strip(8, list(range(8)), 0, 1 << 30)
    tb['es'] = strip(8, list(range(8, 16)), 0, 1 << 30)
    tb['ewo'] = strip(5, list(range(8, 16)), 0, 512)
    tb['ewc'] = tb['ewo'].copy() if h == 1 else np.full_like(tb['ewo'], -MASKV)
    c = (np.arange(2)[None, :, None] * 128 + np.arange(128)[:, None, None])
    t_all = 2048 + np.arange(2048)[None, None, :]
    cv = (16 * c + 31 <= t_all) & (c <= 254)
    if h == 0:
        cv &= (c >= 128)
    tb['cv'] = cv.astype(np.float32)
    q = np.arange(2048)
    qb = (2048 + q) // 256
    j = np.arange(16)[None, :]
    past = (j < qb[:, None]) & ((h == 1) | (j >= 8))
    own = (j == qb[:, None])
    pl = lambda a_: np.ascontiguousarray(a_.reshape(16, 128, a_.shape[1]).transpose(1, 0, 2))
    tb['ma_past'] = pl(past.astype(np.float32))
    tb['ma_add'] = pl(np.where(past, 0.0, -1e30).astype(np.float32))
    tb['ma_own'] = pl(own.astype(np.float32))
    qs = (2048 + q) // 64
    j = np.arange(64)[None, :]
    first = 0 if h == 1 else 32
    forced_first = (j == first)
    forced_own = (j == qs[:, None])
    pasts = (j < qs[:, None]) & (j > first) & ~forced_own
    add = np.full((2048, 64), -1e30, np.float32)
    add[pasts] = 0.0
    add[np.broadcast_to(forced_own, add.shape)] = 1e30
    add[np.broadcast_to(forced_first, add.shape)] = 2e30
    tb['ms_past'] = pl(pasts.astype(np.float32))
    tb['ms_add'] = pl(add)
    col = np.arange(4096)
    tb['oh16'] = (col[None, :] // 256 == np.arange(16)[:, None]).astype(np.float32)
    tb['oh64'] = (col[None, :] // 64 == np.arange(64)[:, None]).astype(np.float32)
    cc = c[:, :, 0][:, :, None]
    jj = np.arange(64)[None, None, :]
    ov = (16 * cc < 64 * jj + 64) & (16 * cc + 32 > 64 * jj) & (cc <= 254)
    tb['ovT'] = ov.astype(np.float32)
    tb['ident'] = np.eye(128, dtype=np.float32)
    return tb


_PROG = {}


def kernel(x, norm_w, w_in, q_norm_a, k_norm_a, q_norm_b, k_norm_cmp, k_norm_sel, k_norm_win,
           cmp_pos_k, cmp_w1_k, cmp_w2_k, cmp_pos_v, cmp_w1_v, cmp_w2_v, rel_bias,
           w_branch_a, w_branch_b, w_out, _debug=False):
    f = lambda a: np.ascontiguousarray(np.asarray(a, dtype=np.float32))
    x = f(x)
    rel_bias = f(rel_bias)
    common = dict(
        norm_w=np.ascontiguousarray(f(norm_w)[0].reshape(8, 128).T), w_in=f(w_in)[0],
        gains=np.stack([f(q_norm_a)[0], f(k_norm_a)[0], f(q_norm_b)[0], f(k_norm_cmp)[0], f(k_norm_sel)[0], f(k_norm_win)[0]]),
        cmp_pos=np.ascontiguousarray(np.stack([f(cmp_pos_k)[0].T, f(cmp_pos_v)[0].T])),
        cmp_w1=np.stack([f(cmp_w1_k)[0], f(cmp_w1_v)[0]]),
        cmp_w2=np.stack([f(cmp_w2_k)[0], f(cmp_w2_v)[0]]),
        rel_bias=rel_bias, w_ba=f(w_branch_a)[0], w_bb=f(w_branch_b)[0], w_out=f(w_out)[0],
    )
    common['gainsT'] = np.ascontiguousarray(common['gains'].T)
    tabs = [_tables(h, rel_bias) for h in range(2)]
    in_maps = []
    for c in range(8):
        b, h = c // 2, c % 2
        xa = np.zeros((4096, 1024), np.float32)
        if h == 1:
            xa[:] = x[b]
        else:
            xa[2048:] = x[b, :2048]
        m = dict(common)
        m.update(tabs[h])
        m['xa'] = xa
        in_maps.append(m)
    key = bool(_debug)
    if key not in _PROG:
        _PROG[key] = build_program(debug=key)
    nc = _PROG[key]
    r = run_bass_kernel_spmd(nc, in_maps, core_ids=list(range(8)))
    outp = np.empty((4, 4096, 1024), np.float32)
    for c in range(8):
        b, h = c // 2, c % 2
        outp[b, h * 2048:(h + 1) * 2048] = r.results[c]["out"]
    if _debug:
        return outp, r.results
    return outp
```
